# Optimizing a Trainium2 kernel written in Bass

```python
import math
import jax, jax.numpy as jnp
from jax import lax
import numpy as np

D_MODEL = 1024
BATCH = 8
SEQ = 4096
DEPTH = 2

GRID_W = 64
CTX_LEN = 256
N_MOD = 6
NORM_EPS = 1e-6
SHORT_CONV = 3

RW_HEADS = 6
RW_HEAD_DIM = 64
RW_WIDTH = RW_HEADS * RW_HEAD_DIM
RW_DECAY_RANK = 64
RW_A_RANK = 64
RW_GATE_RANK = 128
RW_DECAY_SCALE = 0.6065306597
RW_GN_EPS = 64e-5
L2_EPS = 1e-12

MLA_HEADS = 6
MLA_Q_RANK = 256
MLA_KV_RANK = 128
MLA_NOPE_DIM = 64
MLA_ROPE_DIM = 32
MLA_V_DIM = 64
MLA_QK_DIM = MLA_NOPE_DIM + MLA_ROPE_DIM
MLA_WIDTH = MLA_HEADS * MLA_V_DIM
AXIS_ROPE_DIM = MLA_ROPE_DIM // 2
ROPE_THETA = 10000.0
Q_BLOCK = 128

HY_WIDTH = 256
HY_GROUPS = 4
HY_ORDER = 2
HY_POS_BANDS = 16
HY_POS_DIM = 1 + 2 * HY_POS_BANDS
HY_FILTER_HIDDEN = 64
HY_SHORT_DECAY_PCT = 0.3
HY_LONG_DECAY_PCT = 1.5
HY_DECAY_TARGET = 1e-2

PEER_HEADS = 8
PEER_N_KEYS = 128
PEER_N_EXPERTS = PEER_N_KEYS * PEER_N_KEYS
PEER_TOPK = 16
PEER_QUERY_DIM = 256
PEER_HALF = PEER_QUERY_DIM // 2
PEER_CHUNK = 128

RW_PROJ = 3 * RW_WIDTH + RW_DECAY_RANK + RW_A_RANK + RW_GATE_RANK
MLA_PROJ = MLA_Q_RANK + MLA_KV_RANK + MLA_ROPE_DIM
HY_PROJ = (HY_ORDER + 1) * HY_WIDTH
IN_PROJ = RW_PROJ + MLA_PROJ + HY_PROJ
MIX_WIDTH = RW_WIDTH + MLA_WIDTH + HY_WIDTH

kernel_name = 'hybrid_rwkv7_mla_hyena_peer_diffusion'


def rms_norm(x, gain):
    xf = x.astype(jnp.float32)
    y = xf * lax.rsqrt(jnp.mean(xf * xf, axis=-1, keepdims=True) + NORM_EPS)
    return (y * gain.astype(jnp.float32)).astype(x.dtype)


def short_conv(x, w):
    xp = jnp.pad(x, ((0, 0), (1, 1), (0, 0)))
    return xp[:, :-2] * w[0] + xp[:, 1:-1] * w[1] + xp[:, 2:] * w[2]


def _heads(t):
    return t.reshape(t.shape[:-1] + (RW_HEADS, RW_HEAD_DIM))


def rwkv7_prepare(p, conv_w, decay_up, decay0, a_up, a0, gate_up, k_k, k_a):
    z = short_conv(p, conv_w)
    o1, o2, o3 = RW_WIDTH, 2 * RW_WIDTH, 3 * RW_WIDTH
    o4 = o3 + RW_DECAY_RANK
    o5 = o4 + RW_A_RANK
    r, k, v = z[..., :o1], z[..., o1:o2], z[..., o2:o3]
    d_lo, a_lo, g_lo = z[..., o3:o4], z[..., o4:o5], z[..., o5:]
    decay = jnp.exp(-RW_DECAY_SCALE * jax.nn.sigmoid(
        decay0[:, None, None, :] + jnp.einsum('blr,nrc->nblc', jnp.tanh(d_lo), decay_up)))
    a = jax.nn.sigmoid(a0[:, None, None, :] + jnp.einsum('blr,nrc->nblc', a_lo, a_up))
    g = jnp.einsum('blr,rc->blc', jax.nn.sigmoid(g_lo), gate_up)
    kk = _heads(k * k_k).astype(jnp.float32)
    kk = kk * lax.rsqrt(jnp.sum(kk * kk, axis=-1, keepdims=True) + L2_EPS)
    k_rep = _heads(k)[None] * (1.0 + (_heads(a) - 1.0) * _heads(k_a))
    return _heads(r), _heads(k), _heads(v), _heads(decay), _heads(a), kk, k_rep, g


def rwkv7_scan(state0, r, decay, kk, a, k_rep, v, reverse):
    def step(S, inp):
        r_t, w_t, kk_t, a_t, k_t, v_t = inp
        S = (S * w_t[:, :, None, :]
             - jnp.einsum('bhvk,bhk->bhv', S, kk_t)[..., None] * (kk_t * a_t)[:, :, None, :]
             + v_t[..., None] * k_t[:, :, None, :])
        return S, jnp.einsum('bhvk,bhk->bhv', S, r_t)
    seq = tuple(jnp.moveaxis(t.astype(jnp.float32), 1, 0) for t in (r, decay, kk, a, k_rep, v))
    s_fin, y = lax.scan(step, state0, seq, reverse=reverse)
    return s_fin, jnp.moveaxis(y, 0, 1)


def rwkv7_readout(y, r, k, v, g, r_k, gn_g, gn_b):
    B, L = y.shape[:2]
    mu = jnp.mean(y, axis=-1, keepdims=True)
    var = jnp.mean(jnp.square(y - mu), axis=-1, keepdims=True)
    yn = ((y - mu) * lax.rsqrt(var + RW_GN_EPS)).reshape(B, L, RW_WIDTH) * gn_g + gn_b
    bonus = jnp.sum(r * k * r_k.reshape(RW_HEADS, RW_HEAD_DIM), axis=-1, keepdims=True) * v
    return ((yn + bonus.reshape(B, L, RW_WIDTH)) * g).astype(g.dtype)


def rwkv7_mixer(p_lat, p_ctx, conv_w, decay_up, decay0, a_up, a0, gate_up, k_k, k_a, r_k, gn_g, gn_b, need_ctx):
    prm = (conv_w, decay_up, decay0, a_up, a0, gate_up, k_k, k_a)
    lr, lk, lv, ldec, la, lkk, lkr, lg = rwkv7_prepare(p_lat, *prm)
    cr, ck, cv, cdec, ca, ckk, ckr, cg = rwkv7_prepare(p_ctx, *prm)
    B = p_lat.shape[0]
    ys_l, ys_c = [], []
    for d, rev in enumerate((False, True)):
        s0 = jnp.zeros((B, RW_HEADS, RW_HEAD_DIM, RW_HEAD_DIM), jnp.float32)
        s_ctx, yc = rwkv7_scan(s0, cr, cdec[d], ckk, ca[d], ckr[d], cv, rev)
        _, yl = rwkv7_scan(s_ctx, lr, ldec[d], lkk, la[d], lkr[d], lv, rev)
        ys_l.append(yl)
        ys_c.append(yc)
    out_l = rwkv7_readout(ys_l[0] + ys_l[1], lr, lk, lv, lg, r_k, gn_g, gn_b)
    out_c = rwkv7_readout(ys_c[0] + ys_c[1], cr, ck, cv, cg, r_k, gn_g, gn_b) if need_ctx else None
    return out_l, out_c


def axial_rope_tables(L):
    rows = L // GRID_W
    row, col = jnp.meshgrid(jnp.arange(rows), jnp.arange(GRID_W), indexing='ij')
    inv = ROPE_THETA ** (-jnp.arange(0, AXIS_ROPE_DIM, 2, dtype=jnp.float32) / AXIS_ROPE_DIM)
    pos = jnp.stack([row.reshape(-1), col.reshape(-1)], axis=-1).astype(jnp.float32)
    ang = pos[:, :, None] * inv[None, None, :]
    return jnp.cos(ang), jnp.sin(ang)


def apply_axial_rope(x, rope):
    cos, sin = rope
    B, L, H, _ = x.shape
    xa = x.reshape(B, L, H, 2, AXIS_ROPE_DIM).astype(jnp.float32)
    half = AXIS_ROPE_DIM // 2
    x1, x2 = xa[..., :half], xa[..., half:]
    cs, sn = cos[None, :, None], sin[None, :, None]
    out = jnp.concatenate([x1 * cs - x2 * sn, x2 * cs + x1 * sn], axis=-1)
    return out.reshape(B, L, H, MLA_ROPE_DIM).astype(x.dtype)


def mla_qkv(p, rope, q_norm, w_uq, kv_norm, w_ukv, q_gain, k_gain):
    B, L, _ = p.shape
    c_q = p[..., :MLA_Q_RANK]
    c_kv = p[..., MLA_Q_RANK:MLA_Q_RANK + MLA_KV_RANK]
    k_rope = p[..., MLA_Q_RANK + MLA_KV_RANK:]
    q = (rms_norm(c_q, q_norm) @ w_uq).reshape(B, L, MLA_HEADS, MLA_QK_DIM)
    kv = (rms_norm(c_kv, kv_norm) @ w_ukv).reshape(B, L, MLA_HEADS, MLA_NOPE_DIM + MLA_V_DIM)
    k = jnp.concatenate([kv[..., :MLA_NOPE_DIM],
                         jnp.broadcast_to(k_rope[:, :, None, :], (B, L, MLA_HEADS, MLA_ROPE_DIM))], axis=-1)
    v = kv[..., MLA_NOPE_DIM:]
    q = rms_norm(q, q_gain)
    k = rms_norm(k, k_gain)
    if rope is not None:
        q = jnp.concatenate([q[..., :MLA_NOPE_DIM], apply_axial_rope(q[..., MLA_NOPE_DIM:], rope)], axis=-1)
        k = jnp.concatenate([k[..., :MLA_NOPE_DIM], apply_axial_rope(k[..., MLA_NOPE_DIM:], rope)], axis=-1)
    return q, k, v


def softmax_attend(q, k, v):
    s = jnp.einsum('bqhd,bkhd->bhqk', q, k, preferred_element_type=jnp.float32) * (MLA_QK_DIM ** -0.5)
    pr = jax.nn.softmax(s, axis=-1)
    return jnp.einsum('bhqk,bkhd->bqhd', pr.astype(v.dtype), v)


def mla_mixer(p_lat, p_ctx, rope, q_norm, w_uq, kv_norm, w_ukv, q_gain, k_gain, need_ctx):
    prm = (q_norm, w_uq, kv_norm, w_ukv, q_gain, k_gain)
    q_l, k_l, v_l = mla_qkv(p_lat, rope, *prm)
    q_c, k_c, v_c = mla_qkv(p_ctx, None, *prm)
    k_all = jnp.concatenate([k_l, k_c], axis=1)
    v_all = jnp.concatenate([v_l, v_c], axis=1)
    B, L = p_lat.shape[:2]
    q_blocks = jnp.moveaxis(q_l.reshape(B, L // Q_BLOCK, Q_BLOCK, MLA_HEADS, MLA_QK_DIM), 1, 0)
    y_l = lax.map(lambda qb: softmax_attend(qb, k_all, v_all), q_blocks)
    y_l = jnp.moveaxis(y_l, 0, 1).reshape(B, L, MLA_WIDTH)
    y_c = softmax_attend(q_c, k_c, v_c).reshape(B, p_ctx.shape[1], MLA_WIDTH) if need_ctx else None
    return y_l, y_c


def hyena_filters(L, w1, b1, freq1, w2, b2, freq2, w3, b3):
    tn = jnp.arange(L, dtype=jnp.float32) / L
    bands = jnp.arange(1, HY_POS_BANDS + 1, dtype=jnp.float32)
    ang = 2.0 * math.pi * tn[:, None] * bands[None, :]
    z = jnp.concatenate([tn[:, None], jnp.cos(ang), jnp.sin(ang)], axis=-1)
    h = jnp.sin(freq1 * (z @ w1 + b1))
    h = jnp.sin(freq2 * (h @ w2 + b2))
    h = (h @ w3 + b3).reshape(L, HY_ORDER, 2, HY_WIDTH)
    rates = jnp.abs(jnp.linspace(math.log(HY_DECAY_TARGET) / HY_LONG_DECAY_PCT,
                                 math.log(HY_DECAY_TARGET) / HY_SHORT_DECAY_PCT, HY_WIDTH))
    h = h * jnp.exp(-tn[:, None] * rates[None, :])[:, None, None, :]
    zero = jnp.zeros((1, HY_ORDER, HY_WIDTH), h.dtype)
    h_full = jnp.concatenate([h[:, :, 0], zero, h[:0:-1, :, 1]], axis=0)
    return h_full * lax.rsqrt(jnp.sum(jnp.square(h_full), axis=0, keepdims=True))


def fft_long_conv(u, h_full, bias):
    L = u.shape[1]
    uf = jnp.fft.rfft(u.astype(jnp.float32), n=2 * L, axis=1)
    hf = jnp.fft.rfft(h_full, n=2 * L, axis=0)
    y = jnp.fft.irfft(uf * hf[None], n=2 * L, axis=1)[:, :L]
    return (y + u.astype(jnp.float32) * bias.astype(jnp.float32)).astype(u.dtype)


def hyena_mixer(p, conv_w, w1, b1, freq1, w2, b2, freq2, w3, b3, bias):
    L = p.shape[1]
    z = short_conv(p, conv_w)
    gates = (z[..., :HY_WIDTH], z[..., HY_WIDTH:2 * HY_WIDTH])
    y = z[..., 2 * HY_WIDTH:]
    h_full = hyena_filters(L, w1, b1, freq1, w2, b2, freq2, w3, b3)
    for o in range(HY_ORDER):
        y = gates[o] * fft_long_conv(y, h_full[:, o], bias[o])
    return y


def peer_ffn(h, w_q, sub_keys, exp_u, exp_v):
    B, L, D = h.shape
    chunks = h.reshape(B * L // PEER_CHUNK, PEER_CHUNK, D)

    def retrieve(xc):
        q = (xc @ w_q).reshape(PEER_CHUNK, PEER_HEADS, 2, PEER_HALF)
        s = jnp.einsum('thpd,hpnd->thpn', q, sub_keys, preferred_element_type=jnp.float32)
        sv, si = lax.top_k(s, PEER_TOPK)
        cand_s = (sv[:, :, 0, :, None] + sv[:, :, 1, None, :]).reshape(PEER_CHUNK, PEER_HEADS, PEER_TOPK * PEER_TOPK)
        cand_i = (si[:, :, 0, :, None] * PEER_N_KEYS + si[:, :, 1, None, :]).reshape(PEER_CHUNK, PEER_HEADS, PEER_TOPK * PEER_TOPK)
        top_s, top_j = lax.top_k(cand_s, PEER_TOPK)
        e_idx = jnp.take_along_axis(cand_i, top_j, axis=-1)
        gate = jax.nn.softmax(top_s, axis=-1)
        act = jax.nn.gelu(jnp.einsum('thkd,td->thk', exp_u[e_idx], xc).astype(jnp.float32), approximate=False)
        return jnp.einsum('thk,thkd->td', (gate * act).astype(xc.dtype), exp_v[e_idx])

    return lax.map(retrieve, chunks).reshape(B, L, D)


def setup_inputs(seed: int = 0) -> dict:
    key = jax.random.key(seed)
    keys = iter(jax.random.split(key, 64))

    def nrm(shape, scale):
        return scale * jax.random.normal(next(keys), shape, jnp.float32)

    def gain(shape):
        return 1.0 + nrm(shape, 0.02)

    centre = jnp.array([0.0, 1.0, 0.0], jnp.float32)[None, :, None]
    D = D_MODEL
    HF = HY_FILTER_HIDDEN
    return {
        'x': nrm((BATCH, SEQ, D), 1.0),
        'c': nrm((BATCH, D), 1.0),
        'ctx': nrm((BATCH, CTX_LEN, D), 1.0),
        'c_ctx': nrm((D,), 1.0),
        'mod_w': nrm((DEPTH, D, N_MOD * D), 0.5 * D ** -0.5),
        'mod_b': nrm((DEPTH, N_MOD * D), 0.02),
        'mix_norm': gain((DEPTH, D)),
        'w_in': nrm((DEPTH, D, IN_PROJ), D ** -0.5),
        'w_out': nrm((DEPTH, MIX_WIDTH, D), MIX_WIDTH ** -0.5),
        'rw_conv': centre + nrm((DEPTH, SHORT_CONV, RW_PROJ), 0.2),
        'rw_decay_up': nrm((DEPTH, 2, RW_DECAY_RANK, RW_WIDTH), 0.1),
        'rw_decay0': nrm((DEPTH, 2, RW_WIDTH), 0.5),
        'rw_a_up': nrm((DEPTH, 2, RW_A_RANK, RW_WIDTH), 0.5 * RW_A_RANK ** -0.5),
        'rw_a0': nrm((DEPTH, 2, RW_WIDTH), 0.5),
        'rw_gate_up': nrm((DEPTH, RW_GATE_RANK, RW_WIDTH), RW_GATE_RANK ** -0.5),
        'rw_k_k': 1.0 + nrm((DEPTH, RW_WIDTH), 0.1),
        'rw_k_a': 1.0 + nrm((DEPTH, RW_WIDTH), 0.1),
        'rw_r_k': nrm((DEPTH, RW_WIDTH), 0.1),
        'rw_gn_g': gain((DEPTH, RW_WIDTH)),
        'rw_gn_b': nrm((DEPTH, RW_WIDTH), 0.02),
        'mla_q_norm': gain((DEPTH, MLA_Q_RANK)),
        'mla_w_uq': nrm((DEPTH, MLA_Q_RANK, MLA_HEADS * MLA_QK_DIM), MLA_Q_RANK ** -0.5),
        'mla_kv_norm': gain((DEPTH, MLA_KV_RANK)),
        'mla_w_ukv': nrm((DEPTH, MLA_KV_RANK, MLA_HEADS * (MLA_NOPE_DIM + MLA_V_DIM)), MLA_KV_RANK ** -0.5),
        'mla_q_gain': gain((DEPTH, MLA_QK_DIM)),
        'mla_k_gain': gain((DEPTH, MLA_QK_DIM)),
        'hy_conv': centre + nrm((DEPTH, SHORT_CONV, HY_PROJ), 0.2),
        'hy_w1': nrm((DEPTH, HY_POS_DIM, HF), 1.0),
        'hy_b1': nrm((DEPTH, HF), 0.1),
        'hy_freq1': 1.0 + nrm((DEPTH, HF), 0.1),
        'hy_w2': nrm((DEPTH, HF, HF), HF ** -0.5),
        'hy_b2': nrm((DEPTH, HF), 0.1),
        'hy_freq2': 1.0 + nrm((DEPTH, HF), 0.1),
        'hy_w3': nrm((DEPTH, HF, HY_ORDER * 2 * HY_WIDTH), HF ** -0.5),
        'hy_b3': nrm((DEPTH, HY_ORDER * 2 * HY_WIDTH), 0.02),
        'hy_bias': nrm((DEPTH, HY_ORDER, HY_WIDTH), 0.1),
        'ffn_norm': gain((DEPTH, D)),
        'peer_wq': nrm((DEPTH, D, PEER_HEADS * PEER_QUERY_DIM), D ** -0.5),
        'peer_keys': nrm((DEPTH, PEER_HEADS, 2, PEER_N_KEYS, PEER_HALF), PEER_HALF ** -0.5),
        'peer_u': nrm((DEPTH, PEER_N_EXPERTS, D), D ** -0.5),
        'peer_v': nrm((DEPTH, PEER_N_EXPERTS, D), 0.5),
    }


def reference(x, c, ctx, c_ctx, mod_w, mod_b, mix_norm, w_in, w_out,
              rw_conv, rw_decay_up, rw_decay0, rw_a_up, rw_a0, rw_gate_up, rw_k_k, rw_k_a, rw_r_k, rw_gn_g, rw_gn_b,
              mla_q_norm, mla_w_uq, mla_kv_norm, mla_w_ukv, mla_q_gain, mla_k_gain,
              hy_conv, hy_w1, hy_b1, hy_freq1, hy_w2, hy_b2, hy_freq2, hy_w3, hy_b3, hy_bias,
              ffn_norm, peer_wq, peer_keys, peer_u, peer_v):
    L = x.shape[1]
    rope = axial_rope_tables(L)
    s_rw, s_mla = RW_PROJ, RW_PROJ + MLA_PROJ
    for li in range(DEPTH):
        need_ctx = li < DEPTH - 1
        mod_l = (jax.nn.silu(c) @ mod_w[li] + mod_b[li])[:, None, :]
        mod_c = (jax.nn.silu(c_ctx) @ mod_w[li] + mod_b[li])[None, None, :]
        shm_l, scm_l, gm_l, shf_l, scf_l, gf_l = jnp.split(mod_l, N_MOD, axis=-1)
        shm_c, scm_c, gm_c, shf_c, scf_c, gf_c = jnp.split(mod_c, N_MOD, axis=-1)

        p_l = (rms_norm(x, mix_norm[li]) * (1.0 + scm_l) + shm_l) @ w_in[li]
        p_c = (rms_norm(ctx, mix_norm[li]) * (1.0 + scm_c) + shm_c) @ w_in[li]
        rw_l, rw_c = rwkv7_mixer(p_l[..., :s_rw], p_c[..., :s_rw], rw_conv[li], rw_decay_up[li], rw_decay0[li],
                                 rw_a_up[li], rw_a0[li], rw_gate_up[li], rw_k_k[li], rw_k_a[li], rw_r_k[li],
                                 rw_gn_g[li], rw_gn_b[li], need_ctx)
        ml_l, ml_c = mla_mixer(p_l[..., s_rw:s_mla], p_c[..., s_rw:s_mla], rope, mla_q_norm[li], mla_w_uq[li],
                               mla_kv_norm[li], mla_w_ukv[li], mla_q_gain[li], mla_k_gain[li], need_ctx)
        hy_prm = (hy_conv[li], hy_w1[li], hy_b1[li], hy_freq1[li], hy_w2[li], hy_b2[li], hy_freq2[li],
                  hy_w3[li], hy_b3[li], hy_bias[li])
        hy_l = hyena_mixer(p_l[..., s_mla:], *hy_prm)
        x = x + gm_l * (jnp.concatenate([rw_l, ml_l, hy_l], axis=-1) @ w_out[li])

        x = x + gf_l * peer_ffn(rms_norm(x, ffn_norm[li]) * (1.0 + scf_l) + shf_l,
                                peer_wq[li], peer_keys[li], peer_u[li], peer_v[li])

        if need_ctx:
            hy_c = hyena_mixer(p_c[..., s_mla:], *hy_prm)
            ctx = ctx + gm_c * (jnp.concatenate([rw_c, ml_c, hy_c], axis=-1) @ w_out[li])
            ctx = ctx + gf_c * peer_ffn(rms_norm(ctx, ffn_norm[li]) * (1.0 + scf_c) + shf_c,
                                        peer_wq[li], peer_keys[li], peer_u[li], peer_v[li])
    return x
```

```python
import math
from contextlib import ExitStack
import numpy as np
import concourse.bass as bass
import concourse.mybir as mybir
from concourse.bass_utils import run_bass_kernel_spmd

F32 = mybir.dt.float32
BF16 = mybir.dt.bfloat16
I32 = mybir.dt.int32
U32 = mybir.dt.uint32
AF = mybir.ActivationFunctionType
ALU = mybir.AluOpType
AX = mybir.AxisListType

D = 1024
SEQ = 4096
CTX = 256
T = SEQ + CTX
NT = T // 128
DEPTH = 2
RW_PROJ, MLA_PROJ, HY_PROJ = 1408, 416, 768
IN_PROJ = 2592
NORM_EPS = 1e-6

SEM_CAP = 30000


class Buf:
    __slots__ = ("w", "r", "name")

    def __init__(self, name=""):
        self.w = None
        self.r = {}
        self.name = name


class KB:
    def __init__(self):
        self.nc = bass.Bass("TRN2", target_bir_lowering=False)
        nc = self.nc
        self.engs = {"pe": nc.tensor, "act": nc.scalar, "dve": nc.vector,
                     "pool": nc.gpsimd, "sp": nc.sync}
        self.sem = {}
        self.cnt = {}
        self.seen = {e: {} for e in self.engs}
        self.nsem = 0
        self.semh = {}
        self.dma_pool = {}
        self.dma_rr = {}
        self.ndma_sems = {"sp": 12, "pool": 16, "act": 4}
        self.n_ins = 0
        self.same_engine_sync = True

    def new_sem(self, name):
        h = self.nc.alloc_semaphore(name=f"{name}_{self.nsem}")
        sid = self.nsem
        self.nsem += 1
        self.semh[sid] = h
        return sid

    def _wait(self, e, toks):
        seen = self.seen[e]
        best = {}
        for (sid, val, src) in toks:
            if src == "pe" and e == "pe":
                continue
            if (not self.same_engine_sync) and src == e:
                continue
            if seen.get(sid, 0) >= val:
                continue
            if best.get(sid, 0) < val:
                best[sid] = val
        for sid, val in best.items():
            self.engs[e].wait_ge(self.semh[sid], val)
            seen[sid] = val
            self.n_ins += 1

    def _deps(self, r, w):
        toks = []
        for b in r:
            if b.w is not None:
                toks.append(b.w)
        for b in w:
            if b.w is not None:
                toks.append(b.w)
            toks.extend(b.r.values())
        return toks

    def _record(self, tok, r, w):
        for b in r:
            old = b.r.get(tok[0])
            if old is None or old[1] < tok[1]:
                b.r[tok[0]] = tok
        for b in w:
            b.w = tok
            b.r = {}

    def op(self, e, fn, r=(), w=()):
        self._wait(e, self._deps(r, w))
        ins = fn(self.engs[e])
        if e not in self.sem or self.cnt[e] >= SEM_CAP:
            self.sem[e] = self.new_sem(e)
            self.cnt[e] = 0
        self.cnt[e] += 1
        ins.then_inc(self.semh[self.sem[e]], 1)
        tok = (self.sem[e], self.cnt[e], e)
        self._record(tok, r, w)
        self.n_ins += 1
        return tok

    def dma(self, q, out, in_, r=(), w=(), **kw):
        toks = self._deps(r, w)
        if q not in self.dma_pool:
            self.dma_pool[q] = [[self.new_sem(f"dma{q}"), 0] for _ in range(self.ndma_sems[q])]
            self.dma_rr[q] = 0
        slot = self.dma_rr[q]
        self.dma_rr[q] = (slot + 1) % len(self.dma_pool[q])
        ent = self.dma_pool[q][slot]
        if ent[1] + 16 > SEM_CAP:
            ent[0] = self.new_sem(f"dma{q}")
            ent[1] = 0
        if ent[1] > 0:
            toks.append((ent[0], ent[1], "dma"))
        self._wait(q, toks)
        ins = self.engs[q].dma_start(out=out, in_=in_, **kw)
        ent[1] += 16
        ins.then_inc(self.semh[ent[0]], 16)
        tok = (ent[0], ent[1], "dma")
        self._record(tok, r, w)
        self.n_ins += 1
        return tok

    def idma(self, out, in_, in_off, r=(), w=(), **kw):
        q = "pool"
        toks = self._deps(r, w)
        if q not in self.dma_pool:
            self.dma_pool[q] = [[self.new_sem(f"dma{q}"), 0] for _ in range(self.ndma_sems[q])]
            self.dma_rr[q] = 0
        slot = self.dma_rr[q]
        self.dma_rr[q] = (slot + 1) % len(self.dma_pool[q])
        ent = self.dma_pool[q][slot]
        if ent[1] + 16 > SEM_CAP:
            ent[0] = self.new_sem(f"dma{q}")
            ent[1] = 0
        if ent[1] > 0:
            toks.append((ent[0], ent[1], "dma"))
        self._wait(q, toks)
        ins = self.nc.gpsimd.indirect_dma_start(out=out, out_offset=None, in_=in_, in_offset=in_off, **kw)
        ent[1] += 16
        ins.then_inc(self.semh[ent[0]], 16)
        tok = (ent[0], ent[1], "dma")
        self._record(tok, r, w)
        self.n_ins += 1
        return tok

    def wait_all(self, e, bufs):
        toks = []
        for b in bufs:
            if b.w is not None:
                toks.append(b.w)
            toks.extend(b.r.values())
        self._wait(e, toks)


class Tl:
    def __init__(self, t, name, nsub=1):
        self.t = t
        self.b = Buf(name)
        self.sub = [Buf(f"{name}{i}") for i in range(nsub)] if nsub > 1 else None

    def __getitem__(self, idx):
        return self.t[idx]


_UID = [0]


def sb(es, kb, name, shape, dt, nsub=1):
    _UID[0] += 1
    name = f"{name}_{_UID[0]}"
    return Tl(es.enter_context(kb.nc.sbuf_tensor(name, list(shape), dt)), name, nsub)


def ps(es, kb, name, shape, dt, nsub=1):
    _UID[0] += 1
    name = f"{name}_{_UID[0]}"
    return Tl(es.enter_context(kb.nc.psum_tensor(name, list(shape), dt)), name, nsub)


def barrier(kb):
    toks = []
    for e in kb.sem:
        toks.append((kb.sem[e], kb.cnt[e], "bar"))
    for q in kb.dma_pool:
        for ent in kb.dma_pool[q]:
            if ent[1] > 0:
                toks.append((ent[0], ent[1], "bar"))
    for e in kb.engs:
        kb._wait(e, toks)


INPUT_NAMES = ['mod_w', 'mod_b', 'mix_norm', 'w_in', 'w_out',
               'rw_conv', 'rw_decay_up', 'rw_decay0', 'rw_a_up', 'rw_a0', 'rw_gate_up', 'rw_k_k', 'rw_k_a',
               'rw_r_k', 'rw_gn_g', 'rw_gn_b',
               'mla_q_norm', 'mla_w_uq', 'mla_kv_norm', 'mla_w_ukv', 'mla_q_gain', 'mla_k_gain',
               'hy_conv', 'hy_w1', 'hy_b1', 'hy_freq1', 'hy_w2', 'hy_b2', 'hy_freq2', 'hy_w3', 'hy_b3', 'hy_bias',
               'ffn_norm', 'peer_wq', 'peer_keys', 'peer_u', 'peer_v']

WEIGHT_SHAPES = {
    'mod_w': (2, 1024, 6144), 'mod_b': (2, 6144), 'mix_norm': (2, 1024), 'w_in': (2, 1024, 2592),
    'w_out': (2, 1024, 1024), 'rw_conv': (2, 3, 1408), 'rw_decay_up': (2, 2, 64, 384), 'rw_decay0': (2, 2, 384),
    'rw_a_up': (2, 2, 64, 384), 'rw_a0': (2, 2, 384), 'rw_gate_up': (2, 128, 384), 'rw_k_k': (2, 384),
    'rw_k_a': (2, 384), 'rw_r_k': (2, 384), 'rw_gn_g': (2, 384), 'rw_gn_b': (2, 384),
    'mla_q_norm': (2, 256), 'mla_w_uq': (2, 256, 576), 'mla_kv_norm': (2, 128), 'mla_w_ukv': (2, 128, 768),
    'mla_q_gain': (2, 96), 'mla_k_gain': (2, 96), 'hy_conv': (2, 3, 768), 'hy_w1': (2, 33, 64), 'hy_b1': (2, 64),
    'hy_freq1': (2, 64), 'hy_w2': (2, 64, 64), 'hy_b2': (2, 64), 'hy_freq2': (2, 64), 'hy_w3': (2, 64, 1024),
    'hy_b3': (2, 1024), 'hy_bias': (2, 2, 256), 'ffn_norm': (2, 1024), 'peer_wq': (2, 1024, 2048),
    'peer_keys': (2, 8, 2, 128, 128), 'peer_u': (2, 16384, 1024), 'peer_v': (2, 16384, 1024),
}


class Prog:
    def __init__(self, dbg=None, nlayers=DEPTH):
        self.kb = KB()
        self.nc = self.kb.nc
        self.dbg = dbg
        self.nlayers = nlayers
        nc = self.nc
        self.din = {}
        self.din['x'] = Tl(nc.dram_tensor("x", [SEQ, D], F32, kind="ExternalInput").ap(), "x")
        self.din['ctx'] = Tl(nc.dram_tensor("ctx", [CTX, D], F32, kind="ExternalInput").ap(), "ctx")
        self.din['c2'] = Tl(nc.dram_tensor("c2", [2, D], F32, kind="ExternalInput").ap(), "c2")
        for n in INPUT_NAMES:
            self.din[n] = Tl(nc.dram_tensor(n, list(WEIGHT_SHAPES[n]), F32, kind="ExternalInput").ap(), n)
        self.consts = {}
        self.scr = {}

    def const_in(self, name, shape, dt=F32):
        t = Tl(self.nc.dram_tensor(name, list(shape), dt, kind="ExternalInput").ap(), name)
        self.consts[name] = t
        return t

    def scratch(self, name, shape, dt=F32):
        t = Tl(self.nc.dram_tensor(name, list(shape), dt, kind="Internal").ap(), name)
        self.scr[name] = t
        return t

    def out_tensor(self, name, shape, dt=F32):
        t = Tl(self.nc.dram_tensor(name, list(shape), dt, kind="ExternalOutput").ap(), name)
        return t

    def phase_mod(self, li):
        kb, nc = self.kb, self.nc
        modrow = self.scr['modrow']
        with ExitStack() as es:
            cT = sb(es, kb, "cT", [128, 8, 2], F32)
            scT = sb(es, kb, "scT", [128, 8, 2], F32)
            sig = sb(es, kb, "sigT", [128, 8, 2], F32)
            mw = [sb(es, kb, f"mw{i}", [128, 3072], F32) for i in range(2)]
            modb = sb(es, kb, "modb", [2, 6144], F32)
            modsb = sb(es, kb, "modsb", [2, 6144], F32)
            pm = ps(es, kb, "pm", [2, 3072], F32)
            c2 = self.din['c2']
            for r_ in range(2):
                kb.dma("sp", cT[:, :, r_], c2.t[r_].rearrange("(k p) -> p k", p=128), r=[c2.b], w=[cT.b],
                       allow_slow_non_contiguous=True)
            for r_ in range(2):
                kb.dma("sp", modb[r_:r_ + 1, :], self.din['mod_b'].t[li:li + 1, :], w=[modb.b])
            kb.op("act", lambda e: e.activation(out=sig[:], in_=cT[:], func=AF.Sigmoid), r=[cT.b], w=[sig.b])
            kb.op("dve", lambda e: e.tensor_tensor(out=scT[:], in0=cT[:], in1=sig[:], op=ALU.mult),
                  r=[cT.b, sig.b], w=[scT.b])
            it = 0
            for half in range(2):
                for k in range(8):
                    m = mw[it % 2]
                    it += 1
                    kb.dma("sp", m[:], self.din['mod_w'].t[li, k * 128:(k + 1) * 128, half * 3072:(half + 1) * 3072],
                           w=[m.b])
                    for n in range(6):
                        kb.op("pe", lambda e: e.matmul(pm[:, n * 512:(n + 1) * 512], lhsT=scT[:, k, :],
                                                       rhs=m[:, n * 512:(n + 1) * 512], start=(k == 0), stop=(k == 7)),
                              r=[scT.b, m.b], w=[pm.b])
                kb.op("dve", lambda e: e.tensor_tensor(out=modsb[:, half * 3072:(half + 1) * 3072], in0=pm[:],
                                                       in1=modb[:, half * 3072:(half + 1) * 3072], op=ALU.add),
                      r=[pm.b, modb.b], w=[modsb.b])
            kb.dma("sp", modrow[:, :], modsb[:], r=[modsb.b], w=[modrow.b])
        barrier(kb)

    def norm_to_xsT(self, es, li, xsT, tiles, norm_name, sc_idx, sh_idx, ident):
        kb, nc = self.kb, self.nc
        modrow = self.scr['modrow']
        with ExitStack() as es2:
            G = [sb(es2, kb, f"Gbc{w_}", [128, D], F32) for w_ in range(2)]
            SH = [sb(es2, kb, f"SHbc{w_}", [128, D], F32) for w_ in range(2)]
            gn = sb(es2, kb, "gnbc", [128, D], F32)
            kb.dma("sp", gn[:], self.din[norm_name].t[li:li + 1, :].partition_broadcast(128), w=[gn.b])
            for w_ in range(2):
                kb.dma("sp", G[w_][:], modrow.t[w_:w_ + 1, sc_idx * D:(sc_idx + 1) * D].partition_broadcast(128),
                       r=[modrow.b], w=[G[w_].b])
                kb.dma("sp", SH[w_][:], modrow.t[w_:w_ + 1, sh_idx * D:(sh_idx + 1) * D].partition_broadcast(128),
                       r=[modrow.b], w=[SH[w_].b])
                kb.op("dve", lambda e: e.scalar_tensor_tensor(out=G[w_][:], in0=G[w_][:], scalar=1.0, in1=gn[:],
                                                              op0=ALU.add, op1=ALU.mult),
                      r=[gn.b], w=[G[w_].b])
            xt = [sb(es2, kb, f"xt{i}", [128, D], F32) for i in range(2)]
            junk = sb(es2, kb, "junk", [128, D], F32)
            ss = [sb(es2, kb, f"ss{i}", [128, 1], F32) for i in range(2)]
            xs = [sb(es2, kb, f"xs{i}", [128, D], F32) for i in range(2)]
            xsb = [sb(es2, kb, f"xsb{i}", [128, D], BF16) for i in range(2)]
            tp = [ps(es2, kb, f"tp{i}", [128, 8, 128], BF16) for i in range(2)]
            for i, (src, r0, c0, which) in enumerate(tiles):
                b = i % 2
                kb.dma("sp", xt[b][:], src.t[r0:r0 + 128, :], r=[src.b], w=[xt[b].b])
                kb.op("act", lambda e: e.activation(out=junk[:], in_=xt[b][:], func=AF.Square, accum_out=ss[b][:]),
                      r=[xt[b].b], w=[junk.b, ss[b].b])
                kb.op("dve", lambda e: e.tensor_scalar(out=ss[b][:], in0=ss[b][:], scalar1=1.0 / D, scalar2=NORM_EPS,
                                                       op0=ALU.mult, op1=ALU.add), w=[ss[b].b])
                kb.op("act", lambda e: e.activation(out=ss[b][:], in_=ss[b][:], func=AF.Sqrt), w=[ss[b].b])
                kb.op("dve", lambda e: e.reciprocal(out=ss[b][:], in_=ss[b][:]), w=[ss[b].b])
                kb.op("dve", lambda e: e.scalar_tensor_tensor(out=xs[b][:], in0=xt[b][:], scalar=ss[b][:, 0:1],
                                                              in1=G[which][:], op0=ALU.mult, op1=ALU.mult),
                      r=[xt[b].b, ss[b].b, G[which].b], w=[xs[b].b])
                kb.op("pool", lambda e: e.tensor_tensor(out=xsb[b][:], in0=xs[b][:], in1=SH[which][:], op=ALU.add),
                      r=[xs[b].b, SH[which].b], w=[xsb[b].b])
                for k in range(8):
                    kb.op("pe", lambda e: e.transpose(tp[b][:, k, :], xsb[b][:, k * 128:(k + 1) * 128], ident[:]),
                          r=[xsb[b].b, ident.b], w=[tp[b].b])
                kb.op("act", lambda e: e.copy(out=xsT[:, :, c0:c0 + 128], in_=tp[b][:]),
                      r=[tp[b].b], w=[xsT.b])

    def make_ident(self, es, dt, name):
        kb = self.kb
        ident = sb(es, kb, name, [128, 128], dt)
        src = self.consts['ident_f32']
        if dt == F32:
            kb.dma("sp", ident[:], src.t[:, :], w=[ident.b])
        else:
            with ExitStack() as es2:
                tmp = sb(es2, kb, name + "_tmp", [128, 128], F32)
                kb.dma("sp", tmp[:], src.t[:, :], w=[tmp.b])
                kb.op("dve", lambda e: e.tensor_copy(out=ident[:], in_=tmp[:]), r=[tmp.b], w=[ident.b])
                kb.wait_all("sp", [tmp.b])
                barrier(kb)
        return ident

    def phase_inproj(self, li, first):
        kb, nc = self.kb, self.nc
        zT = self.scr['zT']
        krope = self.scr['krope_tm']
        xsrc_l = self.din['x'] if first else self.scr['xcur']
        xsrc_c = self.din['ctx'] if first else self.scr['xcur']
        tiles = []
        for i in range(32):
            tiles.append((xsrc_l, i * 128, i * 128, 0))
        for i in range(2):
            tiles.append((xsrc_c, (i * 128) if first else (SEQ + i * 128), SEQ + i * 128, 1))
        with ExitStack() as es:
            identb = self.make_ident(es, BF16, "identb")
            xsT = sb(es, kb, "xsT", [128, 8, T], BF16)
            self.norm_to_xsT(es, li, xsT, tiles, 'mix_norm', 1, 0, identb)
            barrier(kb)
            W = sb(es, kb, "Win", [128, 8, IN_PROJ], BF16)
            wst = [sb(es, kb, f"wst{i}", [128, IN_PROJ], F32) for i in range(2)]
            for k in range(8):
                kb.dma("sp", wst[k % 2][:], self.din['w_in'].t[li, k * 128:(k + 1) * 128, :], w=[wst[k % 2].b])
                kb.op("pool", lambda e: e.tensor_copy(out=W[:, k, :], in_=wst[k % 2][:]), r=[wst[k % 2].b], w=[W.b])
            cw_rw = sb(es, kb, "cw_rw", [128, 11, 3], F32)
            cw_hy = sb(es, kb, "cw_hy", [128, 6, 3], F32)
            for j in range(3):
                kb.dma("sp", cw_rw[:, :, j], self.din['rw_conv'].t[li, j].rearrange("(c p) -> p c", p=128),
                       w=[cw_rw.b], allow_slow_non_contiguous=True)
                kb.dma("sp", cw_hy[:, :, j], self.din['hy_conv'].t[li, j].rearrange("(c p) -> p c", p=128),
                       w=[cw_hy.b], allow_slow_non_contiguous=True)
            chunks = []
            for c in range(11):
                chunks.append((c * 128, 128, cw_rw, c))
            for c in range(3):
                chunks.append((RW_PROJ + c * 128, 128, None, None))
            for c in range(6):
                chunks.append((RW_PROJ + MLA_PROJ + c * 128, 128, cw_hy, c))
            PW = T + 3
            prow = [sb(es, kb, f"prow{i}", [128, PW], F32) for i in range(2)]
            zrow = [sb(es, kb, f"zrow{i}", [128, PW], F32) for i in range(2)]
            pp = [ps(es, kb, f"pp{i}", [128, 512], F32) for i in range(4)]
            for i in range(2):
                kb.op("pool", lambda e: e.memset(prow[i][:], 0.0), w=[prow[i].b])
            npp = 0
            for ci, (c0, csz, cw, cidx) in enumerate(chunks):
                pr = prow[ci % 2]
                zr = zrow[ci % 2]
                for tb in range(9):
                    t0 = tb * 512
                    n = 512 if tb < 8 else CTX
                    p_ = pp[npp % 4]
                    npp += 1
                    for k in range(8):
                        kb.op("pe", lambda e: e.matmul(p_[:csz, :n], lhsT=W[:, k, c0:c0 + csz],
                                                       rhs=xsT[:, k, t0:t0 + n], start=(k == 0), stop=(k == 7)),
                              r=[W.b, xsT.b], w=[p_.b])
                    o0 = 1 + t0 if tb < 8 else SEQ + 2
                    if cw is None:
                        kb.op("act", lambda e: e.copy(out=zr[:csz, o0:o0 + n], in_=p_[:csz, :n]), r=[p_.b], w=[zr.b])
                    else:
                        kb.op("act", lambda e: e.copy(out=pr[:csz, o0:o0 + n], in_=p_[:csz, :n]), r=[p_.b], w=[pr.b])
                if cw is not None:
                    kb.op("act", lambda e: e.activation(out=zr[:, :], in_=pr[:, :], func=AF.Copy,
                                                        scale=cw[:, cidx, 1:2]), r=[pr.b, cw.b], w=[zr.b])
                    kb.op("dve", lambda e: e.scalar_tensor_tensor(out=zr[:, 1:PW], in0=pr[:, 0:PW - 1],
                                                                  scalar=cw[:, cidx, 0:1], in1=zr[:, 1:PW],
                                                                  op0=ALU.mult, op1=ALU.add),
                          r=[pr.b, cw.b], w=[zr.b])
                    kb.op("dve", lambda e: e.scalar_tensor_tensor(out=zr[:, 0:PW - 1], in0=pr[:, 1:PW],
                                                                  scalar=cw[:, cidx, 2:3], in1=zr[:, 0:PW - 1],
                                                                  op0=ALU.mult, op1=ALU.add),
                          r=[pr.b, cw.b], w=[zr.b])
                kb.dma("sp", zT.t[ci, :, 0:SEQ], zr[:, 1:1 + SEQ], r=[zr.b], w=[zT.b])
                kb.dma("sp", zT.t[ci, :, SEQ:T], zr[:, SEQ + 2:SEQ + 2 + CTX], r=[zr.b], w=[zT.b])
            kr = sb(es, kb, "kr_sb", [128, NT, 32], F32)
            c0 = RW_PROJ + 384
            for tt in range(NT):
                p_ = pp[npp % 4]
                npp += 1
                for k in range(8):
                    kb.op("pe", lambda e: e.matmul(p_[:, :32], lhsT=xsT[:, k, tt * 128:(tt + 1) * 128],
                                                   rhs=W[:, k, c0:c0 + 32], start=(k == 0), stop=(k == 7)),
                          r=[W.b, xsT.b], w=[p_.b])
                kb.op("act", lambda e: e.copy(out=kr[:, tt, :], in_=p_[:, :32]), r=[p_.b], w=[kr.b])
            kb.dma("sp", krope.t.rearrange("(n p) f -> p n f", p=128), kr[:], r=[kr.b], w=[krope.b])
        barrier(kb)


    def fm_to_tm(self, src, dst_ap_pnf, identf, stg, tpp, ntiles, state):
        kb = self.kb
        for n0 in range(0, ntiles, 4):
            nn = min(4, ntiles - n0)
            tp = tpp[state[0] % 2]
            state[0] += 1
            for j in range(nn):
                kb.op("pe", lambda e: e.transpose(tp[:, j, :], src[:, (n0 + j) * 128:(n0 + j + 1) * 128], identf[:]),
                      r=[src.b, identf.b], w=[tp.b])
            eng = "act" if (state[0] % 2) else "dve"
            if eng == "act":
                kb.op("act", lambda e: e.copy(out=stg[:, n0:n0 + nn, :], in_=tp[:, 0:nn, :]), r=[tp.b], w=[stg.b])
            else:
                kb.op("dve", lambda e: e.tensor_copy(out=stg[:, n0:n0 + nn, :], in_=tp[:, 0:nn, :]), r=[tp.b], w=[stg.b])

    def phase_rw_prep(self, li):
        kb, nc = self.kb, self.nc
        zT = self.scr['zT']
        S = self.scr
        with ExitStack() as es:
            identf = self.make_ident(es, F32, "identf")
            bones = sb(es, kb, "bones", [128, 128], F32)
            kb.dma("sp", bones[:], self.consts['blockones'].t[:, :], w=[bones.b])
            dup = sb(es, kb, "dup", [64, 2, 384], F32)
            aup = sb(es, kb, "aup", [64, 2, 384], F32)
            alo = sb(es, kb, "alo", [64, T], F32)
            gup = sb(es, kb, "gup", [128, 384], F32)
            kb.dma("sp", dup[:], self.din['rw_decay_up'].t[li].rearrange("d r c -> r d c"), w=[dup.b])
            kb.dma("sp", aup[:, :, :], self.din['rw_a_up'].t[li].rearrange("d r c -> r d c"), w=[aup.b])
            kb.dma("sp", gup[:], self.din['rw_gate_up'].t[li], w=[gup.b])
            d0c = sb(es, kb, "d0c", [128, 2, 3], F32)
            a0c = sb(es, kb, "a0c", [128, 2, 3], F32)
            kkc = sb(es, kb, "kkc", [128, 3], F32)
            kac = sb(es, kb, "kac", [128, 3], F32)
            omka = sb(es, kb, "omka", [128, 3], F32)
            for d in range(2):
                kb.dma("sp", d0c[:, d, :], self.din['rw_decay0'].t[li, d].rearrange("(g p) -> p g", p=128),
                       w=[d0c.b], allow_slow_non_contiguous=True)
                kb.dma("sp", a0c[:, d, :], self.din['rw_a0'].t[li, d].rearrange("(g p) -> p g", p=128),
                       w=[a0c.b], allow_slow_non_contiguous=True)
            kb.dma("sp", kkc[:], self.din['rw_k_k'].t[li].rearrange("(g p) -> p g", p=128), w=[kkc.b],
                   allow_slow_non_contiguous=True)
            kb.dma("sp", kac[:], self.din['rw_k_a'].t[li].rearrange("(g p) -> p g", p=128), w=[kac.b],
                   allow_slow_non_contiguous=True)
            kb.op("dve", lambda e: e.tensor_scalar(out=omka[:], in0=kac[:], scalar1=-1.0, scalar2=1.0,
                                                   op0=ALU.mult, op1=ALU.add), r=[kac.b], w=[omka.b])
            da = sb(es, kb, "da", [64, T], F32)
            gs = sb(es, kb, "gs", [128, T], F32)
            kg = sb(es, kb, "kg", [128, T], F32)
            t1 = sb(es, kb, "t1", [128, T], F32)
            kkg = sb(es, kb, "kkg", [128, T], F32)
            adg = sb(es, kb, "adg", [128, T], F32)
            tA = sb(es, kb, "tA", [128, T], F32)
            tB = sb(es, kb, "tB", [128, T], F32)
            stg = [sb(es, kb, f"stg{i}", [128, NT, 128], F32) for i in range(2)]
            pp = [ps(es, kb, f"pq{i}", [128, 512], F32) for i in range(4)]
            tpp = [ps(es, kb, f"tq{i}", [128, 4, 128], F32) for i in range(2)]
            st = [0]
            nst = [0]
            npp = [0]

            def to_tm(src, dst_tl, col0, extra=None):
                sg = stg[nst[0] % 2]
                nst[0] += 1
                self.fm_to_tm(src, None, identf, sg, tpp, NT, st)
                if dst_tl is not None:
                    kb.dma("sp", dst_tl.rearrange("(n p) f -> p n f", p=128)[:, :, col0:col0 + 128], sg[:],
                           r=[sg.b], w=[self._prep_buf])
                if extra is not None:
                    for (dst_ap, c0, c1) in extra:
                        kb.dma("pool", dst_ap, sg[:, :, c0:c1], r=[sg.b], w=[self._prep_buf])

            def rows_dst(ap_T_384, g, c0, c1):
                return ap_T_384.rearrange("(n p) (g c) -> p n g c", p=128, c=128)[:, :, g, c0:c1]

            def blocks(fn):
                for tb in range(9):
                    t0 = tb * 512
                    n = 512 if tb < 8 else CTX
                    p_ = pp[npp[0] % 4]
                    npp[0] += 1
                    fn(p_, t0, n)

            kb.dma("sp", da[0:64, :], zT.t[9, 0:64, :], r=[zT.b], w=[da.b])
            kb.dma("sp", alo[:, :], zT.t[9, 64:128, :], r=[zT.b], w=[alo.b])
            kb.dma("sp", gs[:], zT.t[10], r=[zT.b], w=[gs.b])
            kb.op("act", lambda e: e.activation(out=da[0:64, :], in_=da[0:64, :], func=AF.Tanh), w=[da.b])
            kb.op("act", lambda e: e.activation(out=gs[:], in_=gs[:], func=AF.Sigmoid), w=[gs.b])
            import os
            PSTOP = int(os.environ.get("PREP_STOP", "99"))
            for g in range(3):
                gc = slice(g * 128, (g + 1) * 128)
                kb.dma("sp", tA[:], zT.t[g], r=[zT.b], w=[tA.b])
                kb.dma("pool", S['rb_fm'].t[g], tA[:], r=[tA.b], w=[self._prep_buf])
                to_tm(tA, S['r_tm'].t, g * 128)
                if PSTOP == 1:
                    break
                kb.dma("sp", tB[:], zT.t[6 + g], r=[zT.b], w=[tB.b])
                to_tm(tB, S['v_tm'].t, g * 128,
                      extra=[(S['v6_tm'].t[g * 2 + par].rearrange("(n p) (gg c) -> p n gg c", p=128, c=64)[:, :, g, :],
                              par * 64, par * 64 + 64) for par in range(2)])
                def fg(p_, t0, n):
                    kb.op("pe", lambda e: e.matmul(p_[:, :n], lhsT=gup[:, gc], rhs=gs[:, t0:t0 + n], start=True, stop=True),
                          r=[gup.b, gs.b], w=[p_.b])
                    kb.op("act", lambda e: e.copy(out=tA[:, t0:t0 + n], in_=p_[:, :n]), r=[p_.b], w=[tA.b])
                blocks(fg)
                to_tm(tA, S['g_tm'].t, g * 128)
                if PSTOP == 2:
                    break
                kb.dma("sp", kg[:], zT.t[3 + g], r=[zT.b], w=[kg.b])
                to_tm(kg, S['k_tm'].t, g * 128)
                kb.op("act", lambda e: e.activation(out=t1[:], in_=kg[:], func=AF.Copy, scale=kkc[:, g:g + 1]),
                      r=[kg.b, kkc.b], w=[t1.b])
                kb.op("dve", lambda e: e.tensor_tensor(out=tB[:], in0=t1[:], in1=t1[:], op=ALU.mult), r=[t1.b], w=[tB.b])
                def fk(p_, t0, n):
                    kb.op("pe", lambda e: e.matmul(p_[:, :n], lhsT=bones[:], rhs=tB[:, t0:t0 + n], start=True, stop=True),
                          r=[bones.b, tB.b], w=[p_.b])
                    kb.op("dve", lambda e: e.tensor_scalar(out=kkg[:, t0:t0 + n], in0=p_[:, :n], scalar1=1e-12, scalar2=None,
                                                           op0=ALU.add), r=[p_.b], w=[kkg.b])
                blocks(fk)
                kb.op("act", lambda e: e.activation(out=kkg[:], in_=kkg[:], func=AF.Sqrt), w=[kkg.b])
                kb.op("dve", lambda e: e.reciprocal(out=kkg[:], in_=kkg[:]), w=[kkg.b])
                kb.op("dve", lambda e: e.tensor_tensor(out=kkg[:], in0=kkg[:], in1=t1[:], op=ALU.mult), r=[t1.b], w=[kkg.b])
                kb.dma("pool", S['kk_fm'].t[g], kkg[:], r=[kkg.b], w=[self._prep_buf])
                to_tm(kkg, None, 0, extra=[(S['kk6_tm'].t[g * 2 + par].rearrange("(n p) c -> p n c", p=128)[:, :, par * 64:par * 64 + 64],
                                            par * 64, par * 64 + 64) for par in range(2)])
                if PSTOP == 3:
                    break
                for d in range(2):
                    def fd(p_, t0, n):
                        kb.op("pe", lambda e: e.matmul(p_[:, :n], lhsT=dup[0:64, d, gc], rhs=da[0:64, t0:t0 + n],
                                                       start=True, stop=True), r=[dup.b, da.b], w=[p_.b])
                        kb.op("act", lambda e: e.activation(out=tA[:, t0:t0 + n], in_=p_[:, :n], func=AF.Sigmoid,
                                                            bias=d0c[:, d, g:g + 1]), r=[p_.b, d0c.b], w=[tA.b])
                    blocks(fd)
                    kb.op("act", lambda e: e.activation(out=tA[:], in_=tA[:], func=AF.Exp, scale=-0.6065306597),
                          w=[tA.b])
                    kb.dma("sp", S['w_fm'].t[d, g], tA[:], r=[tA.b], w=[self._prep_buf])
                    def fa(p_, t0, n):
                        kb.op("pe", lambda e: e.matmul(p_[:, :n], lhsT=aup[:, d, gc], rhs=alo[:, t0:t0 + n],
                                                       start=True, stop=True), r=[aup.b, alo.b], w=[p_.b])
                        kb.op("act", lambda e: e.activation(out=adg[:, t0:t0 + n], in_=p_[:, :n], func=AF.Sigmoid,
                                                            bias=a0c[:, d, g:g + 1]), r=[p_.b, a0c.b], w=[adg.b])
                    blocks(fa)
                    kb.op("dve", lambda e: e.tensor_scalar(out=tA[:], in0=adg[:], scalar1=kac[:, g:g + 1],
                                                           scalar2=omka[:, g:g + 1], op0=ALU.mult, op1=ALU.add),
                          r=[adg.b, kac.b, omka.b], w=[tA.b])
                    kb.op("dve", lambda e: e.tensor_tensor(out=tA[:], in0=tA[:], in1=kg[:], op=ALU.mult),
                          r=[kg.b], w=[tA.b])
                    to_tm(tA, None, 0, extra=[(S['krep6_tm'].t[d, g * 2 + par].rearrange("(n p) c -> p n c", p=128)[:, :, par * 64:par * 64 + 64],
                                               par * 64, par * 64 + 64) for par in range(2)])
                    kb.op("dve", lambda e: e.scalar_tensor_tensor(out=tB[:], in0=adg[:], scalar=-1.0, in1=kkg[:],
                                                                  op0=ALU.mult, op1=ALU.mult),
                          r=[adg.b, kkg.b], w=[tB.b])
                    to_tm(tB, None, 0, extra=[(rows_dst(S['akk6_tm'].t[d, g * 2 + par], g, par * 64, par * 64 + 64),
                                               par * 64, par * 64 + 64) for par in range(2)])
        barrier(kb)

    def scr_b(self, ap):
        return self._prep_buf

    def phase_zero_init(self):
        kb = self.kb
        with ExitStack() as es:
            z = sb(es, kb, "zeros", [128, 8192], BF16)
            kb.op("pool", lambda e: e.memset(z[:], 0.0), w=[z.b])
            for (flat, tot) in ((self.scr['akk6_tm'].t.rearrange("d r t f -> (d r t f)"), 2 * 6 * T * 384),
                                (self.scr['krep6_tm'].t.rearrange("d r t f -> (d r t f)"), 2 * 6 * T * 128),
                                (self.scr['kk6_tm'].t.rearrange("r t f -> (r t f)"), 6 * T * 128),
                                (self.scr['v6_tm'].t.rearrange("r t f -> (r t f)"), 6 * T * 192)):
                per = 128 * 8192
                off = 0
                while off < tot:
                    n = min(per, tot - off)
                    cols = n // 128
                    kb.dma("sp", flat[off:off + n].rearrange("(p c) -> p c", p=128), z[:, :cols], r=[z.b],
                           w=[self._prep_buf])
                    off += n
        barrier(kb)

    def phase_rw_scan(self, li, nsteps=None):
        kb, nc = self.kb, self.nc
        S = self.scr
        TCF, TCR = 64, 8
        import os
        PARTS = os.environ.get("SCAN_PARTS", "SCUDYO")
        total = T if nsteps is None else nsteps
        pb = self._prep_buf
        with ExitStack() as es:
            St = sb(es, kb, "St", [128, 2, 3, 64], F32, nsub=2)
            kb.op("dve", lambda e: e.memset(St[:], 0.0), w=[St.sub[0], St.sub[1]])
            Stb = sb(es, kb, "Stb", [128, 2, 3, 64], BF16, nsub=2)
            kb.op("dve", lambda e: e.memset(Stb[:], 0.0), w=[Stb.sub[0], Stb.sub[1]])
            Wt = [[sb(es, kb, f"Wt{d}{c}", [128, 3, TCF], F32) for c in range(2)] for d in range(2)]
            KK = [[sb(es, kb, f"KK{d}{c}", [128, 3, 2, TCF], BF16) for c in range(2)] for d in range(2)]
            RR = [[sb(es, kb, f"RR{d}{c}", [128, 3, 2, TCF], BF16) for c in range(2)] for d in range(2)]
            L4 = [[sb(es, kb, f"L4{d}{c}", [4, TCR, 3, 128], BF16) for c in range(2)] for d in range(2)]
            RV = [[sb(es, kb, f"RV{d}{c}", [4, TCR, 3, 64], BF16) for c in range(2)] for d in range(2)]
            YF = [sb(es, kb, f"YF{d}", [64, 3, 2, TCF], F32) for d in range(2)]
            for d in range(2):
                for c in range(2):
                    for tl in (KK[d][c], RR[d][c]):
                        kb.op("pool", lambda e: e.memset(tl[:], 0.0), w=[tl.b])
            pskb = [ps(es, kb, f"psk{d}", [128, 512], F32) for d in range(2)]
            pdb = [ps(es, kb, f"pd{d}", [128, 512], F32) for d in range(2)]
            pyb = [ps(es, kb, f"py{d}", [128, 512], F32) for d in range(2)]
            psk_v = [pskb[d][0:2, 0:192].rearrange("p (g c) -> p g c", g=3) for d in range(2)]
            pd_v = [pdb[d][:, 0:192].rearrange("p (g c) -> p g c", g=3) for d in range(2)]
            py_v = [pyb[d][0:64, 0:6 * TCF].rearrange("p (g h t) -> p g h t", g=3, h=2) for d in range(2)]
            kk_pgt = S['kk_fm'].t.rearrange("g p t -> p g t")
            r_pgt = S['rb_fm'].t.rearrange("g p t -> p g t")

            def seg_time(s, d):
                if s < CTX:
                    return SEQ + s if d == 0 else T - 1 - s
                return s - CTX if d == 0 else SEQ - 1 - (s - CTX)

            def chunk_t0(s0, n, d):
                return seg_time(s0, d) if d == 0 else seg_time(s0 + n - 1, d)

            for s in range(total):
                cf, jf = divmod(s, TCF)
                cr, jr = divmod(s, TCR)
                cbf, cbr = cf % 2, cr % 2
                if jf == 0:
                    for d in range(2):
                        t0 = chunk_t0(s, TCF, d)
                        ts = slice(t0, t0 + TCF)
                        kb.dma("sp", Wt[d][cbf][:], S['w_fm'].t[d].rearrange("g p t -> p g t")[:, :, ts], r=[pb],
                               w=[Wt[d][cbf].b])
                        for h in range(2):
                            hp = slice(h * 64, (h + 1) * 64)
                            kb.dma("sp", KK[d][cbf][hp, :, h, :], kk_pgt[hp, :, ts], r=[pb], w=[KK[d][cbf].b])
                            kb.dma("sp", RR[d][cbf][hp, :, h, :], r_pgt[hp, :, ts], r=[pb], w=[RR[d][cbf].b])
                if jr == 0:
                    for d in range(2):
                        t0 = chunk_t0(s, TCR, d)
                        ts = slice(t0, t0 + TCR)
                        kb.dma("sp", L4[d][cbr][:].rearrange("r t g c -> r t (g c)"), S['L4_tm'].t[d, :, ts, :], r=[pb], w=[L4[d][cbr].b])
                        kb.dma("sp", RV[d][cbr][2:4, :, :, :], S['V2_tm'].t[:, ts, :, :], r=[pb], w=[RV[d][cbr].b])
                lf = [jf, TCF - 1 - jf]
                lr = [jr, TCR - 1 - jr]
                for d in range(2):
                    for g in range(3):
                        if 'S' not in PARTS:
                            continue
                        kb.op("pe", lambda e: e.matmul(psk_v[d][:, g, :], lhsT=KK[d][cbf][:, g, :, lf[d]],
                                                       rhs=Stb[:, d, g, :], start=True, stop=True),
                              r=[KK[d][cbf].b, Stb.sub[d]], w=[pskb[d].b])
                for d in range(2):
                    if 'C' not in PARTS:
                        continue
                    kb.op("act", lambda e: e.copy(out=RV[d][cbr][0:2, lr[d], :, :], in_=psk_v[d][:, :, :]),
                          r=[pskb[d].b], w=[RV[d][cbr].b])
                for d in range(2):
                    for g in range(3):
                        if 'U' not in PARTS:
                            continue
                        kb.op("pe", lambda e: e.matmul(pd_v[d][:, g, :], lhsT=L4[d][cbr][:, lr[d], g, :],
                                                       rhs=RV[d][cbr][:, lr[d], g, :], start=True, stop=True),
                              r=[L4[d][cbr].b, RV[d][cbr].b], w=[pdb[d].b])
                for d in range(2):
                    for g in range(3):
                        if 'D' not in PARTS:
                            continue
                        kb.op("dve", lambda e: e.scalar_tensor_tensor(out=Stb[:, d, g, :], in0=St[:, d, g, :],
                                                                      scalar=Wt[d][cbf][:, g, lf[d]:lf[d] + 1],
                                                                      in1=pd_v[d][:, g, :], op0=ALU.mult, op1=ALU.add),
                              r=[Wt[d][cbf].b, pdb[d].b, St.sub[d]], w=[Stb.sub[d]])
                for d in range(2):
                    for g in range(3):
                        if 'D' not in PARTS:
                            continue
                        kb.op("dve", lambda e: e.scalar_tensor_tensor(out=St[:, d, g, :], in0=St[:, d, g, :],
                                                                      scalar=Wt[d][cbf][:, g, lf[d]:lf[d] + 1],
                                                                      in1=pd_v[d][:, g, :], op0=ALU.mult, op1=ALU.add),
                              r=[Wt[d][cbf].b, pdb[d].b], w=[St.sub[d]])
                for d in range(2):
                    for g in range(3):
                        if 'Y' not in PARTS:
                            continue
                        kb.op("pe", lambda e: e.matmul(py_v[d][:, g, :, lf[d]], lhsT=Stb[:, d, g, :],
                                                       rhs=RR[d][cbf][:, g, :, lf[d]], start=True, stop=True),
                              r=[RR[d][cbf].b, Stb.sub[d]], w=[pyb[d].b])
                if (jf == TCF - 1 or s == total - 1) and 'O' in PARTS:
                    s0 = cf * TCF
                    for d in range(2):
                        t0 = chunk_t0(s0, TCF, d)
                        ts = slice(t0, t0 + TCF)
                        kb.op("act", lambda e: e.copy(out=YF[d][:], in_=py_v[d]), r=[pyb[d].b], w=[YF[d].b])
                        kb.dma("sp", S['y_fm'].t[d].rearrange("g h v t -> v g h t")[:, :, :, ts].rearrange("v g h t -> v (g h) t"),
                               YF[d][:].rearrange("v g h t -> v (g h) t"), r=[YF[d].b], w=[S['y_fm'].b])
        barrier(kb)


    def phase_rw_scan2(self, li, nsteps=None):
        kb, nc = self.kb, self.nc
        S = self.scr
        TCF, TCR = 64, 8
        AHEAD = 2
        NSL = 4
        total = T if nsteps is None else nsteps
        pb = self._prep_buf
        with ExitStack() as es:
            identf = self.make_ident(es, F32, "identf_s")
            Stb = sb(es, kb, "Stb2", [128, 2, 3, 64], BF16, nsub=2)
            kb.op("dve", lambda e: e.memset(Stb[:], 0.0), w=[Stb.sub[0], Stb.sub[1]])
            Wt = [[sb(es, kb, f"Wt{d}{c}", [128, 3, TCF], F32) for c in range(2)] for d in range(2)]
            RR = [[sb(es, kb, f"RR{d}{c}", [128, 3, 2, TCF], BF16) for c in range(2)] for d in range(2)]
            KRW = [[sb(es, kb, f"KRW{d}{c}", [6, TCR, 128], BF16) for c in range(3)] for d in range(2)]
            LA = [[sb(es, kb, f"LA{d}{c}", [6, TCR, 384], BF16) for c in range(3)] for d in range(2)]
            LK = [[sb(es, kb, f"LK{d}{c}", [6, TCR, 128], BF16) for c in range(3)] for d in range(2)]
            VR = [[sb(es, kb, f"VR{d}{c}", [6, TCR, 192], BF16) for c in range(3)] for d in range(2)]
            YF = [sb(es, kb, f"YF{d}", [64, 3, 2, TCF], F32) for d in range(2)]
            Ab = sb(es, kb, "Abuf", [128, NSL * 6, 128], BF16, nsub=NSL * 2)
            for d in range(2):
                for c in range(2):
                    kb.op("pool", lambda e: e.memset(RR[d][c][:], 0.0), w=[RR[d][c].b])
            pA = [[ps(es, kb, f"pA{d}{i}", [128, 512], F32) for i in range(2)] for d in range(2)]
            pSb = [ps(es, kb, f"pS{d}", [128, 512], F32) for d in range(2)]
            pyb = [ps(es, kb, f"py{d}", [128, 512], F32) for d in range(2)]
            pA_v = [[pA[d][i][:, 0:384].rearrange("p (g c) -> p g c", g=3) for i in range(2)] for d in range(2)]
            pS_v = [pSb[d][:, 0:192].rearrange("p (g c) -> p g c", g=3) for d in range(2)]
            py_v = [pyb[d][0:64, 0:6 * TCF].rearrange("p (g h t) -> p g h t", g=3, h=2) for d in range(2)]
            r_pgt = S['rb_fm'].t.rearrange("g p t -> p g t")

            def seg_time(s, d):
                if s < CTX:
                    return SEQ + s if d == 0 else T - 1 - s
                return s - CTX if d == 0 else SEQ - 1 - (s - CTX)

            def chunk_t0(s0, n, d):
                return seg_time(s0, d) if d == 0 else seg_time(s0 + n - 1, d)

            def load_f(s):
                cbf = (s // TCF) % 2
                for d in range(2):
                    t0 = chunk_t0(s, TCF, d)
                    ts = slice(t0, t0 + TCF)
                    kb.dma("sp", Wt[d][cbf][:], S['w_fm'].t[d].rearrange("g p t -> p g t")[:, :, ts], r=[pb], w=[Wt[d][cbf].b])
                    for h in range(2):
                        hp = slice(h * 64, (h + 1) * 64)
                        kb.dma("sp", RR[d][cbf][hp, :, h, :], r_pgt[hp, :, ts], r=[pb], w=[RR[d][cbf].b])

            def load_r(s):
                cbr = (s // TCR) % 3
                for d in range(2):
                    t0 = chunk_t0(s, TCR, d)
                    ts = slice(t0, t0 + TCR)
                    kb.dma("sp", KRW[d][cbr][:], S['kk6_tm'].t[:, ts, :], r=[pb], w=[KRW[d][cbr].b])
                    kb.dma("sp", LA[d][cbr][:], S['akk6_tm'].t[d, :, ts, :], r=[pb], w=[LA[d][cbr].b])
                    kb.dma("sp", LK[d][cbr][:], S['krep6_tm'].t[d, :, ts, :], r=[pb], w=[LK[d][cbr].b])
                    kb.dma("sp", VR[d][cbr][:], S['v6_tm'].t[:, ts, :], r=[pb], w=[VR[d][cbr].b])

            def idx(s):
                jf = s % TCF
                jr = s % TCR
                return (s // TCF) % 2, (s // TCR) % 3, [jf, TCF - 1 - jf], [jr, TCR - 1 - jr]

            def build_A(s):
                if s >= total:
                    return
                if s % TCF == 0:
                    load_f(s)
                if s % TCR == 0:
                    load_r(s)
                cbf, cbr, lf, lr = idx(s)
                sl = s % NSL
                for d in range(2):
                    pa = pA[d][s % 2]
                    kb.op("pe", lambda e: e.matmul(pa[:, 0:384], lhsT=KRW[d][cbr][:, lr[d], :],
                                                   rhs=LA[d][cbr][:, lr[d], :], start=True, stop=True),
                          r=[KRW[d][cbr].b, LA[d][cbr].b], w=[pa.b])
                    for g in range(3):
                        kb.op("dve", lambda e: e.scalar_tensor_tensor(out=Ab[:, sl * 6 + d * 3 + g, :], in0=identf[:],
                                                                      scalar=Wt[d][cbf][:, g, lf[d]:lf[d] + 1],
                                                                      in1=pA_v[d][s % 2][:, g, :], op0=ALU.mult, op1=ALU.add),
                              r=[identf.b, Wt[d][cbf].b, pa.b], w=[Ab.sub[sl * 2 + d]])

            for s0 in range(AHEAD):
                build_A(s0)
            for s in range(total):
                build_A(s + AHEAD)
                cbf, cbr, lf, lr = idx(s)
                sl = s % NSL
                for d in range(2):
                    kb.op("pe", lambda e: e.matmul(pSb[d][:, 0:192], lhsT=LK[d][cbr][:, lr[d], :], rhs=VR[d][cbr][:, lr[d], :],
                                                   start=True, stop=False),
                          r=[LK[d][cbr].b, VR[d][cbr].b], w=[pSb[d].b])
                    for g in range(3):
                        kb.op("pe", lambda e: e.matmul(pS_v[d][:, g, :], lhsT=Ab[:, sl * 6 + d * 3 + g, :], rhs=Stb[:, d, g, :],
                                                       start=False, stop=(g == 2)),
                              r=[Ab.sub[sl * 2 + d], Stb.sub[d]], w=[pSb[d].b])
                for d in range(2):
                    kb.op("act", lambda e: e.copy(out=Stb[:, d, :, :], in_=pS_v[d]), r=[pSb[d].b], w=[Stb.sub[d]])
                for d in range(2):
                    for g in range(3):
                        kb.op("pe", lambda e: e.matmul(py_v[d][:, g, :, lf[d]], lhsT=Stb[:, d, g, :],
                                                       rhs=RR[d][cbf][:, g, :, lf[d]], start=True, stop=True),
                              r=[RR[d][cbf].b, Stb.sub[d]], w=[pyb[d].b])
                if (s % TCF) == TCF - 1 or s == total - 1:
                    s0 = (s // TCF) * TCF
                    for d in range(2):
                        t0 = chunk_t0(s0, TCF, d)
                        ts = slice(t0, t0 + TCF)
                        kb.op("act", lambda e: e.copy(out=YF[d][:], in_=py_v[d]), r=[pyb[d].b], w=[YF[d].b])
                        kb.dma("sp", S['y_fm'].t[d].rearrange("g h v t -> v g h t")[:, :, :, ts].rearrange("v g h t -> v (g h) t"),
                               YF[d][:].rearrange("v g h t -> v (g h) t"), r=[YF[d].b], w=[S['y_fm'].b])
        barrier(kb)

    def phase_rw_readout(self, li, need_ctx):
        kb, nc = self.kb, self.nc
        S = self.scr
        pb = self._prep_buf
        ntile = NT if need_ctx else SEQ // 128
        with ExitStack() as es:
            identf = self.make_ident(es, F32, "identf2")
            gng = sb(es, kb, "gng", [128, 384], F32)
            gnb = sb(es, kb, "gnb", [128, 384], F32)
            rkb = sb(es, kb, "rkb", [128, 384], F32)
            kb.dma("sp", gng[:], self.din['rw_gn_g'].t[li:li + 1, :].partition_broadcast(128), w=[gng.b])
            kb.dma("sp", gnb[:], self.din['rw_gn_b'].t[li:li + 1, :].partition_broadcast(128), w=[gnb.b])
            kb.dma("sp", rkb[:], self.din['rw_r_k'].t[li:li + 1, :].partition_broadcast(128), w=[rkb.b])
            yfa = [sb(es, kb, f"yfa{i}", [64, 6, 128], F32) for i in range(2)]
            yfb = [sb(es, kb, f"yfb{i}", [64, 6, 128], F32) for i in range(2)]
            rt = [sb(es, kb, f"rt{i}", [128, 384], F32) for i in range(2)]
            kt = [sb(es, kb, f"kt{i}", [128, 384], F32) for i in range(2)]
            vt = [sb(es, kb, f"vt{i}", [128, 384], F32) for i in range(2)]
            gt = [sb(es, kb, f"gt{i}", [128, 384], F32) for i in range(2)]
            yc = sb(es, kb, "yc", [128, 6, 64], F32)
            sq = sb(es, kb, "sqr", [128, 6, 64], F32)
            mu = sb(es, kb, "mu", [128, 6], F32)
            var = sb(es, kb, "var", [128, 6], F32)
            bon = sb(es, kb, "bon", [128, 6], F32)
            ot = [sb(es, kb, f"ot{i}", [128, 384], F32) for i in range(2)]
            pyt = [ps(es, kb, f"pyt{i}", [128, 512], F32) for i in range(2)]
            mix = S['mix_tm']
            for tt in range(ntile):
                b = tt % 2
                ts = slice(tt * 128, (tt + 1) * 128)
                kb.dma("sp", yfa[b][:], S['y_fm'].t[0].rearrange("g h v t -> v (g h) t")[:, :, ts], r=[S['y_fm'].b], w=[yfa[b].b])
                kb.dma("sp", yfb[b][:], S['y_fm'].t[1].rearrange("g h v t -> v (g h) t")[:, :, ts], r=[S['y_fm'].b], w=[yfb[b].b])
                kb.dma("sp", rt[b][:], S['r_tm'].t[ts, :], r=[pb], w=[rt[b].b])
                kb.dma("sp", kt[b][:], S['k_tm'].t[ts, :], r=[pb], w=[kt[b].b])
                kb.dma("sp", vt[b][:], S['v_tm'].t[ts, :], r=[pb], w=[vt[b].b])
                kb.dma("sp", gt[b][:], S['g_tm'].t[ts, :], r=[pb], w=[gt[b].b])
                kb.op("pool", lambda e: e.tensor_tensor(out=yfa[b][:], in0=yfa[b][:], in1=yfb[b][:], op=ALU.add),
                      r=[yfb[b].b], w=[yfa[b].b])
                pv = pyt[b][:, 0:384].rearrange("p (h c) -> p h c", h=6)
                for h in range(6):
                    kb.op("pe", lambda e: e.transpose(pv[:, h, :], yfa[b][:, h, :], identf[0:64, 0:64]),
                          r=[yfa[b].b, identf.b], w=[pyt[b].b])
                kb.op("dve", lambda e: e.tensor_reduce(out=mu[:], in_=pv, axis=AX.X, op=ALU.add), r=[pyt[b].b], w=[mu.b])
                kb.op("dve", lambda e: e.tensor_scalar(out=mu[:], in0=mu[:], scalar1=-1.0 / 64, scalar2=None, op0=ALU.mult),
                      w=[mu.b])
                kb.op("dve", lambda e: e.tensor_tensor(out=yc[:], in0=pv, in1=mu[:].unsqueeze(2).to_broadcast([128, 6, 64]),
                                                       op=ALU.add), r=[pyt[b].b, mu.b], w=[yc.b])
                kb.op("pool", lambda e: e.tensor_tensor(out=sq[:], in0=yc[:], in1=yc[:], op=ALU.mult), r=[yc.b], w=[sq.b])
                kb.op("dve", lambda e: e.tensor_reduce(out=var[:], in_=sq[:], axis=AX.X, op=ALU.add), r=[sq.b], w=[var.b])
                kb.op("dve", lambda e: e.tensor_scalar(out=var[:], in0=var[:], scalar1=1.0 / 64, scalar2=64e-5,
                                                       op0=ALU.mult, op1=ALU.add), w=[var.b])
                kb.op("act", lambda e: e.activation(out=var[:], in_=var[:], func=AF.Sqrt), w=[var.b])
                kb.op("dve", lambda e: e.reciprocal(out=var[:], in_=var[:]), w=[var.b])
                kb.op("dve", lambda e: e.tensor_tensor(out=yc[:], in0=yc[:], in1=var[:].unsqueeze(2).to_broadcast([128, 6, 64]),
                                                       op=ALU.mult), r=[var.b], w=[yc.b])
                ycf = yc[:].rearrange("p h c -> p (h c)")
                kb.op("dve", lambda e: e.tensor_tensor(out=ycf, in0=ycf, in1=gng[:], op=ALU.mult), r=[gng.b], w=[yc.b])
                kb.op("dve", lambda e: e.tensor_tensor(out=ycf, in0=ycf, in1=gnb[:], op=ALU.add), r=[gnb.b], w=[yc.b])
                kb.op("pool", lambda e: e.tensor_tensor(out=rt[b][:], in0=rt[b][:], in1=kt[b][:], op=ALU.mult),
                      r=[kt[b].b], w=[rt[b].b])
                kb.op("pool", lambda e: e.tensor_tensor(out=rt[b][:], in0=rt[b][:], in1=rkb[:], op=ALU.mult),
                      r=[rkb.b], w=[rt[b].b])
                kb.op("dve", lambda e: e.tensor_reduce(out=bon[:], in_=rt[b][:].rearrange("p (h c) -> p h c", h=6),
                                                       axis=AX.X, op=ALU.add), r=[rt[b].b], w=[bon.b])
                v3 = vt[b][:].rearrange("p (h c) -> p h c", h=6)
                kb.op("dve", lambda e: e.tensor_tensor(out=v3, in0=v3, in1=bon[:].unsqueeze(2).to_broadcast([128, 6, 64]),
                                                       op=ALU.mult), r=[bon.b], w=[vt[b].b])
                kb.op("dve", lambda e: e.tensor_tensor(out=ot[b][:], in0=ycf, in1=vt[b][:], op=ALU.add),
                      r=[yc.b, vt[b].b], w=[ot[b].b])
                kb.op("dve", lambda e: e.tensor_tensor(out=ot[b][:], in0=ot[b][:], in1=gt[b][:], op=ALU.mult),
                      r=[gt[b].b], w=[ot[b].b])
                kb.dma("sp", mix.t[ts, 0:384], ot[b][:], r=[ot[b].b], w=[mix.b])
        barrier(kb)

    def phase_outproj(self, li, first, need_ctx, final_out=None):
        kb, nc = self.kb, self.nc
        S = self.scr
        mix = S['mix_tm']
        modrow = S['modrow']
        ntile = NT if need_ctx else SEQ // 128
        with ExitStack() as es:
            identb = self.make_ident(es, BF16, "identb2")
            Wo = sb(es, kb, "Wo", [128, 8, D], BF16)
            wst = [sb(es, kb, f"wost{i}", [128, D], F32) for i in range(2)]
            for k in range(8):
                kb.dma("sp", wst[k % 2][:], self.din['w_out'].t[li, k * 128:(k + 1) * 128, :], w=[wst[k % 2].b])
                kb.op("pool", lambda e: e.tensor_copy(out=Wo[:, k, :], in_=wst[k % 2][:]), r=[wst[k % 2].b], w=[Wo.b])
            gm = [sb(es, kb, f"gm{w_}", [128, D], F32) for w_ in range(2)]
            for w_ in range(2):
                kb.dma("sp", gm[w_][:], modrow.t[w_:w_ + 1, 2 * D:3 * D].partition_broadcast(128), r=[modrow.b], w=[gm[w_].b])
            mt = [sb(es, kb, f"mt{i}", [128, D], F32) for i in range(2)]
            mtb = [sb(es, kb, f"mtb{i}", [128, D], BF16) for i in range(2)]
            mT = [sb(es, kb, f"mT{i}", [128, 8, 128], BF16) for i in range(2)]
            xt = [sb(es, kb, f"xo{i}", [128, D], F32) for i in range(2)]
            tmp = [sb(es, kb, f"xtmp{i}", [128, D], F32) for i in range(2)]
            tp = [ps(es, kb, f"otp{i}", [128, 8, 128], BF16) for i in range(2)]
            po = [ps(es, kb, f"po{i}", [128, 512], F32) for i in range(4)]
            for tt in range(ntile):
                b = tt % 2
                ts = slice(tt * 128, (tt + 1) * 128)
                which = 0 if tt < 32 else 1
                if first:
                    xsrc = self.din['x'] if tt < 32 else self.din['ctx']
                    xs_ap = xsrc.t[ts, :] if tt < 32 else xsrc.t[(tt - 32) * 128:(tt - 31) * 128, :]
                else:
                    xsrc = S['xcur']
                    xs_ap = xsrc.t[ts, :]
                kb.dma("sp", mt[b][:], mix.t[ts, :], r=[mix.b], w=[mt[b].b])
                kb.dma("sp", xt[b][:], xs_ap, r=[xsrc.b], w=[xt[b].b])
                kb.op("pool", lambda e: e.tensor_copy(out=mtb[b][:], in_=mt[b][:]), r=[mt[b].b], w=[mtb[b].b])
                for k in range(8):
                    kb.op("pe", lambda e: e.transpose(tp[b][:, k, :], mtb[b][:, k * 128:(k + 1) * 128], identb[:]),
                          r=[mtb[b].b, identb.b], w=[tp[b].b])
                kb.op("act", lambda e: e.copy(out=mT[b][:], in_=tp[b][:]), r=[tp[b].b], w=[mT[b].b])
                for n in range(2):
                    p_ = po[(2 * tt + n) % 4]
                    for k in range(8):
                        kb.op("pe", lambda e: e.matmul(p_[:, :], lhsT=mT[b][:, k, :], rhs=Wo[:, k, n * 512:(n + 1) * 512],
                                                       start=(k == 0), stop=(k == 7)), r=[mT[b].b, Wo.b], w=[p_.b])
                    kb.op("dve", lambda e: e.tensor_tensor(out=tmp[b][:, n * 512:(n + 1) * 512], in0=p_[:, :],
                                                           in1=gm[which][:, n * 512:(n + 1) * 512], op=ALU.mult),
                          r=[p_.b, gm[which].b], w=[tmp[b].b])
                kb.op("pool", lambda e: e.tensor_tensor(out=xt[b][:], in0=xt[b][:], in1=tmp[b][:], op=ALU.add),
                      r=[tmp[b].b], w=[xt[b].b])
                if final_out is not None and tt < 32:
                    kb.dma("sp", final_out.t[ts, :], xt[b][:], r=[xt[b].b], w=[final_out.b])
                else:
                    kb.dma("sp", S['xcur'].t[ts, :], xt[b][:], r=[xt[b].b], w=[S['xcur'].b])
        barrier(kb)

    def phase_mla_prep(self, li):
        kb, nc = self.kb, self.nc
        S = self.scr
        zT = S['zT']
        inv96 = 1.0 / math.sqrt(96.0)
        with ExitStack() as es:
            identb = self.make_ident(es, BF16, "identb3")
            ones = sb(es, kb, "ones_c", [128, 1], F32)
            kb.op("pool", lambda e: e.memset(ones[:], 1.0), w=[ones.b])
            wq_st = sb(es, kb, "wq_st", [128, 2, 576], F32)
            wkv_st = sb(es, kb, "wkv_st", [128, 768], F32)
            qn = sb(es, kb, "qn_c", [128, 2], F32)
            kvn = sb(es, kb, "kvn_c", [128, 1], F32)
            Wq = sb(es, kb, "Wq", [128, 2, 576], BF16)
            Wkv = sb(es, kb, "Wkv", [128, 768], BF16)
            kb.dma("sp", wq_st[:], self.din['mla_w_uq'].t[li].rearrange("(k p) n -> p k n", p=128), w=[wq_st.b])
            kb.dma("sp", wkv_st[:], self.din['mla_w_ukv'].t[li], w=[wkv_st.b])
            kb.dma("sp", qn[:], self.din['mla_q_norm'].t[li].rearrange("(k p) -> p k", p=128), w=[qn.b],
                   allow_slow_non_contiguous=True)
            kb.dma("sp", kvn[:], self.din['mla_kv_norm'].t[li].rearrange("(k p) -> p k", p=128), w=[kvn.b],
                   allow_slow_non_contiguous=True)
            for k in range(2):
                kb.op("dve", lambda e: e.tensor_scalar(out=Wq[:, k, :], in0=wq_st[:, k, :], scalar1=qn[:, k:k + 1],
                                                       scalar2=None, op0=ALU.mult), r=[wq_st.b, qn.b], w=[Wq.b])
            kb.op("dve", lambda e: e.tensor_scalar(out=Wkv[:], in0=wkv_st[:], scalar1=kvn[:, 0:1], scalar2=None,
                                                   op0=ALU.mult), r=[wkv_st.b, kvn.b], w=[Wkv.b])
            qg = sb(es, kb, "qg_bc", [128, 96], F32)
            kg_ = sb(es, kb, "kg_bc", [128, 96], F32)
            kb.dma("sp", qg[:], self.din['mla_q_gain'].t[li:li + 1, :].partition_broadcast(128), w=[qg.b])
            kb.dma("sp", kg_[:], self.din['mla_k_gain'].t[li:li + 1, :].partition_broadcast(128), w=[kg_.b])
            kb.op("dve", lambda e: e.tensor_scalar(out=qg[:], in0=qg[:], scalar1=inv96, scalar2=None, op0=ALU.mult),
                  w=[qg.b])
            cq = sb(es, kb, "cq", [128, 2, T], F32)
            ckv = sb(es, kb, "ckv", [128, T], F32)
            cqb = sb(es, kb, "cqb", [128, 2, T], BF16)
            ckvb = sb(es, kb, "ckvb", [128, T], BF16)
            for k in range(2):
                kb.dma("sp", cq[:, k, :], zT.t[11 + k], r=[zT.b], w=[cq.b])
            kb.dma("sp", ckv[:], zT.t[13], r=[zT.b], w=[ckv.b])
            kb.op("pool", lambda e: e.tensor_copy(out=cqb[:], in_=cq[:]), r=[cq.b], w=[cqb.b])
            kb.op("pool", lambda e: e.tensor_copy(out=ckvb[:], in_=ckv[:]), r=[ckv.b], w=[ckvb.b])
            kb.op("act", lambda e: e.activation(out=cq[:], in_=cq[:], func=AF.Square), w=[cq.b])
            kb.op("act", lambda e: e.activation(out=ckv[:], in_=ckv[:], func=AF.Square), w=[ckv.b])
            pq = [ps(es, kb, f"mq{i}", [128, 512], F32) for i in range(2)]
            pkv = [ps(es, kb, f"mkv{i}", [128, 512], F32) for i in range(2)]
            pss = ps(es, kb, "mss", [128, 512], F32)
            ptr = [ps(es, kb, f"mtr{i}", [128, 1024], BF16) for i in range(2)]
            rs = sb(es, kb, "m_rs", [128, 2], F32)
            q = sb(es, kb, "m_q", [128, 6, 96], F32)
            kk_ = sb(es, kb, "m_k", [128, 6, 96], F32)
            kvt = sb(es, kb, "m_kv", [128, 6, 128], F32)
            sq = sb(es, kb, "m_sq", [128, 6, 96], F32)
            hs = sb(es, kb, "m_hs", [128, 6], F32)
            krt = sb(es, kb, "m_kr", [128, 32], F32)
            cs = sb(es, kb, "m_cs", [128, 32], F32)
            r1 = sb(es, kb, "m_r1", [128, 6, 2, 8], F32)
            r2 = sb(es, kb, "m_r2", [128, 6, 2, 8], F32)
            r3 = sb(es, kb, "m_r3", [128, 6, 2, 8], F32)
            qb = sb(es, kb, "m_qb", [128, 6, 96], BF16)
            kbb = sb(es, kb, "m_kb", [128, 6, 96], BF16)
            v1 = [sb(es, kb, f"m_v1{i}", [128, 6, 65], BF16) for i in range(2)]
            qTs = [sb(es, kb, f"m_qT{i}", [96, 6, 128], BF16) for i in range(2)]
            kTs = [sb(es, kb, f"m_kT{i}", [96, 6, 128], BF16) for i in range(2)]
            for i in range(2):
                kb.op("pool", lambda e: e.memset(v1[i][:], 1.0), w=[v1[i].b])

            def head_norm(x, gain):
                kb.op("pool", lambda e: e.tensor_tensor(out=sq[:], in0=x[:], in1=x[:], op=ALU.mult), r=[x.b], w=[sq.b])
                kb.op("dve", lambda e: e.tensor_reduce(out=hs[:], in_=sq[:], axis=AX.X, op=ALU.add), r=[sq.b], w=[hs.b])
                kb.op("dve", lambda e: e.tensor_scalar(out=hs[:], in0=hs[:], scalar1=1.0 / 96, scalar2=NORM_EPS,
                                                       op0=ALU.mult, op1=ALU.add), w=[hs.b])
                kb.op("act", lambda e: e.activation(out=hs[:], in_=hs[:], func=AF.Sqrt), w=[hs.b])
                kb.op("dve", lambda e: e.reciprocal(out=hs[:], in_=hs[:]), w=[hs.b])
                kb.op("dve", lambda e: e.tensor_tensor(out=x[:], in0=x[:], in1=hs[:].unsqueeze(2).to_broadcast([128, 6, 96]),
                                                       op=ALU.mult), r=[hs.b], w=[x.b])
                kb.op("dve", lambda e: e.tensor_tensor(out=x[:], in0=x[:], in1=gain[:].unsqueeze(1).to_broadcast([128, 6, 96]),
                                                       op=ALU.mult), r=[gain.b], w=[x.b])

            def rope(x):
                xr = x[:, :, 64:96].rearrange("p h (a f e) -> p h a f e", a=2, f=2)
                x1 = xr[:, :, :, 0, :]
                x2 = xr[:, :, :, 1, :]
                c_ = cs[:, 0:16].rearrange("p (a e) -> p a e", a=2).unsqueeze(1).to_broadcast([128, 6, 2, 8])
                s_ = cs[:, 16:32].rearrange("p (a e) -> p a e", a=2).unsqueeze(1).to_broadcast([128, 6, 2, 8])
                kb.op("dve", lambda e: e.tensor_tensor(out=r1[:], in0=x1, in1=c_, op=ALU.mult), r=[x.b, cs.b], w=[r1.b])
                kb.op("dve", lambda e: e.tensor_tensor(out=r2[:], in0=x2, in1=s_, op=ALU.mult), r=[x.b, cs.b], w=[r2.b])
                kb.op("dve", lambda e: e.tensor_tensor(out=r1[:], in0=r1[:], in1=r2[:], op=ALU.subtract), r=[r2.b], w=[r1.b])
                kb.op("dve", lambda e: e.tensor_tensor(out=r2[:], in0=x2, in1=c_, op=ALU.mult), r=[x.b, cs.b], w=[r2.b])
                kb.op("dve", lambda e: e.tensor_tensor(out=r3[:], in0=x1, in1=s_, op=ALU.mult), r=[x.b, cs.b], w=[r3.b])
                kb.op("dve", lambda e: e.tensor_tensor(out=x2, in0=r2[:], in1=r3[:], op=ALU.add), r=[r2.b, r3.b], w=[x.b])
                kb.op("dve", lambda e: e.tensor_copy(out=x1, in_=r1[:]), r=[r1.b], w=[x.b])

            for tt in range(NT):
                b = tt % 2
                ts = slice(tt * 128, (tt + 1) * 128)
                kb.dma("sp", krt[:], S['krope_tm'].t[ts, :], r=[S['krope_tm'].b], w=[krt.b])
                if tt < 32:
                    kb.dma("sp", cs[:], self.consts['rope_cs'].t[ts, :], w=[cs.b])
                for k in range(2):
                    kb.op("pe", lambda e: e.matmul(pss[:, 0:1], lhsT=cq[:, k, ts], rhs=ones[:, 0:1], start=(k == 0),
                                                   stop=(k == 1)), r=[cq.b, ones.b], w=[pss.b])
                kb.op("pe", lambda e: e.matmul(pss[:, 1:2], lhsT=ckv[:, ts], rhs=ones[:, 0:1], start=True, stop=True),
                      r=[ckv.b, ones.b], w=[pss.b])
                kb.op("dve", lambda e: e.tensor_scalar(out=rs[:, 0:1], in0=pss[:, 0:1], scalar1=1.0 / 256, scalar2=NORM_EPS,
                                                       op0=ALU.mult, op1=ALU.add), r=[pss.b], w=[rs.b])
                kb.op("dve", lambda e: e.tensor_scalar(out=rs[:, 1:2], in0=pss[:, 1:2], scalar1=1.0 / 128, scalar2=NORM_EPS,
                                                       op0=ALU.mult, op1=ALU.add), r=[pss.b], w=[rs.b])
                kb.op("act", lambda e: e.activation(out=rs[:], in_=rs[:], func=AF.Sqrt), w=[rs.b])
                kb.op("dve", lambda e: e.reciprocal(out=rs[:], in_=rs[:]), w=[rs.b])
                for n in range(2):
                    for k in range(2):
                        kb.op("pe", lambda e: e.matmul(pq[n][:, 0:288], lhsT=cqb[:, k, ts], rhs=Wq[:, k, n * 288:(n + 1) * 288],
                                                       start=(k == 0), stop=(k == 1)), r=[cqb.b, Wq.b], w=[pq[n].b])
                    kb.op("act", lambda e: e.activation(out=q[:, 3 * n:3 * n + 3, :].rearrange("p h c -> p (h c)"),
                                                        in_=pq[n][:, 0:288], func=AF.Copy, scale=rs[:, 0:1]),
                          r=[pq[n].b, rs.b], w=[q.b])
                for n in range(2):
                    kb.op("pe", lambda e: e.matmul(pkv[n][:, 0:384], lhsT=ckvb[:, ts], rhs=Wkv[:, n * 384:(n + 1) * 384],
                                                   start=True, stop=True), r=[ckvb.b, Wkv.b], w=[pkv[n].b])
                    kb.op("act", lambda e: e.activation(out=kvt[:, 3 * n:3 * n + 3, :].rearrange("p h c -> p (h c)"),
                                                        in_=pkv[n][:, 0:384], func=AF.Copy, scale=rs[:, 1:2]),
                          r=[pkv[n].b, rs.b], w=[kvt.b])
                kb.op("pool", lambda e: e.tensor_copy(out=kk_[:, :, 0:64], in_=kvt[:, :, 0:64]), r=[kvt.b], w=[kk_.b])
                kb.op("pool", lambda e: e.tensor_copy(out=kk_[:, :, 64:96],
                                                      in_=krt[:].unsqueeze(1).to_broadcast([128, 6, 32])),
                      r=[krt.b], w=[kk_.b])
                kb.op("pool", lambda e: e.tensor_copy(out=v1[b][:, :, 0:64], in_=kvt[:, :, 64:128]), r=[kvt.b], w=[v1[b].b])
                head_norm(q, qg)
                head_norm(kk_, kg_)
                if tt < 32:
                    rope(q)
                    rope(kk_)
                kb.op("pool", lambda e: e.tensor_copy(out=qb[:], in_=q[:]), r=[q.b], w=[qb.b])
                kb.op("pool", lambda e: e.tensor_copy(out=kbb[:], in_=kk_[:]), r=[kk_.b], w=[kbb.b])
                pt = ptr[b]
                for h in range(6):
                    kb.op("pe", lambda e: e.transpose(pt[0:96, h * 128:(h + 1) * 128], qb[:, h, :], identb[:]),
                          r=[qb.b, identb.b], w=[pt.b])
                kb.op("act", lambda e: e.copy(out=qTs[b][:].rearrange("p h t -> p (h t)"), in_=pt[0:96, 0:768]),
                      r=[pt.b], w=[qTs[b].b])
                for h in range(6):
                    kb.op("pe", lambda e: e.transpose(pt[0:96, h * 128:(h + 1) * 128], kbb[:, h, :], identb[:]),
                          r=[kbb.b, identb.b], w=[pt.b])
                kb.op("dve", lambda e: e.tensor_copy(out=kTs[b][:].rearrange("p h t -> p (h t)"), in_=pt[0:96, 0:768]),
                      r=[pt.b], w=[kTs[b].b])
                kb.dma("sp", S['qT'].t[:, :, ts], qTs[b][:], r=[qTs[b].b], w=[S['qT'].b])
                kb.dma("sp", S['kT'].t[:, :, ts], kTs[b][:], r=[kTs[b].b], w=[S['kT'].b])
                kb.dma("sp", S['V1'].t[ts, :, :], v1[b][:], r=[v1[b].b], w=[S['V1'].b])
        barrier(kb)

    def phase_mla_attn(self, li, need_ctx):
        kb, nc = self.kb, self.nc
        S = self.scr
        mix = S['mix_tm']
        with ExitStack() as es:
            kT = sb(es, kb, "a_kT", [96, 6, T], BF16)
            V1 = sb(es, kb, "a_V1", [128, NT, 6 * 65], BF16)
            kb.dma("sp", kT[:], S['kT'].t, r=[S['kT'].b], w=[kT.b])
            kb.dma("sp", V1[:], S['V1'].t.rearrange("(n p) h c -> p n (h c)", p=128), r=[S['V1'].b], w=[V1.b])
            qT = [sb(es, kb, f"a_qT{i}", [96, 6, 128], BF16) for i in range(2)]
            Pt = [sb(es, kb, f"a_P{i}", [128, 4, 128], BF16) for i in range(3)]
            psc = [ps(es, kb, f"a_sc{i}", [128, 512], F32) for i in range(3)]
            pov = [ps(es, kb, f"a_po{i}", [128, 512], F32) for i in range(2)]
            ot = [sb(es, kb, f"a_ot{i}", [128, 6, 64], F32) for i in range(2)]
            rcp = sb(es, kb, "a_rcp", [128, 1], F32)
            nsc = 0
            npo = 0
            qtiles = list(range(NT if need_ctx else 32))
            for qi, qt in enumerate(qtiles):
                b = qi % 2
                qs = slice(qt * 128, (qt + 1) * 128)
                kb.dma("sp", qT[b][:], S['qT'].t[:, :, qs], r=[S['qT'].b], w=[qT[b].b])
                ktiles = list(range(NT)) if qt < 32 else [32, 33]
                for h in range(6):
                    po = pov[npo % 2]
                    npo += 1
                    for g0 in range(0, len(ktiles), 4):
                        grp = ktiles[g0:g0 + 4]
                        sc = psc[nsc % 3]
                        P_ = Pt[nsc % 3]
                        nsc += 1
                        for j, kt_ in enumerate(grp):
                            kb.op("pe", lambda e: e.matmul(sc[:, j * 128:(j + 1) * 128], lhsT=kT[:, h, kt_ * 128:(kt_ + 1) * 128],
                                                           rhs=qT[b][:, h, :], start=True, stop=True),
                                  r=[kT.b, qT[b].b], w=[sc.b])
                        n = len(grp)
                        kb.op("act", lambda e: e.activation(out=P_[:, 0:n, :].rearrange("p j q -> p (j q)"),
                                                            in_=sc[:, 0:n * 128], func=AF.Exp), r=[sc.b], w=[P_.b])
                        for j, kt_ in enumerate(grp):
                            first = (g0 == 0 and j == 0)
                            last = (g0 + j == len(ktiles) - 1)
                            kb.op("pe", lambda e: e.matmul(po[:, 0:65], lhsT=P_[:, j, :], rhs=V1[:, kt_, h * 65:(h + 1) * 65],
                                                           start=first, stop=last), r=[P_.b, V1.b], w=[po.b])
                    kb.op("dve", lambda e: e.reciprocal(out=rcp[:], in_=po[:, 64:65]), r=[po.b], w=[rcp.b])
                    kb.op("dve", lambda e: e.tensor_scalar(out=ot[b][:, h, :], in0=po[:, 0:64], scalar1=rcp[:, 0:1],
                                                           scalar2=None, op0=ALU.mult), r=[po.b, rcp.b], w=[ot[b].b])
                kb.dma("sp", mix.t[qs, 384:768], ot[b][:].rearrange("p h c -> p (h c)"), r=[ot[b].b], w=[mix.b])
        barrier(kb)

    def phase_hy_filter(self, li, L, Hd, suffix):
        kb, nc = self.kb, self.nc
        TWO_PI = 2.0 * math.pi
        C = self.consts
        with ExitStack() as es:
            w1 = sb(es, kb, "hw1", [33, 64], F32)
            w2 = sb(es, kb, "hw2", [64, 64], F32)
            w3 = sb(es, kb, "hw3", [64, 1024], F32)
            b1 = sb(es, kb, "hb1", [64, 1], F32)
            f1 = sb(es, kb, "hf1", [64, 1], F32)
            b2 = sb(es, kb, "hb2", [64, 1], F32)
            f2 = sb(es, kb, "hf2", [64, 1], F32)
            b3 = sb(es, kb, "hb3", [128, 8], F32)
            nrate = sb(es, kb, "hnrate", [128, 2], F32)
            kb.dma("sp", w1[:], self.din['hy_w1'].t[li], w=[w1.b])
            kb.dma("sp", w2[:], self.din['hy_w2'].t[li], w=[w2.b])
            kb.dma("sp", w3[:], self.din['hy_w3'].t[li], w=[w3.b])
            for (tl, nm) in ((b1, 'hy_b1'), (f1, 'hy_freq1'), (b2, 'hy_b2'), (f2, 'hy_freq2')):
                kb.dma("sp", tl[:], self.din[nm].t[li].rearrange("(p o) -> p o", o=1), w=[tl.b])
            kb.dma("sp", b3[:], self.din['hy_b3'].t[li].rearrange("(k p) -> p k", p=128), w=[b3.b],
                   allow_slow_non_contiguous=True)
            kb.dma("sp", nrate[:], C['hy_nrate'].t.rearrange("(k p) -> p k", p=128), w=[nrate.b],
                   allow_slow_non_contiguous=True)
            zp = sb(es, kb, "hzp", [33, L], F32)
            tnb = sb(es, kb, "htn", [128, L], F32)
            h1 = sb(es, kb, "hh1", [64, L], F32)
            h2 = sb(es, kb, "hh2", [64, 2, L], F32)
            arg = sb(es, kb, "harg", [64, 512], F32)
            ki = sb(es, kb, "hki", [64, 512], I32)
            kf = sb(es, kb, "hkf", [64, 512], F32)
            msk = sb(es, kb, "hmsk", [64, 512], F32)
            dec = sb(es, kb, "hdec", [128, L], F32)
            H = sb(es, kb, "hH", [128, 8192], F32)
            Hb = sb(es, kb, "hHb", [128, 8192], BF16)
            junk = Hb
            ssq = sb(es, kb, "hssq", [128, 1], F32)
            pp = [ps(es, kb, f"hp{i}", [128, 512], F32) for i in range(3)]
            npp = [0]
            nb = max(1, L // 512)
            bw = min(512, L)

            def sin_layer(dstf, w, srcf, srcb, bcol, fcol, K):
                for d in [DCUR[0]]:
                    for blk in range(nb):
                        cs_ = slice(blk * bw, (blk + 1) * bw)
                        p_ = pp[npp[0] % 3]
                        npp[0] += 1
                        kb.op("pe", lambda e: e.matmul(p_[0:64, 0:bw], lhsT=w[0:K, :], rhs=srcf(cs_), start=True, stop=True),
                              r=[w.b, srcb], w=[p_.b])
                        a_ = arg[:, 0:bw]
                        kb.op("dve", lambda e: e.tensor_scalar(out=a_, in0=p_[0:64, 0:bw], scalar1=bcol[:, 0:1], scalar2=fcol[:, 0:1],
                                                               op0=ALU.add, op1=ALU.mult), r=[p_.b, bcol.b, fcol.b], w=[arg.b])
                        kb.op("dve", lambda e: e.tensor_scalar(out=ki[:, 0:bw], in0=a_, scalar1=1.0 / TWO_PI, scalar2=None,
                                                               op0=ALU.mult), r=[arg.b], w=[ki.b])
                        kb.op("dve", lambda e: e.tensor_copy(out=kf[:, 0:bw], in_=ki[:, 0:bw]), r=[ki.b], w=[kf.b])
                        kb.op("dve", lambda e: e.scalar_tensor_tensor(out=a_, in0=kf[:, 0:bw], scalar=-TWO_PI, in1=a_,
                                                                      op0=ALU.mult, op1=ALU.add), r=[kf.b], w=[arg.b])
                        kb.op("dve", lambda e: e.tensor_scalar(out=msk[:, 0:bw], in0=a_, scalar1=math.pi, scalar2=-TWO_PI,
                                                               op0=ALU.is_gt, op1=ALU.mult), r=[arg.b], w=[msk.b])
                        kb.op("dve", lambda e: e.tensor_tensor(out=a_, in0=a_, in1=msk[:, 0:bw], op=ALU.add), r=[msk.b], w=[arg.b])
                        kb.op("dve", lambda e: e.tensor_scalar(out=msk[:, 0:bw], in0=a_, scalar1=-math.pi, scalar2=TWO_PI,
                                                               op0=ALU.is_lt, op1=ALU.mult), r=[arg.b], w=[msk.b])
                        kb.op("dve", lambda e: e.tensor_tensor(out=a_, in0=a_, in1=msk[:, 0:bw], op=ALU.add), r=[msk.b], w=[arg.b])
                        kb.op("dve", lambda e: e.tensor_scalar(out=a_, in0=a_, scalar1=3.14159, scalar2=-3.14159,
                                                               op0=ALU.min, op1=ALU.max), w=[arg.b])
                        kb.op("act", lambda e: e.activation(out=dstf(cs_)[0], in_=a_, func=AF.Sin), r=[arg.b], w=[dstf(cs_)[1]])

            DCUR = [0]
            for d in range(2):
                DCUR[0] = d
                kb.dma("sp", zp[:], C['hy_z' + suffix].t[d], w=[zp.b])
                sin_layer(lambda cs_: (h1[:, cs_], h1.b), w1, lambda cs_: zp[0:33, cs_], zp.b, b1, f1, 33)
                sin_layer(lambda cs_: (h2[:, d, cs_], h2.b), w2, lambda cs_: h1[:, cs_], h1.b, b2, f2, 64)
            for o in range(2):
                for c in range(2):
                    kb.op("pool", lambda e: e.memset(H[:], 0.0), w=[H.b])
                    for d in range(2):
                        col0 = o * 512 + d * 256 + c * 128
                        kcol = col0 // 128
                        kb.dma("sp", tnb[:], C['hy_tn' + suffix].t[d:d + 1, :].partition_broadcast(128), w=[tnb.b])
                        kb.op("act", lambda e: e.activation(out=dec[:], in_=tnb[:], func=AF.Exp, scale=nrate[:, c:c + 1]),
                              r=[tnb.b, nrate.b], w=[dec.b])
                        for blk in range(nb):
                            cs_ = slice(blk * bw, (blk + 1) * bw)
                            p_ = pp[npp[0] % 3]
                            npp[0] += 1
                            kb.op("pe", lambda e: e.matmul(p_[:, 0:bw], lhsT=w3[:, col0:col0 + 128], rhs=h2[:, d, cs_],
                                                           start=True, stop=True), r=[w3.b, h2.b], w=[p_.b])
                            if d == 0:
                                n0 = 4096 + blk * bw
                                width = bw
                            else:
                                n0 = 4096 - L + 1 + blk * bw
                                width = bw if blk < nb - 1 else bw - 1
                            kb.op("dve", lambda e: e.scalar_tensor_tensor(out=H[:, n0:n0 + width], in0=p_[:, 0:width],
                                                                          scalar=b3[:, kcol:kcol + 1],
                                                                          in1=dec[:, blk * bw:blk * bw + width],
                                                                          op0=ALU.add, op1=ALU.mult),
                                  r=[p_.b, b3.b, dec.b], w=[H.b])
                    kb.op("act", lambda e: e.activation(out=junk[:], in_=H[:], func=AF.Square, accum_out=ssq[:]),
                          r=[H.b], w=[junk.b, ssq.b])
                    kb.op("act", lambda e: e.activation(out=ssq[:], in_=ssq[:], func=AF.Sqrt), w=[ssq.b])
                    kb.op("dve", lambda e: e.reciprocal(out=ssq[:], in_=ssq[:]), w=[ssq.b])
                    kb.op("dve", lambda e: e.tensor_scalar(out=Hb[:], in0=H[:], scalar1=ssq[:, 0:1], scalar2=None, op0=ALU.mult),
                          r=[H.b, ssq.b], w=[Hb.b])
                    kb.dma("sp", Hd.t[o, c * 128:(c + 1) * 128, :], Hb[:], r=[Hb.b], w=[Hd.b])
        barrier(kb)

    def phase_hyena(self, li, need_ctx):
        kb, nc = self.kb, self.nc
        S = self.scr
        zT = S['zT']
        mix = S['mix_tm']
        C = self.consts
        with ExitStack() as es:
            identf = self.make_ident(es, F32, "identf_h")
            src = [sb(es, kb, f"hsrc{i}", [128, T], F32) for i in range(2)]
            stg = [sb(es, kb, f"hstg{i}", [128, NT, 128], F32) for i in range(2)]
            tpp = [ps(es, kb, f"htq{i}", [128, 4, 128], F32) for i in range(2)]
            st = [0]
            for i in range(6):
                s_ = src[i % 2]
                kb.dma("sp", s_[:], zT.t[14 + i], r=[zT.b], w=[s_.b])
                self.fm_to_tm(s_, None, identf, stg[i % 2], tpp, NT, st)
                kb.dma("sp", S['hy_tm'].t[i // 2].rearrange("(n p) f -> p n f", p=128)[:, :, (i % 2) * 128:(i % 2 + 1) * 128],
                       stg[i % 2][:], r=[stg[i % 2].b], w=[S['hy_tm'].b])
        barrier(kb)
        segs = [(0, 32, S['Hd_l'], 31)]
        if need_ctx:
            segs.append((32, 2, S['Hd_c'], 1))
        with ExitStack() as es:
            J = sb(es, kb, "hJ", [128, 128], F32)
            kb.dma("sp", J[:], C['antiident'].t, w=[J.b])
            bias = sb(es, kb, "hbias", [128, 2, 256], F32)
            for o in range(2):
                kb.dma("sp", bias[:, o, :], self.din['hy_bias'].t[li, o:o + 1, :].partition_broadcast(128), w=[bias.b])
            U = sb(es, kb, "hU", [128, NT, 256], F32)
            G = sb(es, kb, "hG", [128, NT, 256], F32)
            Y = sb(es, kb, "hY", [128, NT, 256], F32)
            Uf = sb(es, kb, "hUf", [128, NT, 256], BF16)
            At = [sb(es, kb, f"hAt{i}", [128, 8064], BF16) for i in range(3)]
            pf = [ps(es, kb, f"hpf{i}", [128, 512], F32) for i in range(2)]
            pc = [ps(es, kb, f"hpc{i}", [128, 512], F32) for i in range(2)]
            nat = 0
            kb.dma("sp", U[:], S['hy_tm'].t[2].rearrange("(n p) f -> p n f", p=128), r=[S['hy_tm'].b], w=[U.b])
            for o in range(2):
                kb.dma("sp", G[:], S['hy_tm'].t[o].rearrange("(n p) f -> p n f", p=128), r=[S['hy_tm'].b], w=[G.b])
                ntl = NT if need_ctx else 32
                for tt in range(ntl):
                    p_ = pf[tt % 2]
                    kb.op("pe", lambda e: e.matmul(p_[:, 0:256], lhsT=J[:], rhs=U[:, tt, :], start=True, stop=True),
                          r=[J.b, U.b], w=[p_.b])
                    kb.op("act", lambda e: e.copy(out=Uf[:, tt, :], in_=p_[:, 0:256]), r=[p_.b], w=[Uf.b])
                for (tile0, nti, Hd, mmax) in segs:
                    ncol = (2 * mmax + 1) * 128
                    x0 = (31 - mmax) * 128
                    for cg in range(16):
                        pcb = pc[cg % 2]
                        for cc in range(16):
                            ch = cg * 16 + cc
                            A = At[nat % 3]
                            nat += 1
                            src_ap = bass.AP(Hd.t.tensor, Hd.t.offset + (o * 256 + ch) * 8192 + 1 + x0, [[1, 128], [1, ncol]])
                            kb.dma("sp", A[:, 0:ncol], src_ap, r=[Hd.b], w=[A.b])
                            outv = pcb[:, cc * 32:cc * 32 + nti]
                            order = [0] + [m for m in range(-mmax, mmax + 1) if m != 0]
                            for mi, m in enumerate(order):
                                i0 = max(0, m)
                                i1 = min(nti - 1, nti - 1 + m)
                                if i1 < i0:
                                    continue
                                n = i1 - i0 + 1
                                j0 = i0 - m
                                kb.op("pe", lambda e: e.matmul(pcb[:, cc * 32 + i0:cc * 32 + i0 + n],
                                                               lhsT=A[:, (m + mmax) * 128:(m + mmax + 1) * 128],
                                                               rhs=Uf[:, tile0 + j0:tile0 + j0 + n, ch],
                                                               start=(mi == 0), stop=(mi == len(order) - 1)),
                                      r=[A.b, Uf.b], w=[pcb.b])
                        kb.op("act", lambda e: e.copy(out=Y[:, tile0:tile0 + nti, cg * 16:(cg + 1) * 16].rearrange("p i c -> p c i"),
                                                      in_=pcb[:, :].rearrange("p (c i) -> p c i", c=16)[:, :, 0:nti]),
                              r=[pcb.b], w=[Y.b])
                bb = bias[:, o, :].unsqueeze(1).to_broadcast([128, ntl, 256])
                kb.op("pool", lambda e: e.tensor_tensor(out=U[:, 0:ntl, :], in0=U[:, 0:ntl, :], in1=bb, op=ALU.mult),
                      r=[bias.b], w=[U.b])
                kb.op("dve", lambda e: e.tensor_tensor(out=U[:, 0:ntl, :], in0=U[:, 0:ntl, :], in1=Y[:, 0:ntl, :], op=ALU.add),
                      r=[Y.b], w=[U.b])
                kb.op("dve", lambda e: e.tensor_tensor(out=U[:, 0:ntl, :], in0=U[:, 0:ntl, :], in1=G[:, 0:ntl, :], op=ALU.mult),
                      r=[G.b], w=[U.b])
            ntl = NT if need_ctx else 32
            kb.dma("sp", mix.t.rearrange("(n p) f -> p n f", p=128)[:, 0:ntl, 768:1024], U[:, 0:ntl, :], r=[U.b], w=[mix.b])
        barrier(kb)

    def phase_peer(self, li, need_ctx, final_out=None):
        kb, nc = self.kb, self.nc
        S = self.scr
        modrow = S['modrow']
        xcur = S['xcur']
        ntile = NT if need_ctx else 32
        pu = S['pu_bf'].t
        pv_ = S['pv_bf'].t
        with ExitStack() as es0:
            cbuf = [sb(es0, kb, f"pcv{i}", [128, 8192], BF16) for i in range(3)]
            n = 0
            for (src, dst) in ((self.din['peer_u'], S['pu_bf']), (self.din['peer_v'], S['pv_bf'])):
                for c in range(16):
                    cb_ = cbuf[n % 3]
                    n += 1
                    kb.dma("pool", cb_[:], src.t[li, c * 1024:(c + 1) * 1024, :].rearrange("(p r) d -> p (r d)", p=128),
                           w=[cb_.b])
                    r0 = li * 16384 + c * 1024
                    kb.dma("sp", dst.t[r0:r0 + 1024, :].rearrange("(p r) d -> p (r d)", p=128), cb_[:], r=[cb_.b], w=[dst.b])
        barrier(kb)
        if not hasattr(self, '_bc_reg'):
            self._bc_reg = nc.gpsimd.alloc_register("peer_bc")
            nc.gpsimd.reg_mov(self._bc_reg, 32767)
        bc_reg = self._bc_reg
        NB = 16
        with ExitStack() as es:
            identb = self.make_ident(es, BF16, "identb_p")
            identf = self.make_ident(es, F32, "identf_p")
            Wq = sb(es, kb, "pWq", [128, 8, 2048], BF16)
            keysT = sb(es, kb, "pkeysT", [128, 16, 128], BF16)
            es_w = ExitStack()
            wst = [sb(es_w, kb, f"pwst{i}", [128, 2048], F32) for i in range(2)]
            for k in range(8):
                kb.dma("sp", wst[k % 2][:], self.din['peer_wq'].t[li, k * 128:(k + 1) * 128, :], w=[wst[k % 2].b])
                kb.op("pool", lambda e: e.tensor_copy(out=Wq[:, k, :], in_=wst[k % 2][:]), r=[wst[k % 2].b], w=[Wq.b])
            pq = [ps(es, kb, f"ppq{i}", [128, 4, 128], F32) for i in range(2)]
            psr = [ps(es, kb, f"ppsr{i}", [128, 4, 128], F32) for i in range(2)]
            tp = [ps(es, kb, f"pptp{i}", [128, 8, 128], BF16) for i in range(2)]
            for c4 in range(4):
                kst = wst[c4 % 2]
                kb.dma("sp", kst[:, 0:512].rearrange("n (c d) -> n c d", c=4),
                       self.din['peer_keys'].t[li].rearrange("h p n d -> n (h p) d")[:, c4 * 4:(c4 + 1) * 4, :], w=[kst.b])
                p_ = pq[c4 % 2]
                for j in range(4):
                    kb.op("pe", lambda e: e.transpose(p_[:, j, :], kst[:, j * 128:(j + 1) * 128], identf[:]),
                          r=[kst.b, identf.b], w=[p_.b])
                kb.op("act", lambda e: e.copy(out=keysT[:, c4 * 4:(c4 + 1) * 4, :], in_=p_[:]), r=[p_.b], w=[keysT.b])
            barrier(kb)
            es_w.close()
            G = [sb(es, kb, f"pG{w_}", [128, D], F32) for w_ in range(2)]
            SH = [sb(es, kb, f"pSH{w_}", [128, D], F32) for w_ in range(2)]
            GF = [sb(es, kb, f"pGF{w_}", [128, D], F32) for w_ in range(2)]
            gn = sb(es, kb, "pgn", [128, D], F32)
            kb.dma("sp", gn[:], self.din['ffn_norm'].t[li:li + 1, :].partition_broadcast(128), w=[gn.b])
            for w_ in range(2 if need_ctx else 1):
                kb.dma("sp", G[w_][:], modrow.t[w_:w_ + 1, 4 * D:5 * D].partition_broadcast(128), r=[modrow.b], w=[G[w_].b])
                kb.dma("sp", SH[w_][:], modrow.t[w_:w_ + 1, 3 * D:4 * D].partition_broadcast(128), r=[modrow.b], w=[SH[w_].b])
                kb.dma("sp", GF[w_][:], modrow.t[w_:w_ + 1, 5 * D:6 * D].partition_broadcast(128), r=[modrow.b], w=[GF[w_].b])
                kb.op("dve", lambda e: e.scalar_tensor_tensor(out=G[w_][:], in0=G[w_][:], scalar=1.0, in1=gn[:],
                                                              op0=ALU.add, op1=ALU.mult), r=[gn.b], w=[G[w_].b])
            xt = [sb(es, kb, f"pxt{i}", [128, D], F32) for i in range(2)]
            hn = [sb(es, kb, f"phn{i}", [128, D], F32) for i in range(2)]
            hnb2 = [sb(es, kb, f"phnb{i}", [128, D], BF16) for i in range(2)]
            junk = sb(es, kb, "pjunk", [128, D], F32)
            junkb = sb(es, kb, "pjunkb", [128, D], BF16)
            ss = sb(es, kb, "pss", [128, 1], F32)
            hnT = sb(es, kb, "phnT", [128, 8, 128], BF16)
            qTs = sb(es, kb, "pqTs", [128, 16, 128], BF16)
            s_sb = sb(es, kb, "ps_sb", [128, 16, 128], F32)
            tmpc = sb(es, kb, "ptmpc", [128, 256], F32)
            sv = sb(es, kb, "psv", [128, 16, 16], F32)
            siu = sb(es, kb, "psiu", [128, 16, 16], U32)
            sif = sb(es, kb, "psif", [128, 16, 16], F32)
            s1x = sb(es, kb, "ps1x", [128, 8, 16], F32)
            cand = sb(es, kb, "pcand", [128, 8, 256], F32)
            ci = sb(es, kb, "pci", [128, 8, 256], F32)
            tsv = sb(es, kb, "ptsv", [128, 8, 16], F32)
            posu = sb(es, kb, "pposu", [128, 8, 16], U32)
            pau = sb(es, kb, "ppau", [128, 8, 16], U32)
            pbu = sb(es, kb, "ppbu", [128, 8, 16], U32)
            paf = sb(es, kb, "ppaf", [128, 8, 16], F32)
            pbf = sb(es, kb, "ppbf", [128, 8, 16], F32)
            oh = sb(es, kb, "poh", [128, 16, 16], F32)
            eidb = sb(es, kb, "peidb", [128, 8, 16], F32)
            iota16 = sb(es, kb, "piota", [128, 16], F32)
            kb.dma("sp", iota16[:], self.consts['iota16'].t[0:1, :].partition_broadcast(128), w=[iota16.b])
            eidf = sb(es, kb, "peidf", [128, 8, 16], F32)
            eidx = [sb(es, kb, f"peidx{i}", [128, 128], I32) for i in range(2)]
            gate2 = [sb(es, kb, f"pgate{i}", [128, 8, 16], F32) for i in range(2)]
            gsum = sb(es, kb, "pgsum", [128, 8], F32)
            act = sb(es, kb, "pact", [128, 128], F32)
            wgt = sb(es, kb, "pwgt", [128, 128], F32)
            acc = sb(es, kb, "pacc", [128, D], F32)
            dgs = [sb(es, kb, f"pdg{i}", [128, 128], BF16) for i in range(4)]
            pacc = [ps(es, kb, f"ppacc{i}", [128, 512], F32) for i in range(2)]
            ug = [sb(es, kb, f"pug{i}", [128, D], BF16) for i in range(NB)]
            vg = ug
            nev = [0]

            def stage_a(tt):
                b = tt % 2
                ts = slice(tt * 128, (tt + 1) * 128)
                which = 0 if tt < 32 else 1
                x_, h_ = xt[b], hn[b]
                hnb = hnb2[b]
                gate = gate2[b]
                kb.dma("sp", x_[:], xcur.t[ts, :], r=[xcur.b], w=[x_.b])
                kb.op("act", lambda e: e.activation(out=junk[:], in_=x_[:], func=AF.Square, accum_out=ss[:]),
                      r=[x_.b], w=[junk.b, ss.b])
                kb.op("dve", lambda e: e.tensor_scalar(out=ss[:], in0=ss[:], scalar1=1.0 / D, scalar2=NORM_EPS,
                                                       op0=ALU.mult, op1=ALU.add), w=[ss.b])
                kb.op("act", lambda e: e.activation(out=ss[:], in_=ss[:], func=AF.Sqrt), w=[ss.b])
                kb.op("dve", lambda e: e.reciprocal(out=ss[:], in_=ss[:]), w=[ss.b])
                kb.op("dve", lambda e: e.scalar_tensor_tensor(out=h_[:], in0=x_[:], scalar=ss[:, 0:1], in1=G[which][:],
                                                              op0=ALU.mult, op1=ALU.mult), r=[x_.b, ss.b, G[which].b], w=[h_.b])
                kb.op("dve", lambda e: e.tensor_tensor(out=h_[:], in0=h_[:], in1=SH[which][:], op=ALU.add),
                      r=[SH[which].b], w=[h_.b])
                kb.op("act", lambda e: e.copy(out=hnb[:], in_=h_[:]), r=[h_.b], w=[hnb.b])
                for k in range(8):
                    kb.op("pe", lambda e: e.transpose(tp[b][:, k, :], hnb[:, k * 128:(k + 1) * 128], identb[:]),
                          r=[hnb.b, identb.b], w=[tp[b].b])
                kb.op("act", lambda e: e.copy(out=hnT[:], in_=tp[b][:]), r=[tp[b].b], w=[hnT.b])
                for c4 in range(4):
                    p_ = pq[c4 % 2]
                    for j in range(4):
                        c = c4 * 4 + j
                        for k in range(8):
                            kb.op("pe", lambda e: e.matmul(p_[:, j, :], lhsT=Wq[:, k, c * 128:(c + 1) * 128], rhs=hnT[:, k, :],
                                                           start=(k == 0), stop=(k == 7)), r=[Wq.b, hnT.b], w=[p_.b])
                    nev[0] += 1
                    if nev[0] % 2:
                        kb.op("act", lambda e: e.copy(out=qTs[:, c4 * 4:(c4 + 1) * 4, :], in_=p_[:]), r=[p_.b], w=[qTs.b])
                    else:
                        kb.op("dve", lambda e: e.tensor_copy(out=qTs[:, c4 * 4:(c4 + 1) * 4, :], in_=p_[:]), r=[p_.b], w=[qTs.b])
                for c4 in range(4):
                    p_ = psr[c4 % 2]
                    for j in range(4):
                        c = c4 * 4 + j
                        kb.op("pe", lambda e: e.matmul(p_[:, j, :], lhsT=qTs[:, c, :], rhs=keysT[:, c, :], start=True, stop=True),
                              r=[qTs.b, keysT.b], w=[p_.b])
                    kb.op("act", lambda e: e.copy(out=s_sb[:, c4 * 4:(c4 + 1) * 4, :], in_=p_[:]), r=[p_.b], w=[s_sb.b])
                for c in range(16):
                    kb.op("dve", lambda e: e.max(out=sv[:, c, 0:8], in_=s_sb[:, c, :]), r=[s_sb.b], w=[sv.b])
                    kb.op("dve", lambda e: e.max_index(out=siu[:, c, 0:8], in_max=sv[:, c, 0:8], in_values=s_sb[:, c, :]),
                          r=[s_sb.b, sv.b], w=[siu.b])
                    kb.op("dve", lambda e: e.match_replace(out=tmpc[:, 0:128], in_to_replace=sv[:, c, 0:8], in_values=s_sb[:, c, :],
                                                           imm_value=-1e30), r=[s_sb.b, sv.b], w=[tmpc.b])
                    kb.op("dve", lambda e: e.max(out=sv[:, c, 8:16], in_=tmpc[:, 0:128]), r=[tmpc.b], w=[sv.b])
                    kb.op("dve", lambda e: e.max_index(out=siu[:, c, 8:16], in_max=sv[:, c, 8:16], in_values=tmpc[:, 0:128]),
                          r=[tmpc.b, sv.b], w=[siu.b])
                kb.op("dve", lambda e: e.tensor_copy(out=sif[:], in_=siu[:]), r=[siu.b], w=[sif.b])
                sv4 = sv[:].rearrange("p (h q) k -> p h q k", q=2)
                si4 = sif[:].rearrange("p (h q) k -> p h q k", q=2)
                kb.op("dve", lambda e: e.tensor_scalar(out=s1x[:], in0=si4[:, :, 0, :], scalar1=128.0, scalar2=None, op0=ALU.mult),
                      r=[sif.b], w=[s1x.b])
                c4v = cand[:].rearrange("p h (a b) -> p h a b", a=16)
                i4v = ci[:].rearrange("p h (a b) -> p h a b", a=16)
                kb.op("dve", lambda e: e.tensor_tensor(out=c4v, in0=sv4[:, :, 0, :].unsqueeze(3).to_broadcast([128, 8, 16, 16]),
                                                       in1=sv4[:, :, 1, :].unsqueeze(2).to_broadcast([128, 8, 16, 16]), op=ALU.add),
                      r=[sv.b], w=[cand.b])
                kb.op("dve", lambda e: e.tensor_tensor(out=i4v, in0=s1x[:].unsqueeze(3).to_broadcast([128, 8, 16, 16]),
                                                       in1=si4[:, :, 1, :].unsqueeze(2).to_broadcast([128, 8, 16, 16]), op=ALU.add),
                      r=[s1x.b, sif.b], w=[ci.b])
                for h in range(8):
                    kb.op("dve", lambda e: e.max(out=tsv[:, h, 0:8], in_=cand[:, h, :]), r=[cand.b], w=[tsv.b])
                    kb.op("dve", lambda e: e.max_index(out=posu[:, h, 0:8], in_max=tsv[:, h, 0:8], in_values=cand[:, h, :]),
                          r=[cand.b, tsv.b], w=[posu.b])
                    kb.op("dve", lambda e: e.match_replace(out=tmpc[:], in_to_replace=tsv[:, h, 0:8], in_values=cand[:, h, :],
                                                           imm_value=-1e30), r=[cand.b, tsv.b], w=[tmpc.b])
                    kb.op("dve", lambda e: e.max(out=tsv[:, h, 8:16], in_=tmpc[:]), r=[tmpc.b], w=[tsv.b])
                    kb.op("dve", lambda e: e.max_index(out=posu[:, h, 8:16], in_max=tsv[:, h, 8:16], in_values=tmpc[:]),
                          r=[tmpc.b, tsv.b], w=[posu.b])
                kb.op("dve", lambda e: e.tensor_single_scalar(out=pau[:], in_=posu[:], scalar=4, op=ALU.logical_shift_right),
                      r=[posu.b], w=[pau.b])
                kb.op("dve", lambda e: e.tensor_single_scalar(out=pbu[:], in_=posu[:], scalar=15, op=ALU.bitwise_and),
                      r=[posu.b], w=[pbu.b])
                kb.op("dve", lambda e: e.tensor_copy(out=paf[:], in_=pau[:]), r=[pau.b], w=[paf.b])
                kb.op("dve", lambda e: e.tensor_copy(out=pbf[:], in_=pbu[:]), r=[pbu.b], w=[pbf.b])
                for h in range(8):
                    for (pf, src, first) in ((paf, s1x[:, h, :], True), (pbf, si4[:, h, 1, :], False)):
                        kb.op("dve", lambda e: e.tensor_tensor(out=oh[:], in0=pf[:, h, :].unsqueeze(2).to_broadcast([128, 16, 16]),
                                                               in1=iota16[:].unsqueeze(1).to_broadcast([128, 16, 16]),
                                                               op=ALU.is_equal), r=[pf.b, iota16.b], w=[oh.b])
                        kb.op("dve", lambda e: e.tensor_tensor(out=oh[:], in0=oh[:], in1=src.unsqueeze(1).to_broadcast([128, 16, 16]),
                                                               op=ALU.mult), r=[s1x.b, sif.b], w=[oh.b])
                        dst = eidf if first else eidb
                        kb.op("dve", lambda e: e.tensor_reduce(out=dst[:, h, :], in_=oh[:], axis=AX.X, op=ALU.add), r=[oh.b], w=[dst.b])
                kb.op("dve", lambda e: e.tensor_tensor(out=eidf[:], in0=eidf[:], in1=eidb[:], op=ALU.add), r=[eidb.b], w=[eidf.b])
                ei = eidx[b]
                if li > 0:
                    kb.op("dve", lambda e: e.tensor_scalar(out=eidf[:], in0=eidf[:], scalar1=float(li * 16384), scalar2=None,
                                                           op0=ALU.add), w=[eidf.b])
                kb.op("dve", lambda e: e.tensor_copy(out=ei[:], in_=eidf[:].rearrange("p h k -> p (h k)")), r=[eidf.b], w=[ei.b])
                kb.op("dve", lambda e: e.tensor_tensor(out=gate[:], in0=tsv[:], in1=tsv[:, :, 0:1].to_broadcast([128, 8, 16]),
                                                       op=ALU.subtract), r=[tsv.b], w=[gate.b])
                kb.op("act", lambda e: e.activation(out=gate[:], in_=gate[:], func=AF.Exp), w=[gate.b])
                kb.op("dve", lambda e: e.tensor_reduce(out=gsum[:], in_=gate[:], axis=AX.X, op=ALU.add), r=[gate.b], w=[gsum.b])
                kb.op("dve", lambda e: e.reciprocal(out=gsum[:], in_=gsum[:]), w=[gsum.b])
                kb.op("dve", lambda e: e.tensor_tensor(out=gate[:], in0=gate[:], in1=gsum[:].unsqueeze(2).to_broadcast([128, 8, 16]),
                                                       op=ALU.mult), r=[gsum.b], w=[gate.b])

            def stage_b(tt):
                b = tt % 2
                ts = slice(tt * 128, (tt + 1) * 128)
                which = 0 if tt < 32 else 1
                x_, h_ = xt[b], hn[b]
                hnb = hnb2[b]
                gate = gate2[b]
                ei = eidx[b]
                kb.op("dve", lambda e: e.memset(act[:], 0.0), w=[act.b])
                for slot in range(128):
                    u_ = ug[slot % NB]
                    kb.idma(u_[:, :], pu, bass.IndirectOffsetOnAxis(ap=ei[:, slot:slot + 1], axis=0), r=[ei.b, S['pu_bf'].b], w=[u_.b],
                            bounds_check=bc_reg, oob_is_err=False)
                    kb.op("dve", lambda e: e.scalar_tensor_tensor(out=junkb[:], in0=u_[:], scalar=1.0, in1=hnb[:],
                                                                  op0=ALU.mult, op1=ALU.mult, accum_out=act[:, slot:slot + 1]),
                          r=[u_.b, hnb.b], w=[junkb.b, act.b])
                kb.op("act", lambda e: e.activation(out=wgt[:], in_=act[:], func=AF.Gelu), r=[act.b], w=[wgt.b])
                kb.op("dve", lambda e: e.tensor_tensor(out=wgt[:], in0=wgt[:], in1=gate[:].rearrange("p h k -> p (h k)"), op=ALU.mult),
                      r=[gate.b], w=[wgt.b])
                for slot in range(128):
                    v_ = vg[slot % NB]
                    kb.idma(v_[:, :], pv_, bass.IndirectOffsetOnAxis(ap=ei[:, slot:slot + 1], axis=0), r=[ei.b, S['pv_bf'].b], w=[v_.b],
                            bounds_check=bc_reg, oob_is_err=False)
                    dg = dgs[slot % 4]
                    kb.op("dve", lambda e: e.tensor_scalar(out=dg[:], in0=identb[:], scalar1=wgt[:, slot:slot + 1], scalar2=None,
                                                           op0=ALU.mult), r=[identb.b, wgt.b], w=[dg.b])
                    for n in range(2):
                        kb.op("pe", lambda e: e.matmul(pacc[n][:, :], lhsT=dg[:], rhs=v_[:, n * 512:(n + 1) * 512],
                                                       start=(slot == 0), stop=(slot == 127)), r=[dg.b, v_.b], w=[pacc[n].b])
                for n in range(2):
                    kb.op("dve", lambda e: e.tensor_tensor(out=acc[:, n * 512:(n + 1) * 512], in0=pacc[n][:, :],
                                                           in1=GF[which][:, n * 512:(n + 1) * 512], op=ALU.mult),
                          r=[pacc[n].b, GF[which].b], w=[acc.b])
                kb.op("dve", lambda e: e.tensor_tensor(out=x_[:], in0=x_[:], in1=acc[:], op=ALU.add), r=[acc.b], w=[x_.b])
                if final_out is not None and tt < 32:
                    kb.dma("sp", final_out.t[ts, :], x_[:], r=[x_.b], w=[final_out.b])
                else:
                    kb.dma("sp", xcur.t[ts, :], x_[:], r=[x_.b], w=[xcur.b])

            stage_a(0)
            for tt in range(ntile):
                if tt + 1 < ntile:
                    stage_a(tt + 1)
                stage_b(tt)
        barrier(kb)

    def zero_dram(self, tl, nelem):
        kb = self.kb
        with ExitStack() as es:
            z = sb(es, kb, "zeros2", [128, 8192], F32)
            kb.op("pool", lambda e: e.memset(z[:], 0.0), w=[z.b])
            nd = len(tl.t.shape)
            names = " ".join(f"a{i}" for i in range(nd))
            flat = tl.t.rearrange(f"{names} -> ({names})")
            per = 128 * 8192
            off = 0
            while off < nelem:
                n = min(per, nelem - off)
                cols = n // 128
                kb.dma("sp", flat[off:off + n].rearrange("(p c) -> p c", p=128), z[:, :cols], r=[z.b], w=[tl.b])
                off += n
        barrier(kb)

USE_HY = True
USE_SCAN2 = True
USE_PEER = True


def build(dbg=None, nlayers=DEPTH, scan_steps=None):
    P = Prog(dbg, nlayers)
    kb = P.kb
    P._prep_buf = Buf("prep")
    P.const_in('ident_f32', [128, 128])
    P.const_in('blockones', [128, 128])
    P.scratch('modrow', [2, 6 * D])
    P.scratch('zT', [20, 128, T])
    P.scratch('krope_tm', [T, 32])
    P.scratch('xcur', [T, D])
    P.scratch('w_fm', [2, 3, 128, T])
    P.scratch('kk_fm', [3, 128, T], BF16)
    P.scratch('rb_fm', [3, 128, T], BF16)
    P.scratch('kk6_tm', [6, T, 128], BF16)
    P.scratch('akk6_tm', [2, 6, T, 384], BF16)
    P.scratch('krep6_tm', [2, 6, T, 128], BF16)
    P.scratch('v6_tm', [6, T, 192], BF16)
    P.scratch('L4_tm', [2, 4, T, 384], BF16)
    P.scratch('V2_tm', [2, T, 3, 64], BF16)
    P.scratch('y_fm', [2, 3, 2, 64, T])
    for n in ('v_tm', 'r_tm', 'k_tm', 'g_tm'):
        P.scratch(n, [T, 384])
    P.scratch('mix_tm', [T, D])
    P.const_in('rope_cs', [SEQ, 32])
    P.const_in('antiident', [128, 128])
    P.const_in('iota16', [1, 16])
    P.const_in('hy_nrate', [256])
    P.const_in('hy_z_l', [2, 33, SEQ])
    P.const_in('hy_tn_l', [2, SEQ])
    P.const_in('hy_z_c', [2, 33, CTX])
    P.const_in('hy_tn_c', [2, CTX])
    P.scratch('Hd_l', [2, 256, 8192], BF16)
    P.scratch('Hd_c', [2, 256, 8192], BF16)
    P.scratch('hy_tm', [3, T, 256])
    P.scratch('pu_bf', [2 * 16384, D], BF16)
    P.scratch('pv_bf', [2 * 16384, D], BF16)
    P.scratch('qT', [96, 6, T], BF16)
    P.scratch('kT', [96, 6, T], BF16)
    P.scratch('V1', [T, 6, 65], BF16)

    def dump(names):
        outs = []
        for n in names:
            src = P.scr[n]
            o = P.out_tensor("o_" + n, list(src.t.shape), src.t.dtype)
            kb.dma("sp", o.t, src.t, r=[src.b, P._prep_buf], w=[o.b])
            outs.append(o.b)
        kb.wait_all("sp", outs)

    out = None
    if dbg is None:
        out = P.out_tensor("out", [SEQ, D])
    P.phase_zero_init()
    P.zero_dram(P.scr['mix_tm'], T * D)
    for li in range(nlayers):
        first = (li == 0)
        need_ctx = li < DEPTH - 1
        last = (li == nlayers - 1)
        P.phase_mod(li)
        P.phase_inproj(li, first)
        if dbg == 'inproj':
            dump(['zT', 'krope_tm', 'modrow'])
            return P
        P.phase_rw_prep(li)
        if dbg == 'rwprep':
            dump(['w_fm', 'kk_fm', 'L4_tm', 'V2_tm', 'v_tm', 'r_tm', 'k_tm', 'g_tm'])
            return P
        if USE_SCAN2:
            P.phase_rw_scan2(li, scan_steps)
        else:
            P.phase_rw_scan(li, scan_steps)
        if dbg == 'rwscan':
            dump(['y_fm', 'w_fm', 'kk_fm', 'L4_tm', 'V2_tm', 'v_tm', 'r_tm', 'k_tm', 'g_tm'])
            return P
        P.phase_rw_readout(li, need_ctx)
        if dbg == 'readout':
            dump(['mix_tm', 'y_fm'])
            return P
        P.phase_mla_prep(li)
        if dbg == 'mlaprep':
            dump(['qT', 'kT', 'V1'])
            return P
        P.phase_mla_attn(li, need_ctx)
        if dbg == 'mla':
            dump(['mix_tm'])
            return P
        if USE_HY:
            P.phase_hy_filter(li, SEQ, P.scr['Hd_l'], '_l')
            if need_ctx:
                P.phase_hy_filter(li, CTX, P.scr['Hd_c'], '_c')
            if dbg == 'hyfilt':
                dump(['Hd_l', 'Hd_c'])
                return P
            P.phase_hyena(li, need_ctx)
            if dbg == 'hyena':
                dump(['mix_tm'])
                return P
        P.phase_outproj(li, first, need_ctx, final_out=(out if (last and dbg is None and not USE_PEER) else None))
        if dbg == 'outproj':
            dump(['xcur', 'mix_tm'])
            return P
        if USE_PEER:
            P.phase_peer(li, need_ctx, final_out=(out if (last and dbg is None) else None))
            if dbg == 'peer':
                dump(['xcur'])
                return P
    kb.wait_all("sp", [out.b])
    return P


def host_consts():
    c = {}
    c['ident_f32'] = np.eye(128, dtype=np.float32)
    bo = np.zeros((128, 128), np.float32)
    bo[:64, :64] = 1.0
    bo[64:, 64:] = 1.0
    c['blockones'] = bo
    rows = SEQ // 64
    row, col = np.meshgrid(np.arange(rows), np.arange(64), indexing='ij')
    inv = (10000.0 ** (-np.arange(0, 16, 2, dtype=np.float32) / 16)).astype(np.float32)
    pos = np.stack([row.reshape(-1), col.reshape(-1)], axis=-1).astype(np.float32)
    ang = (pos[:, :, None] * inv[None, None, :]).astype(np.float32)
    c['iota16'] = np.arange(16, dtype=np.float32).reshape(1, 16)
    c['antiident'] = np.ascontiguousarray(np.eye(128, dtype=np.float32)[::-1])
    rates = np.abs(np.linspace(math.log(1e-2) / 1.5, math.log(1e-2) / 0.3, 256, dtype=np.float32)).astype(np.float32)
    c['hy_nrate'] = (-rates).astype(np.float32)
    for L_, suf in ((SEQ, '_l'), (CTX, '_c')):
        tn = (np.arange(L_, dtype=np.float32) / np.float32(L_)).astype(np.float32)
        bands = np.arange(1, 17, dtype=np.float32)
        angh = (np.float32(2.0 * math.pi) * tn[:, None] * bands[None, :]).astype(np.float32)
        z = np.concatenate([tn[:, None], np.cos(angh), np.sin(angh)], axis=-1).astype(np.float32)
        c['hy_z' + suf] = np.ascontiguousarray(np.stack([z.T, z[::-1].T]).astype(np.float32))
        c['hy_tn' + suf] = np.ascontiguousarray(np.stack([tn, tn[::-1]]).astype(np.float32))
    c['rope_cs'] = np.concatenate([np.cos(ang).reshape(SEQ, 16), np.sin(ang).reshape(SEQ, 16)], axis=1).astype(np.float32)
    return c


_PROG = {}


def kernel(**inputs):
    if 'p' not in _PROG:
        _PROG['p'] = build(None)
    P = _PROG['p']
    hc = host_consts()
    in_maps = []
    for b in range(8):
        m = {'x': np.ascontiguousarray(inputs['x'][b], dtype=np.float32),
             'ctx': np.ascontiguousarray(inputs['ctx'][b], dtype=np.float32),
             'c2': np.ascontiguousarray(np.stack([inputs['c'][b], inputs['c_ctx']]), dtype=np.float32)}
        for n in INPUT_NAMES:
            m[n] = np.ascontiguousarray(inputs[n], dtype=np.float32)
        for k in P.consts:
            m[k] = hc[k]
        in_maps.append(m)
    res = run_bass_kernel_spmd(P.nc, in_maps, core_ids=list(range(8)))
    return np.stack([np.asarray(r['out'], dtype=np.float32) for r in res.results], axis=0)
```

```python
import math
from contextlib import ExitStack
import numpy as np
import concourse.bass as bass
import concourse.mybir as mybir
from concourse.bass_utils import run_bass_kernel_spmd

F32 = mybir.dt.float32
BF16 = mybir.dt.bfloat16
I32 = mybir.dt.int32
U32 = mybir.dt.uint32
AF = mybir.ActivationFunctionType
ALU = mybir.AluOpType
AX = mybir.AxisListType

D = 1024
SEQ = 4096
CTX = 256
T = SEQ + CTX
NT = T // 128
DEPTH = 2
RW_PROJ, MLA_PROJ, HY_PROJ = 1408, 416, 768
IN_PROJ = 2592
NORM_EPS = 1e-6

SEM_CAP = 30000


class Buf:
    __slots__ = ("w", "r", "name")

    def __init__(self, name=""):
        self.w = None
        self.r = {}
        self.name = name


class KB:
    def __init__(self):
        self.nc = bass.Bass("TRN2", target_bir_lowering=False)
        nc = self.nc
        self.engs = {"pe": nc.tensor, "act": nc.scalar, "dve": nc.vector,
                     "pool": nc.gpsimd, "sp": nc.sync}
        self.sem = {}
        self.cnt = {}
        self.seen = {e: {} for e in self.engs}
        self.nsem = 0
        self.semh = {}
        self.dma_pool = {}
        self.dma_rr = {}
        self.ndma_sems = {"sp": 12, "pool": 16, "act": 4}
        self.n_ins = 0
        self.same_engine_sync = True

    def new_sem(self, name):
        h = self.nc.alloc_semaphore(name=f"{name}_{self.nsem}")
        sid = self.nsem
        self.nsem += 1
        self.semh[sid] = h
        return sid

    def _wait(self, e, toks):
        seen = self.seen[e]
        best = {}
        for (sid, val, src) in toks:
            if src == "pe" and e == "pe":
                continue
            if (not self.same_engine_sync) and src == e:
                continue
            if seen.get(sid, 0) >= val:
                continue
            if best.get(sid, 0) < val:
                best[sid] = val
        for sid, val in best.items():
            self.engs[e].wait_ge(self.semh[sid], val)
            seen[sid] = val
            self.n_ins += 1

    def _deps(self, r, w):
        toks = []
        for b in r:
            if b.w is not None:
                toks.append(b.w)
        for b in w:
            if b.w is not None:
                toks.append(b.w)
            toks.extend(b.r.values())
        return toks

    def _record(self, tok, r, w):
        for b in r:
            old = b.r.get(tok[0])
            if old is None or old[1] < tok[1]:
                b.r[tok[0]] = tok
        for b in w:
            b.w = tok
            b.r = {}

    def op(self, e, fn, r=(), w=()):
        self._wait(e, self._deps(r, w))
        ins = fn(self.engs[e])
        if e not in self.sem or self.cnt[e] >= SEM_CAP:
            self.sem[e] = self.new_sem(e)
            self.cnt[e] = 0
        self.cnt[e] += 1
        ins.then_inc(self.semh[self.sem[e]], 1)
        tok = (self.sem[e], self.cnt[e], e)
        self._record(tok, r, w)
        self.n_ins += 1
        return tok

    def dma(self, q, out, in_, r=(), w=(), **kw):
        toks = self._deps(r, w)
        if q not in self.dma_pool:
            self.dma_pool[q] = [[self.new_sem(f"dma{q}"), 0] for _ in range(self.ndma_sems[q])]
            self.dma_rr[q] = 0
        slot = self.dma_rr[q]
        self.dma_rr[q] = (slot + 1) % len(self.dma_pool[q])
        ent = self.dma_pool[q][slot]
        if ent[1] + 16 > SEM_CAP:
            ent[0] = self.new_sem(f"dma{q}")
            ent[1] = 0
        if ent[1] > 0:
            toks.append((ent[0], ent[1], "dma"))
        self._wait(q, toks)
        ins = self.engs[q].dma_start(out=out, in_=in_, **kw)
        ent[1] += 16
        ins.then_inc(self.semh[ent[0]], 16)
        tok = (ent[0], ent[1], "dma")
        self._record(tok, r, w)
        self.n_ins += 1
        return tok

    def idma(self, out, in_, in_off, r=(), w=(), **kw):
        q = "pool"
        toks = self._deps(r, w)
        if q not in self.dma_pool:
            self.dma_pool[q] = [[self.new_sem(f"dma{q}"), 0] for _ in range(self.ndma_sems[q])]
            self.dma_rr[q] = 0
        slot = self.dma_rr[q]
        self.dma_rr[q] = (slot + 1) % len(self.dma_pool[q])
        ent = self.dma_pool[q][slot]
        if ent[1] + 16 > SEM_CAP:
            ent[0] = self.new_sem(f"dma{q}")
            ent[1] = 0
        if ent[1] > 0:
            toks.append((ent[0], ent[1], "dma"))
        self._wait(q, toks)
        ins = self.nc.gpsimd.indirect_dma_start(out=out, out_offset=None, in_=in_, in_offset=in_off, **kw)
        ent[1] += 16
        ins.then_inc(self.semh[ent[0]], 16)
        tok = (ent[0], ent[1], "dma")
        self._record(tok, r, w)
        self.n_ins += 1
        return tok

    def wait_all(self, e, bufs):
        toks = []
        for b in bufs:
            if b.w is not None:
                toks.append(b.w)
            toks.extend(b.r.values())
        self._wait(e, toks)


class Tl:
    def __init__(self, t, name, nsub=1):
        self.t = t
        self.b = Buf(name)
        self.sub = [Buf(f"{name}{i}") for i in range(nsub)] if nsub > 1 else None

    def __getitem__(self, idx):
        return self.t[idx]


_UID = [0]


def sb(es, kb, name, shape, dt, nsub=1):
    _UID[0] += 1
    name = f"{name}_{_UID[0]}"
    return Tl(es.enter_context(kb.nc.sbuf_tensor(name, list(shape), dt)), name, nsub)


def ps(es, kb, name, shape, dt, nsub=1):
    _UID[0] += 1
    name = f"{name}_{_UID[0]}"
    return Tl(es.enter_context(kb.nc.psum_tensor(name, list(shape), dt)), name, nsub)


def barrier(kb):
    toks = []
    for e in kb.sem:
        toks.append((kb.sem[e], kb.cnt[e], "bar"))
    for q in kb.dma_pool:
        for ent in kb.dma_pool[q]:
            if ent[1] > 0:
                toks.append((ent[0], ent[1], "bar"))
    for e in kb.engs:
        kb._wait(e, toks)


INPUT_NAMES = ['mod_w', 'mod_b', 'mix_norm', 'w_in', 'w_out',
               'rw_conv', 'rw_decay_up', 'rw_decay0', 'rw_a_up', 'rw_a0', 'rw_gate_up', 'rw_k_k', 'rw_k_a',
               'rw_r_k', 'rw_gn_g', 'rw_gn_b',
               'mla_q_norm', 'mla_w_uq', 'mla_kv_norm', 'mla_w_ukv', 'mla_q_gain', 'mla_k_gain',
               'hy_conv', 'hy_w1', 'hy_b1', 'hy_freq1', 'hy_w2', 'hy_b2', 'hy_freq2', 'hy_w3', 'hy_b3', 'hy_bias',
               'ffn_norm', 'peer_wq', 'peer_keys', 'peer_u', 'peer_v']

WEIGHT_SHAPES = {
    'mod_w': (2, 1024, 6144), 'mod_b': (2, 6144), 'mix_norm': (2, 1024), 'w_in': (2, 1024, 2592),
    'w_out': (2, 1024, 1024), 'rw_conv': (2, 3, 1408), 'rw_decay_up': (2, 2, 64, 384), 'rw_decay0': (2, 2, 384),
    'rw_a_up': (2, 2, 64, 384), 'rw_a0': (2, 2, 384), 'rw_gate_up': (2, 128, 384), 'rw_k_k': (2, 384),
    'rw_k_a': (2, 384), 'rw_r_k': (2, 384), 'rw_gn_g': (2, 384), 'rw_gn_b': (2, 384),
    'mla_q_norm': (2, 256), 'mla_w_uq': (2, 256, 576), 'mla_kv_norm': (2, 128), 'mla_w_ukv': (2, 128, 768),
    'mla_q_gain': (2, 96), 'mla_k_gain': (2, 96), 'hy_conv': (2, 3, 768), 'hy_w1': (2, 33, 64), 'hy_b1': (2, 64),
    'hy_freq1': (2, 64), 'hy_w2': (2, 64, 64), 'hy_b2': (2, 64), 'hy_freq2': (2, 64), 'hy_w3': (2, 64, 1024),
    'hy_b3': (2, 1024), 'hy_bias': (2, 2, 256), 'ffn_norm': (2, 1024), 'peer_wq': (2, 1024, 2048),
    'peer_keys': (2, 8, 2, 128, 128), 'peer_u': (2, 16384, 1024), 'peer_v': (2, 16384, 1024),
}


class Prog:
    def __init__(self, dbg=None, nlayers=DEPTH):
        self.kb = KB()
        self.nc = self.kb.nc
        self.dbg = dbg
        self.nlayers = nlayers
        nc = self.nc
        self.din = {}
        self.din['x'] = Tl(nc.dram_tensor("x", [SEQ, D], F32, kind="ExternalInput").ap(), "x")
        self.din['ctx'] = Tl(nc.dram_tensor("ctx", [CTX, D], F32, kind="ExternalInput").ap(), "ctx")
        self.din['c2'] = Tl(nc.dram_tensor("c2", [2, D], F32, kind="ExternalInput").ap(), "c2")
        for n in INPUT_NAMES:
            self.din[n] = Tl(nc.dram_tensor(n, list(WEIGHT_SHAPES[n]), F32, kind="ExternalInput").ap(), n)
        self.consts = {}
        self.scr = {}

    def const_in(self, name, shape, dt=F32):
        t = Tl(self.nc.dram_tensor(name, list(shape), dt, kind="ExternalInput").ap(), name)
        self.consts[name] = t
        return t

    def scratch(self, name, shape, dt=F32):
        t = Tl(self.nc.dram_tensor(name, list(shape), dt, kind="Internal").ap(), name)
        self.scr[name] = t
        return t

    def out_tensor(self, name, shape, dt=F32):
        t = Tl(self.nc.dram_tensor(name, list(shape), dt, kind="ExternalOutput").ap(), name)
        return t

    def phase_mod(self, li):
        kb, nc = self.kb, self.nc
        modrow = self.scr['modrow']
        with ExitStack() as es:
            cT = sb(es, kb, "cT", [128, 8, 2], F32)
            scT = sb(es, kb, "scT", [128, 8, 2], F32)
            sig = sb(es, kb, "sigT", [128, 8, 2], F32)
            mw = [sb(es, kb, f"mw{i}", [128, 3072], F32) for i in range(2)]
            modb = sb(es, kb, "modb", [2, 6144], F32)
            modsb = sb(es, kb, "modsb", [2, 6144], F32)
            pm = ps(es, kb, "pm", [2, 3072], F32)
            c2 = self.din['c2']
            for r_ in range(2):
                kb.dma("sp", cT[:, :, r_], c2.t[r_].rearrange("(k p) -> p k", p=128), r=[c2.b], w=[cT.b],
                       allow_slow_non_contiguous=True)
            for r_ in range(2):
                kb.dma("sp", modb[r_:r_ + 1, :], self.din['mod_b'].t[li:li + 1, :], w=[modb.b])
            kb.op("act", lambda e: e.activation(out=sig[:], in_=cT[:], func=AF.Sigmoid), r=[cT.b], w=[sig.b])
            kb.op("dve", lambda e: e.tensor_tensor(out=scT[:], in0=cT[:], in1=sig[:], op=ALU.mult),
                  r=[cT.b, sig.b], w=[scT.b])
            it = 0
            for half in range(2):
                for k in range(8):
                    m = mw[it % 2]
                    it += 1
                    kb.dma("sp", m[:], self.din['mod_w'].t[li, k * 128:(k + 1) * 128, half * 3072:(half + 1) * 3072],
                           w=[m.b])
                    for n in range(6):
                        kb.op("pe", lambda e: e.matmul(pm[:, n * 512:(n + 1) * 512], lhsT=scT[:, k, :],
                                                       rhs=m[:, n * 512:(n + 1) * 512], start=(k == 0), stop=(k == 7)),
                              r=[scT.b, m.b], w=[pm.b])
                kb.op("dve", lambda e: e.tensor_tensor(out=modsb[:, half * 3072:(half + 1) * 3072], in0=pm[:],
                                                       in1=modb[:, half * 3072:(half + 1) * 3072], op=ALU.add),
                      r=[pm.b, modb.b], w=[modsb.b])
            kb.dma("sp", modrow[:, :], modsb[:], r=[modsb.b], w=[modrow.b])
        barrier(kb)

    def norm_to_xsT(self, es, li, xsT, tiles, norm_name, sc_idx, sh_idx, ident):
        kb, nc = self.kb, self.nc
        modrow = self.scr['modrow']
        with ExitStack() as es2:
            G = [sb(es2, kb, f"Gbc{w_}", [128, D], F32) for w_ in range(2)]
            SH = [sb(es2, kb, f"SHbc{w_}", [128, D], F32) for w_ in range(2)]
            gn = sb(es2, kb, "gnbc", [128, D], F32)
            kb.dma("sp", gn[:], self.din[norm_name].t[li:li + 1, :].partition_broadcast(128), w=[gn.b])
            for w_ in range(2):
                kb.dma("sp", G[w_][:], modrow.t[w_:w_ + 1, sc_idx * D:(sc_idx + 1) * D].partition_broadcast(128),
                       r=[modrow.b], w=[G[w_].b])
                kb.dma("sp", SH[w_][:], modrow.t[w_:w_ + 1, sh_idx * D:(sh_idx + 1) * D].partition_broadcast(128),
                       r=[modrow.b], w=[SH[w_].b])
                kb.op("dve", lambda e: e.scalar_tensor_tensor(out=G[w_][:], in0=G[w_][:], scalar=1.0, in1=gn[:],
                                                              op0=ALU.add, op1=ALU.mult),
                      r=[gn.b], w=[G[w_].b])
            xt = [sb(es2, kb, f"xt{i}", [128, D], F32) for i in range(2)]
            junk = sb(es2, kb, "junk", [128, D], F32)
            ss = [sb(es2, kb, f"ss{i}", [128, 1], F32) for i in range(2)]
            xs = [sb(es2, kb, f"xs{i}", [128, D], F32) for i in range(2)]
            xsb = [sb(es2, kb, f"xsb{i}", [128, D], BF16) for i in range(2)]
            tp = [ps(es2, kb, f"tp{i}", [128, 8, 128], BF16) for i in range(2)]
            for i, (src, r0, c0, which) in enumerate(tiles):
                b = i % 2
                kb.dma("sp", xt[b][:], src.t[r0:r0 + 128, :], r=[src.b], w=[xt[b].b])
                kb.op("act", lambda e: e.activation(out=junk[:], in_=xt[b][:], func=AF.Square, accum_out=ss[b][:]),
                      r=[xt[b].b], w=[junk.b, ss[b].b])
                kb.op("dve", lambda e: e.tensor_scalar(out=ss[b][:], in0=ss[b][:], scalar1=1.0 / D, scalar2=NORM_EPS,
                                                       op0=ALU.mult, op1=ALU.add), w=[ss[b].b])
                kb.op("act", lambda e: e.activation(out=ss[b][:], in_=ss[b][:], func=AF.Sqrt), w=[ss[b].b])
                kb.op("dve", lambda e: e.reciprocal(out=ss[b][:], in_=ss[b][:]), w=[ss[b].b])
                kb.op("dve", lambda e: e.scalar_tensor_tensor(out=xs[b][:], in0=xt[b][:], scalar=ss[b][:, 0:1],
                                                              in1=G[which][:], op0=ALU.mult, op1=ALU.mult),
                      r=[xt[b].b, ss[b].b, G[which].b], w=[xs[b].b])
                kb.op("pool", lambda e: e.tensor_tensor(out=xsb[b][:], in0=xs[b][:], in1=SH[which][:], op=ALU.add),
                      r=[xs[b].b, SH[which].b], w=[xsb[b].b])
                for k in range(8):
                    kb.op("pe", lambda e: e.transpose(tp[b][:, k, :], xsb[b][:, k * 128:(k + 1) * 128], ident[:]),
                          r=[xsb[b].b, ident.b], w=[tp[b].b])
                kb.op("act", lambda e: e.copy(out=xsT[:, :, c0:c0 + 128], in_=tp[b][:]),
                      r=[tp[b].b], w=[xsT.b])

    def make_ident(self, es, dt, name):
        kb = self.kb
        ident = sb(es, kb, name, [128, 128], dt)
        src = self.consts['ident_f32']
        if dt == F32:
            kb.dma("sp", ident[:], src.t[:, :], w=[ident.b])
        else:
            with ExitStack() as es2:
                tmp = sb(es2, kb, name + "_tmp", [128, 128], F32)
                kb.dma("sp", tmp[:], src.t[:, :], w=[tmp.b])
                kb.op("dve", lambda e: e.tensor_copy(out=ident[:], in_=tmp[:]), r=[tmp.b], w=[ident.b])
                kb.wait_all("sp", [tmp.b])
                barrier(kb)
        return ident

    def phase_inproj(self, li, first):
        kb, nc = self.kb, self.nc
        zT = self.scr['zT']
        krope = self.scr['krope_tm']
        xsrc_l = self.din['x'] if first else self.scr['xcur']
        xsrc_c = self.din['ctx'] if first else self.scr['xcur']
        tiles = []
        for i in range(32):
            tiles.append((xsrc_l, i * 128, i * 128, 0))
        for i in range(2):
            tiles.append((xsrc_c, (i * 128) if first else (SEQ + i * 128), SEQ + i * 128, 1))
        with ExitStack() as es:
            identb = self.make_ident(es, BF16, "identb")
            xsT = sb(es, kb, "xsT", [128, 8, T], BF16)
            self.norm_to_xsT(es, li, xsT, tiles, 'mix_norm', 1, 0, identb)
            barrier(kb)
            W = sb(es, kb, "Win", [128, 8, IN_PROJ], BF16)
            wst = [sb(es, kb, f"wst{i}", [128, IN_PROJ], F32) for i in range(2)]
            for k in range(8):
                kb.dma("sp", wst[k % 2][:], self.din['w_in'].t[li, k * 128:(k + 1) * 128, :], w=[wst[k % 2].b])
                kb.op("pool", lambda e: e.tensor_copy(out=W[:, k, :], in_=wst[k % 2][:]), r=[wst[k % 2].b], w=[W.b])
            cw_rw = sb(es, kb, "cw_rw", [128, 11, 3], F32)
            cw_hy = sb(es, kb, "cw_hy", [128, 6, 3], F32)
            for j in range(3):
                kb.dma("sp", cw_rw[:, :, j], self.din['rw_conv'].t[li, j].rearrange("(c p) -> p c", p=128),
                       w=[cw_rw.b], allow_slow_non_contiguous=True)
                kb.dma("sp", cw_hy[:, :, j], self.din['hy_conv'].t[li, j].rearrange("(c p) -> p c", p=128),
                       w=[cw_hy.b], allow_slow_non_contiguous=True)
            chunks = []
            for c in range(11):
                chunks.append((c * 128, 128, cw_rw, c))
            for c in range(3):
                chunks.append((RW_PROJ + c * 128, 128, None, None))
            for c in range(6):
                chunks.append((RW_PROJ + MLA_PROJ + c * 128, 128, cw_hy, c))
            PW = T + 3
            prow = [sb(es, kb, f"prow{i}", [128, PW], F32) for i in range(2)]
            zrow = [sb(es, kb, f"zrow{i}", [128, PW], F32) for i in range(2)]
            pp = [ps(es, kb, f"pp{i}", [128, 512], F32) for i in range(4)]
            for i in range(2):
                kb.op("pool", lambda e: e.memset(prow[i][:], 0.0), w=[prow[i].b])
            npp = 0
            for ci, (c0, csz, cw, cidx) in enumerate(chunks):
                pr = prow[ci % 2]
                zr = zrow[ci % 2]
                for tb in range(9):
                    t0 = tb * 512
                    n = 512 if tb < 8 else CTX
                    p_ = pp[npp % 4]
                    npp += 1
                    for k in range(8):
                        kb.op("pe", lambda e: e.matmul(p_[:csz, :n], lhsT=W[:, k, c0:c0 + csz],
                                                       rhs=xsT[:, k, t0:t0 + n], start=(k == 0), stop=(k == 7)),
                              r=[W.b, xsT.b], w=[p_.b])
                    o0 = 1 + t0 if tb < 8 else SEQ + 2
                    if cw is None:
                        kb.op("act", lambda e: e.copy(out=zr[:csz, o0:o0 + n], in_=p_[:csz, :n]), r=[p_.b], w=[zr.b])
                    else:
                        kb.op("act", lambda e: e.copy(out=pr[:csz, o0:o0 + n], in_=p_[:csz, :n]), r=[p_.b], w=[pr.b])
                if cw is not None:
                    kb.op("act", lambda e: e.activation(out=zr[:, :], in_=pr[:, :], func=AF.Copy,
                                                        scale=cw[:, cidx, 1:2]), r=[pr.b, cw.b], w=[zr.b])
                    kb.op("dve", lambda e: e.scalar_tensor_tensor(out=zr[:, 1:PW], in0=pr[:, 0:PW - 1],
                                                                  scalar=cw[:, cidx, 0:1], in1=zr[:, 1:PW],
                                                                  op0=ALU.mult, op1=ALU.add),
                          r=[pr.b, cw.b], w=[zr.b])
                    kb.op("dve", lambda e: e.scalar_tensor_tensor(out=zr[:, 0:PW - 1], in0=pr[:, 1:PW],
                                                                  scalar=cw[:, cidx, 2:3], in1=zr[:, 0:PW - 1],
                                                                  op0=ALU.mult, op1=ALU.add),
                          r=[pr.b, cw.b], w=[zr.b])
                kb.dma("sp", zT.t[ci, :, 0:SEQ], zr[:, 1:1 + SEQ], r=[zr.b], w=[zT.b])
                kb.dma("sp", zT.t[ci, :, SEQ:T], zr[:, SEQ + 2:SEQ + 2 + CTX], r=[zr.b], w=[zT.b])
            kr = sb(es, kb, "kr_sb", [128, NT, 32], F32)
            c0 = RW_PROJ + 384
            for tt in range(NT):
                p_ = pp[npp % 4]
                npp += 1
                for k in range(8):
                    kb.op("pe", lambda e: e.matmul(p_[:, :32], lhsT=xsT[:, k, tt * 128:(tt + 1) * 128],
                                                   rhs=W[:, k, c0:c0 + 32], start=(k == 0), stop=(k == 7)),
                          r=[W.b, xsT.b], w=[p_.b])
                kb.op("act", lambda e: e.copy(out=kr[:, tt, :], in_=p_[:, :32]), r=[p_.b], w=[kr.b])
            kb.dma("sp", krope.t.rearrange("(n p) f -> p n f", p=128), kr[:], r=[kr.b], w=[krope.b])
        barrier(kb)


    def fm_to_tm(self, src, dst_ap_pnf, identf, stg, tpp, ntiles, state):
        kb = self.kb
        for n0 in range(0, ntiles, 4):
            nn = min(4, ntiles - n0)
            tp = tpp[state[0] % 2]
            state[0] += 1
            for j in range(nn):
                kb.op("pe", lambda e: e.transpose(tp[:, j, :], src[:, (n0 + j) * 128:(n0 + j + 1) * 128], identf[:]),
                      r=[src.b, identf.b], w=[tp.b])
            eng = "act" if (state[0] % 2) else "dve"
            if eng == "act":
                kb.op("act", lambda e: e.copy(out=stg[:, n0:n0 + nn, :], in_=tp[:, 0:nn, :]), r=[tp.b], w=[stg.b])
            else:
                kb.op("dve", lambda e: e.tensor_copy(out=stg[:, n0:n0 + nn, :], in_=tp[:, 0:nn, :]), r=[tp.b], w=[stg.b])

    def phase_rw_prep(self, li):
        kb, nc = self.kb, self.nc
        zT = self.scr['zT']
        S = self.scr
        with ExitStack() as es:
            identf = self.make_ident(es, F32, "identf")
            bones = sb(es, kb, "bones", [128, 128], F32)
            kb.dma("sp", bones[:], self.consts['blockones'].t[:, :], w=[bones.b])
            dup = sb(es, kb, "dup", [64, 2, 384], F32)
            aup = sb(es, kb, "aup", [64, 2, 384], F32)
            alo = sb(es, kb, "alo", [64, T], F32)
            gup = sb(es, kb, "gup", [128, 384], F32)
            kb.dma("sp", dup[:], self.din['rw_decay_up'].t[li].rearrange("d r c -> r d c"), w=[dup.b])
            kb.dma("sp", aup[:, :, :], self.din['rw_a_up'].t[li].rearrange("d r c -> r d c"), w=[aup.b])
            kb.dma("sp", gup[:], self.din['rw_gate_up'].t[li], w=[gup.b])
            d0c = sb(es, kb, "d0c", [128, 2, 3], F32)
            a0c = sb(es, kb, "a0c", [128, 2, 3], F32)
            kkc = sb(es, kb, "kkc", [128, 3], F32)
            kac = sb(es, kb, "kac", [128, 3], F32)
            omka = sb(es, kb, "omka", [128, 3], F32)
            for d in range(2):
                kb.dma("sp", d0c[:, d, :], self.din['rw_decay0'].t[li, d].rearrange("(g p) -> p g", p=128),
                       w=[d0c.b], allow_slow_non_contiguous=True)
                kb.dma("sp", a0c[:, d, :], self.din['rw_a0'].t[li, d].rearrange("(g p) -> p g", p=128),
                       w=[a0c.b], allow_slow_non_contiguous=True)
            kb.dma("sp", kkc[:], self.din['rw_k_k'].t[li].rearrange("(g p) -> p g", p=128), w=[kkc.b],
                   allow_slow_non_contiguous=True)
            kb.dma("sp", kac[:], self.din['rw_k_a'].t[li].rearrange("(g p) -> p g", p=128), w=[kac.b],
                   allow_slow_non_contiguous=True)
            kb.op("dve", lambda e: e.tensor_scalar(out=omka[:], in0=kac[:], scalar1=-1.0, scalar2=1.0,
                                                   op0=ALU.mult, op1=ALU.add), r=[kac.b], w=[omka.b])
            da = sb(es, kb, "da", [64, T], F32)
            gs = sb(es, kb, "gs", [128, T], F32)
            kg = sb(es, kb, "kg", [128, T], F32)
            t1 = sb(es, kb, "t1", [128, T], F32)
            kkg = sb(es, kb, "kkg", [128, T], F32)
            adg = sb(es, kb, "adg", [128, T], F32)
            tA = sb(es, kb, "tA", [128, T], F32)
            tB = sb(es, kb, "tB", [128, T], F32)
            stg = [sb(es, kb, f"stg{i}", [128, NT, 128], F32) for i in range(2)]
            pp = [ps(es, kb, f"pq{i}", [128, 512], F32) for i in range(4)]
            tpp = [ps(es, kb, f"tq{i}", [128, 4, 128], F32) for i in range(2)]
            st = [0]
            nst = [0]
            npp = [0]

            def to_tm(src, dst_tl, col0, extra=None):
                sg = stg[nst[0] % 2]
                nst[0] += 1
                self.fm_to_tm(src, None, identf, sg, tpp, NT, st)
                if dst_tl is not None:
                    kb.dma("sp", dst_tl.rearrange("(n p) f -> p n f", p=128)[:, :, col0:col0 + 128], sg[:],
                           r=[sg.b], w=[self._prep_buf])
                if extra is not None:
                    for (dst_ap, c0, c1) in extra:
                        kb.dma("pool", dst_ap, sg[:, :, c0:c1], r=[sg.b], w=[self._prep_buf])

            def rows_dst(ap_T_384, g, c0, c1):
                return ap_T_384.rearrange("(n p) (g c) -> p n g c", p=128, c=128)[:, :, g, c0:c1]

            def blocks(fn):
                for tb in range(9):
                    t0 = tb * 512
                    n = 512 if tb < 8 else CTX
                    p_ = pp[npp[0] % 4]
                    npp[0] += 1
                    fn(p_, t0, n)

            kb.dma("sp", da[0:64, :], zT.t[9, 0:64, :], r=[zT.b], w=[da.b])
            kb.dma("sp", alo[:, :], zT.t[9, 64:128, :], r=[zT.b], w=[alo.b])
            kb.dma("sp", gs[:], zT.t[10], r=[zT.b], w=[gs.b])
            kb.op("act", lambda e: e.activation(out=da[0:64, :], in_=da[0:64, :], func=AF.Tanh), w=[da.b])
            kb.op("act", lambda e: e.activation(out=gs[:], in_=gs[:], func=AF.Sigmoid), w=[gs.b])
            import os
            PSTOP = int(os.environ.get("PREP_STOP", "99"))
            for g in range(3):
                gc = slice(g * 128, (g + 1) * 128)
                kb.dma("sp", tA[:], zT.t[g], r=[zT.b], w=[tA.b])
                kb.dma("pool", S['rb_fm'].t[g], tA[:], r=[tA.b], w=[self._prep_buf])
                to_tm(tA, S['r_tm'].t, g * 128)
                if PSTOP == 1:
                    break
                kb.dma("sp", tB[:], zT.t[6 + g], r=[zT.b], w=[tB.b])
                to_tm(tB, S['v_tm'].t, g * 128,
                      extra=[(S['v6_tm'].t[g * 2 + par].rearrange("(n p) (gg c) -> p n gg c", p=128, c=64)[:, :, g, :],
                              par * 64, par * 64 + 64) for par in range(2)])
                def fg(p_, t0, n):
                    kb.op("pe", lambda e: e.matmul(p_[:, :n], lhsT=gup[:, gc], rhs=gs[:, t0:t0 + n], start=True, stop=True),
                          r=[gup.b, gs.b], w=[p_.b])
                    kb.op("act", lambda e: e.copy(out=tA[:, t0:t0 + n], in_=p_[:, :n]), r=[p_.b], w=[tA.b])
                blocks(fg)
                to_tm(tA, S['g_tm'].t, g * 128)
                if PSTOP == 2:
                    break
                kb.dma("sp", kg[:], zT.t[3 + g], r=[zT.b], w=[kg.b])
                to_tm(kg, S['k_tm'].t, g * 128)
                kb.op("act", lambda e: e.activation(out=t1[:], in_=kg[:], func=AF.Copy, scale=kkc[:, g:g + 1]),
                      r=[kg.b, kkc.b], w=[t1.b])
                kb.op("dve", lambda e: e.tensor_tensor(out=tB[:], in0=t1[:], in1=t1[:], op=ALU.mult), r=[t1.b], w=[tB.b])
                def fk(p_, t0, n):
                    kb.op("pe", lambda e: e.matmul(p_[:, :n], lhsT=bones[:], rhs=tB[:, t0:t0 + n], start=True, stop=True),
                          r=[bones.b, tB.b], w=[p_.b])
                    kb.op("dve", lambda e: e.tensor_scalar(out=kkg[:, t0:t0 + n], in0=p_[:, :n], scalar1=1e-12, scalar2=None,
                                                           op0=ALU.add), r=[p_.b], w=[kkg.b])
                blocks(fk)
                kb.op("act", lambda e: e.activation(out=kkg[:], in_=kkg[:], func=AF.Sqrt), w=[kkg.b])
                kb.op("dve", lambda e: e.reciprocal(out=kkg[:], in_=kkg[:]), w=[kkg.b])
                kb.op("dve", lambda e: e.tensor_tensor(out=kkg[:], in0=kkg[:], in1=t1[:], op=ALU.mult), r=[t1.b], w=[kkg.b])
                kb.dma("pool", S['kk_fm'].t[g], kkg[:], r=[kkg.b], w=[self._prep_buf])
                to_tm(kkg, None, 0, extra=[(S['kk6_tm'].t[g * 2 + par].rearrange("(n p) c -> p n c", p=128)[:, :, par * 64:par * 64 + 64],
                                            par * 64, par * 64 + 64) for par in range(2)])
                if PSTOP == 3:
                    break
                for d in range(2):
                    def fd(p_, t0, n):
                        kb.op("pe", lambda e: e.matmul(p_[:, :n], lhsT=dup[0:64, d, gc], rhs=da[0:64, t0:t0 + n],
                                                       start=True, stop=True), r=[dup.b, da.b], w=[p_.b])
                        kb.op("act", lambda e: e.activation(out=tA[:, t0:t0 + n], in_=p_[:, :n], func=AF.Sigmoid,
                                                            bias=d0c[:, d, g:g + 1]), r=[p_.b, d0c.b], w=[tA.b])
                    blocks(fd)
                    kb.op("act", lambda e: e.activation(out=tA[:], in_=tA[:], func=AF.Exp, scale=-0.6065306597),
                          w=[tA.b])
                    kb.dma("sp", S['w_fm'].t[d, g], tA[:], r=[tA.b], w=[self._prep_buf])
                    def fa(p_, t0, n):
                        kb.op("pe", lambda e: e.matmul(p_[:, :n], lhsT=aup[:, d, gc], rhs=alo[:, t0:t0 + n],
                                                       start=True, stop=True), r=[aup.b, alo.b], w=[p_.b])
                        kb.op("act", lambda e: e.activation(out=adg[:, t0:t0 + n], in_=p_[:, :n], func=AF.Sigmoid,
                                                            bias=a0c[:, d, g:g + 1]), r=[p_.b, a0c.b], w=[adg.b])
                    blocks(fa)
                    kb.op("dve", lambda e: e.tensor_scalar(out=tA[:], in0=adg[:], scalar1=kac[:, g:g + 1],
                                                           scalar2=omka[:, g:g + 1], op0=ALU.mult, op1=ALU.add),
                          r=[adg.b, kac.b, omka.b], w=[tA.b])
                    kb.op("dve", lambda e: e.tensor_tensor(out=tA[:], in0=tA[:], in1=kg[:], op=ALU.mult),
                          r=[kg.b], w=[tA.b])
                    to_tm(tA, None, 0, extra=[(S['krep6_tm'].t[d, g * 2 + par].rearrange("(n p) c -> p n c", p=128)[:, :, par * 64:par * 64 + 64],
                                               par * 64, par * 64 + 64) for par in range(2)])
                    kb.op("dve", lambda e: e.scalar_tensor_tensor(out=tB[:], in0=adg[:], scalar=-1.0, in1=kkg[:],
                                                                  op0=ALU.mult, op1=ALU.mult),
                          r=[adg.b, kkg.b], w=[tB.b])
                    to_tm(tB, None, 0, extra=[(rows_dst(S['akk6_tm'].t[d, g * 2 + par], g, par * 64, par * 64 + 64),
                                               par * 64, par * 64 + 64) for par in range(2)])
        barrier(kb)

    def scr_b(self, ap):
        return self._prep_buf

    def phase_zero_init(self):
        kb = self.kb
        with ExitStack() as es:
            z = sb(es, kb, "zeros", [128, 8192], BF16)
            kb.op("pool", lambda e: e.memset(z[:], 0.0), w=[z.b])
            for (flat, tot) in ((self.scr['akk6_tm'].t.rearrange("d r t f -> (d r t f)"), 2 * 6 * T * 384),
                                (self.scr['krep6_tm'].t.rearrange("d r t f -> (d r t f)"), 2 * 6 * T * 128),
                                (self.scr['kk6_tm'].t.rearrange("r t f -> (r t f)"), 6 * T * 128),
                                (self.scr['v6_tm'].t.rearrange("r t f -> (r t f)"), 6 * T * 192)):
                per = 128 * 8192
                off = 0
                while off < tot:
                    n = min(per, tot - off)
                    cols = n // 128
                    kb.dma("sp", flat[off:off + n].rearrange("(p c) -> p c", p=128), z[:, :cols], r=[z.b],
                           w=[self._prep_buf])
                    off += n
        barrier(kb)

    def phase_rw_scan(self, li, nsteps=None):
        kb, nc = self.kb, self.nc
        S = self.scr
        TCF, TCR = 64, 8
        import os
        PARTS = os.environ.get("SCAN_PARTS", "SCUDYO")
        total = T if nsteps is None else nsteps
        pb = self._prep_buf
        with ExitStack() as es:
            St = sb(es, kb, "St", [128, 2, 3, 64], F32, nsub=2)
            kb.op("dve", lambda e: e.memset(St[:], 0.0), w=[St.sub[0], St.sub[1]])
            Stb = sb(es, kb, "Stb", [128, 2, 3, 64], BF16, nsub=2)
            kb.op("dve", lambda e: e.memset(Stb[:], 0.0), w=[Stb.sub[0], Stb.sub[1]])
            Wt = [[sb(es, kb, f"Wt{d}{c}", [128, 3, TCF], F32) for c in range(2)] for d in range(2)]
            KK = [[sb(es, kb, f"KK{d}{c}", [128, 3, 2, TCF], BF16) for c in range(2)] for d in range(2)]
            RR = [[sb(es, kb, f"RR{d}{c}", [128, 3, 2, TCF], BF16) for c in range(2)] for d in range(2)]
            L4 = [[sb(es, kb, f"L4{d}{c}", [4, TCR, 3, 128], BF16) for c in range(2)] for d in range(2)]
            RV = [[sb(es, kb, f"RV{d}{c}", [4, TCR, 3, 64], BF16) for c in range(2)] for d in range(2)]
            YF = [sb(es, kb, f"YF{d}", [64, 3, 2, TCF], F32) for d in range(2)]
            for d in range(2):
                for c in range(2):
                    for tl in (KK[d][c], RR[d][c]):
                        kb.op("pool", lambda e: e.memset(tl[:], 0.0), w=[tl.b])
            pskb = [ps(es, kb, f"psk{d}", [128, 512], F32) for d in range(2)]
            pdb = [ps(es, kb, f"pd{d}", [128, 512], F32) for d in range(2)]
            pyb = [ps(es, kb, f"py{d}", [128, 512], F32) for d in range(2)]
            psk_v = [pskb[d][0:2, 0:192].rearrange("p (g c) -> p g c", g=3) for d in range(2)]
            pd_v = [pdb[d][:, 0:192].rearrange("p (g c) -> p g c", g=3) for d in range(2)]
            py_v = [pyb[d][0:64, 0:6 * TCF].rearrange("p (g h t) -> p g h t", g=3, h=2) for d in range(2)]
            kk_pgt = S['kk_fm'].t.rearrange("g p t -> p g t")
            r_pgt = S['rb_fm'].t.rearrange("g p t -> p g t")

            def seg_time(s, d):
                if s < CTX:
                    return SEQ + s if d == 0 else T - 1 - s
                return s - CTX if d == 0 else SEQ - 1 - (s - CTX)

            def chunk_t0(s0, n, d):
                return seg_time(s0, d) if d == 0 else seg_time(s0 + n - 1, d)

            for s in range(total):
                cf, jf = divmod(s, TCF)
                cr, jr = divmod(s, TCR)
                cbf, cbr = cf % 2, cr % 2
                if jf == 0:
                    for d in range(2):
                        t0 = chunk_t0(s, TCF, d)
                        ts = slice(t0, t0 + TCF)
                        kb.dma("sp", Wt[d][cbf][:], S['w_fm'].t[d].rearrange("g p t -> p g t")[:, :, ts], r=[pb],
                               w=[Wt[d][cbf].b])
                        for h in range(2):
                            hp = slice(h * 64, (h + 1) * 64)
                            kb.dma("sp", KK[d][cbf][hp, :, h, :], kk_pgt[hp, :, ts], r=[pb], w=[KK[d][cbf].b])
                            kb.dma("sp", RR[d][cbf][hp, :, h, :], r_pgt[hp, :, ts], r=[pb], w=[RR[d][cbf].b])
                if jr == 0:
                    for d in range(2):
                        t0 = chunk_t0(s, TCR, d)
                        ts = slice(t0, t0 + TCR)
                        kb.dma("sp", L4[d][cbr][:].rearrange("r t g c -> r t (g c)"), S['L4_tm'].t[d, :, ts, :], r=[pb], w=[L4[d][cbr].b])
                        kb.dma("sp", RV[d][cbr][2:4, :, :, :], S['V2_tm'].t[:, ts, :, :], r=[pb], w=[RV[d][cbr].b])
                lf = [jf, TCF - 1 - jf]
                lr = [jr, TCR - 1 - jr]
                for d in range(2):
                    for g in range(3):
                        if 'S' not in PARTS:
                            continue
                        kb.op("pe", lambda e: e.matmul(psk_v[d][:, g, :], lhsT=KK[d][cbf][:, g, :, lf[d]],
                                                       rhs=Stb[:, d, g, :], start=True, stop=True),
                              r=[KK[d][cbf].b, Stb.sub[d]], w=[pskb[d].b])
                for d in range(2):
                    if 'C' not in PARTS:
                        continue
                    kb.op("act", lambda e: e.copy(out=RV[d][cbr][0:2, lr[d], :, :], in_=psk_v[d][:, :, :]),
                          r=[pskb[d].b], w=[RV[d][cbr].b])
                for d in range(2):
                    for g in range(3):
                        if 'U' not in PARTS:
                            continue
                        kb.op("pe", lambda e: e.matmul(pd_v[d][:, g, :], lhsT=L4[d][cbr][:, lr[d], g, :],
                                                       rhs=RV[d][cbr][:, lr[d], g, :], start=True, stop=True),
                              r=[L4[d][cbr].b, RV[d][cbr].b], w=[pdb[d].b])
                for d in range(2):
                    for g in range(3):
                        if 'D' not in PARTS:
                            continue
                        kb.op("dve", lambda e: e.scalar_tensor_tensor(out=Stb[:, d, g, :], in0=St[:, d, g, :],
                                                                      scalar=Wt[d][cbf][:, g, lf[d]:lf[d] + 1],
                                                                      in1=pd_v[d][:, g, :], op0=ALU.mult, op1=ALU.add),
                              r=[Wt[d][cbf].b, pdb[d].b, St.sub[d]], w=[Stb.sub[d]])
                for d in range(2):
                    for g in range(3):
                        if 'D' not in PARTS:
                            continue
                        kb.op("dve", lambda e: e.scalar_tensor_tensor(out=St[:, d, g, :], in0=St[:, d, g, :],
                                                                      scalar=Wt[d][cbf][:, g, lf[d]:lf[d] + 1],
                                                                      in1=pd_v[d][:, g, :], op0=ALU.mult, op1=ALU.add),
                              r=[Wt[d][cbf].b, pdb[d].b], w=[St.sub[d]])
                for d in range(2):
                    for g in range(3):
                        if 'Y' not in PARTS:
                            continue
                        kb.op("pe", lambda e: e.matmul(py_v[d][:, g, :, lf[d]], lhsT=Stb[:, d, g, :],
                                                       rhs=RR[d][cbf][:, g, :, lf[d]], start=True, stop=True),
                              r=[RR[d][cbf].b, Stb.sub[d]], w=[pyb[d].b])
                if (jf == TCF - 1 or s == total - 1) and 'O' in PARTS:
                    s0 = cf * TCF
                    for d in range(2):
                        t0 = chunk_t0(s0, TCF, d)
                        ts = slice(t0, t0 + TCF)
                        kb.op("act", lambda e: e.copy(out=YF[d][:], in_=py_v[d]), r=[pyb[d].b], w=[YF[d].b])
                        kb.dma("sp", S['y_fm'].t[d].rearrange("g h v t -> v g h t")[:, :, :, ts].rearrange("v g h t -> v (g h) t"),
                               YF[d][:].rearrange("v g h t -> v (g h) t"), r=[YF[d].b], w=[S['y_fm'].b])
        barrier(kb)


    def phase_rw_scan2(self, li, nsteps=None):
        kb, nc = self.kb, self.nc
        S = self.scr
        TCF, TCR = 64, 8
        AHEAD = 2
        NSL = 4
        total = T if nsteps is None else nsteps
        pb = self._prep_buf
        with ExitStack() as es:
            identf = self.make_ident(es, F32, "identf_s")
            Stb = sb(es, kb, "Stb2", [128, 2, 3, 64], BF16, nsub=2)
            kb.op("dve", lambda e: e.memset(Stb[:], 0.0), w=[Stb.sub[0], Stb.sub[1]])
            Wt = [[sb(es, kb, f"Wt{d}{c}", [128, 3, TCF], F32) for c in range(2)] for d in range(2)]
            RR = [[sb(es, kb, f"RR{d}{c}", [128, 3, 2, TCF], BF16) for c in range(2)] for d in range(2)]
            KRW = [[sb(es, kb, f"KRW{d}{c}", [6, TCR, 128], BF16) for c in range(3)] for d in range(2)]
            LA = [[sb(es, kb, f"LA{d}{c}", [6, TCR, 384], BF16) for c in range(3)] for d in range(2)]
            LK = [[sb(es, kb, f"LK{d}{c}", [6, TCR, 128], BF16) for c in range(3)] for d in range(2)]
            VR = [[sb(es, kb, f"VR{d}{c}", [6, TCR, 192], BF16) for c in range(3)] for d in range(2)]
            YF = [sb(es, kb, f"YF{d}", [64, 3, 2, TCF], F32) for d in range(2)]
            Ab = sb(es, kb, "Abuf", [128, NSL * 6, 128], BF16, nsub=NSL * 2)
            for d in range(2):
                for c in range(2):
                    kb.op("pool", lambda e: e.memset(RR[d][c][:], 0.0), w=[RR[d][c].b])
            pA = [[ps(es, kb, f"pA{d}{i}", [128, 512], F32) for i in range(2)] for d in range(2)]
            pSb = [ps(es, kb, f"pS{d}", [128, 512], F32) for d in range(2)]
            pyb = [ps(es, kb, f"py{d}", [128, 512], F32) for d in range(2)]
            pA_v = [[pA[d][i][:, 0:384].rearrange("p (g c) -> p g c", g=3) for i in range(2)] for d in range(2)]
            pS_v = [pSb[d][:, 0:192].rearrange("p (g c) -> p g c", g=3) for d in range(2)]
            py_v = [pyb[d][0:64, 0:6 * TCF].rearrange("p (g h t) -> p g h t", g=3, h=2) for d in range(2)]
            r_pgt = S['rb_fm'].t.rearrange("g p t -> p g t")

            def seg_time(s, d):
                if s < CTX:
                    return SEQ + s if d == 0 else T - 1 - s
                return s - CTX if d == 0 else SEQ - 1 - (s - CTX)

            def chunk_t0(s0, n, d):
                return seg_time(s0, d) if d == 0 else seg_time(s0 + n - 1, d)

            def load_f(s):
                cbf = (s // TCF) % 2
                for d in range(2):
                    t0 = chunk_t0(s, TCF, d)
                    ts = slice(t0, t0 + TCF)
                    kb.dma("sp", Wt[d][cbf][:], S['w_fm'].t[d].rearrange("g p t -> p g t")[:, :, ts], r=[pb], w=[Wt[d][cbf].b])
                    for h in range(2):
                        hp = slice(h * 64, (h + 1) * 64)
                        kb.dma("sp", RR[d][cbf][hp, :, h, :], r_pgt[hp, :, ts], r=[pb], w=[RR[d][cbf].b])

            def load_r(s):
                cbr = (s // TCR) % 3
                for d in range(2):
                    t0 = chunk_t0(s, TCR, d)
                    ts = slice(t0, t0 + TCR)
                    kb.dma("sp", KRW[d][cbr][:], S['kk6_tm'].t[:, ts, :], r=[pb], w=[KRW[d][cbr].b])
                    kb.dma("sp", LA[d][cbr][:], S['akk6_tm'].t[d, :, ts, :], r=[pb], w=[LA[d][cbr].b])
                    kb.dma("sp", LK[d][cbr][:], S['krep6_tm'].t[d, :, ts, :], r=[pb], w=[LK[d][cbr].b])
                    kb.dma("sp", VR[d][cbr][:], S['v6_tm'].t[:, ts, :], r=[pb], w=[VR[d][cbr].b])

            def idx(s):
                jf = s % TCF
                jr = s % TCR
                return (s // TCF) % 2, (s // TCR) % 3, [jf, TCF - 1 - jf], [jr, TCR - 1 - jr]

            def build_A(s):
                if s >= total:
                    return
                if s % TCF == 0:
                    load_f(s)
                if s % TCR == 0:
                    load_r(s)
                cbf, cbr, lf, lr = idx(s)
                sl = s % NSL
                for d in range(2):
                    pa = pA[d][s % 2]
                    kb.op("pe", lambda e: e.matmul(pa[:, 0:384], lhsT=KRW[d][cbr][:, lr[d], :],
                                                   rhs=LA[d][cbr][:, lr[d], :], start=True, stop=True),
                          r=[KRW[d][cbr].b, LA[d][cbr].b], w=[pa.b])
                    for g in range(3):
                        kb.op("dve", lambda e: e.scalar_tensor_tensor(out=Ab[:, sl * 6 + d * 3 + g, :], in0=identf[:],
                                                                      scalar=Wt[d][cbf][:, g, lf[d]:lf[d] + 1],
                                                                      in1=pA_v[d][s % 2][:, g, :], op0=ALU.mult, op1=ALU.add),
                              r=[identf.b, Wt[d][cbf].b, pa.b], w=[Ab.sub[sl * 2 + d]])

            for s0 in range(AHEAD):
                build_A(s0)
            for s in range(total):
                build_A(s + AHEAD)
                cbf, cbr, lf, lr = idx(s)
                sl = s % NSL
                for d in range(2):
                    kb.op("pe", lambda e: e.matmul(pSb[d][:, 0:192], lhsT=LK[d][cbr][:, lr[d], :], rhs=VR[d][cbr][:, lr[d], :],
                                                   start=True, stop=False),
                          r=[LK[d][cbr].b, VR[d][cbr].b], w=[pSb[d].b])
                    for g in range(3):
                        kb.op("pe", lambda e: e.matmul(pS_v[d][:, g, :], lhsT=Ab[:, sl * 6 + d * 3 + g, :], rhs=Stb[:, d, g, :],
                                                       start=False, stop=(g == 2)),
                              r=[Ab.sub[sl * 2 + d], Stb.sub[d]], w=[pSb[d].b])
                for d in range(2):
                    kb.op("act", lambda e: e.copy(out=Stb[:, d, :, :], in_=pS_v[d]), r=[pSb[d].b], w=[Stb.sub[d]])
                for d in range(2):
                    for g in range(3):
                        kb.op("pe", lambda e: e.matmul(py_v[d][:, g, :, lf[d]], lhsT=Stb[:, d, g, :],
                                                       rhs=RR[d][cbf][:, g, :, lf[d]], start=True, stop=True),
                              r=[RR[d][cbf].b, Stb.sub[d]], w=[pyb[d].b])
                if (s % TCF) == TCF - 1 or s == total - 1:
                    s0 = (s // TCF) * TCF
                    for d in range(2):
                        t0 = chunk_t0(s0, TCF, d)
                        ts = slice(t0, t0 + TCF)
                        kb.op("act", lambda e: e.copy(out=YF[d][:], in_=py_v[d]), r=[pyb[d].b], w=[YF[d].b])
                        kb.dma("sp", S['y_fm'].t[d].rearrange("g h v t -> v g h t")[:, :, :, ts].rearrange("v g h t -> v (g h) t"),
                               YF[d][:].rearrange("v g h t -> v (g h) t"), r=[YF[d].b], w=[S['y_fm'].b])
        barrier(kb)

    def phase_rw_readout(self, li, need_ctx):
        kb, nc = self.kb, self.nc
        S = self.scr
        pb = self._prep_buf
        ntile = NT if need_ctx else SEQ // 128
        with ExitStack() as es:
            identf = self.make_ident(es, F32, "identf2")
            gng = sb(es, kb, "gng", [128, 384], F32)
            gnb = sb(es, kb, "gnb", [128, 384], F32)
            rkb = sb(es, kb, "rkb", [128, 384], F32)
            kb.dma("sp", gng[:], self.din['rw_gn_g'].t[li:li + 1, :].partition_broadcast(128), w=[gng.b])
            kb.dma("sp", gnb[:], self.din['rw_gn_b'].t[li:li + 1, :].partition_broadcast(128), w=[gnb.b])
            kb.dma("sp", rkb[:], self.din['rw_r_k'].t[li:li + 1, :].partition_broadcast(128), w=[rkb.b])
            yfa = [sb(es, kb, f"yfa{i}", [64, 6, 128], F32) for i in range(2)]
            yfb = [sb(es, kb, f"yfb{i}", [64, 6, 128], F32) for i in range(2)]
            rt = [sb(es, kb, f"rt{i}", [128, 384], F32) for i in range(2)]
            kt = [sb(es, kb, f"kt{i}", [128, 384], F32) for i in range(2)]
            vt = [sb(es, kb, f"vt{i}", [128, 384], F32) for i in range(2)]
            gt = [sb(es, kb, f"gt{i}", [128, 384], F32) for i in range(2)]
            yc = sb(es, kb, "yc", [128, 6, 64], F32)
            sq = sb(es, kb, "sqr", [128, 6, 64], F32)
            mu = sb(es, kb, "mu", [128, 6], F32)
            var = sb(es, kb, "var", [128, 6], F32)
            bon = sb(es, kb, "bon", [128, 6], F32)
            ot = [sb(es, kb, f"ot{i}", [128, 384], F32) for i in range(2)]
            pyt = [ps(es, kb, f"pyt{i}", [128, 512], F32) for i in range(2)]
            mix = S['mix_tm']
            for tt in range(ntile):
                b = tt % 2
                ts = slice(tt * 128, (tt + 1) * 128)
                kb.dma("sp", yfa[b][:], S['y_fm'].t[0].rearrange("g h v t -> v (g h) t")[:, :, ts], r=[S['y_fm'].b], w=[yfa[b].b])
                kb.dma("sp", yfb[b][:], S['y_fm'].t[1].rearrange("g h v t -> v (g h) t")[:, :, ts], r=[S['y_fm'].b], w=[yfb[b].b])
                kb.dma("sp", rt[b][:], S['r_tm'].t[ts, :], r=[pb], w=[rt[b].b])
                kb.dma("sp", kt[b][:], S['k_tm'].t[ts, :], r=[pb], w=[kt[b].b])
                kb.dma("sp", vt[b][:], S['v_tm'].t[ts, :], r=[pb], w=[vt[b].b])
                kb.dma("sp", gt[b][:], S['g_tm'].t[ts, :], r=[pb], w=[gt[b].b])
                kb.op("pool", lambda e: e.tensor_tensor(out=yfa[b][:], in0=yfa[b][:], in1=yfb[b][:], op=ALU.add),
                      r=[yfb[b].b], w=[yfa[b].b])
                pv = pyt[b][:, 0:384].rearrange("p (h c) -> p h c", h=6)
                for h in range(6):
                    kb.op("pe", lambda e: e.transpose(pv[:, h, :], yfa[b][:, h, :], identf[0:64, 0:64]),
                          r=[yfa[b].b, identf.b], w=[pyt[b].b])
                kb.op("dve", lambda e: e.tensor_reduce(out=mu[:], in_=pv, axis=AX.X, op=ALU.add), r=[pyt[b].b], w=[mu.b])
                kb.op("dve", lambda e: e.tensor_scalar(out=mu[:], in0=mu[:], scalar1=-1.0 / 64, scalar2=None, op0=ALU.mult),
                      w=[mu.b])
                kb.op("dve", lambda e: e.tensor_tensor(out=yc[:], in0=pv, in1=mu[:].unsqueeze(2).to_broadcast([128, 6, 64]),
                                                       op=ALU.add), r=[pyt[b].b, mu.b], w=[yc.b])
                kb.op("pool", lambda e: e.tensor_tensor(out=sq[:], in0=yc[:], in1=yc[:], op=ALU.mult), r=[yc.b], w=[sq.b])
                kb.op("dve", lambda e: e.tensor_reduce(out=var[:], in_=sq[:], axis=AX.X, op=ALU.add), r=[sq.b], w=[var.b])
                kb.op("dve", lambda e: e.tensor_scalar(out=var[:], in0=var[:], scalar1=1.0 / 64, scalar2=64e-5,
                                                       op0=ALU.mult, op1=ALU.add), w=[var.b])
                kb.op("act", lambda e: e.activation(out=var[:], in_=var[:], func=AF.Sqrt), w=[var.b])
                kb.op("dve", lambda e: e.reciprocal(out=var[:], in_=var[:]), w=[var.b])
                kb.op("dve", lambda e: e.tensor_tensor(out=yc[:], in0=yc[:], in1=var[:].unsqueeze(2).to_broadcast([128, 6, 64]),
                                                       op=ALU.mult), r=[var.b], w=[yc.b])
                ycf = yc[:].rearrange("p h c -> p (h c)")
                kb.op("dve", lambda e: e.tensor_tensor(out=ycf, in0=ycf, in1=gng[:], op=ALU.mult), r=[gng.b], w=[yc.b])
                kb.op("dve", lambda e: e.tensor_tensor(out=ycf, in0=ycf, in1=gnb[:], op=ALU.add), r=[gnb.b], w=[yc.b])
                kb.op("pool", lambda e: e.tensor_tensor(out=rt[b][:], in0=rt[b][:], in1=kt[b][:], op=ALU.mult),
                      r=[kt[b].b], w=[rt[b].b])
                kb.op("pool", lambda e: e.tensor_tensor(out=rt[b][:], in0=rt[b][:], in1=rkb[:], op=ALU.mult),
                      r=[rkb.b], w=[rt[b].b])
                kb.op("dve", lambda e: e.tensor_reduce(out=bon[:], in_=rt[b][:].rearrange("p (h c) -> p h c", h=6),
                                                       axis=AX.X, op=ALU.add), r=[rt[b].b], w=[bon.b])
                v3 = vt[b][:].rearrange("p (h c) -> p h c", h=6)
                kb.op("dve", lambda e: e.tensor_tensor(out=v3, in0=v3, in1=bon[:].unsqueeze(2).to_broadcast([128, 6, 64]),
                                                       op=ALU.mult), r=[bon.b], w=[vt[b].b])
                kb.op("dve", lambda e: e.tensor_tensor(out=ot[b][:], in0=ycf, in1=vt[b][:], op=ALU.add),
                      r=[yc.b, vt[b].b], w=[ot[b].b])
                kb.op("dve", lambda e: e.tensor_tensor(out=ot[b][:], in0=ot[b][:], in1=gt[b][:], op=ALU.mult),
                      r=[gt[b].b], w=[ot[b].b])
                kb.dma("sp", mix.t[ts, 0:384], ot[b][:], r=[ot[b].b], w=[mix.b])
        barrier(kb)

    def phase_outproj(self, li, first, need_ctx, final_out=None):
        kb, nc = self.kb, self.nc
        S = self.scr
        mix = S['mix_tm']
        modrow = S['modrow']
        ntile = NT if need_ctx else SEQ // 128
        with ExitStack() as es:
            identb = self.make_ident(es, BF16, "identb2")
            Wo = sb(es, kb, "Wo", [128, 8, D], BF16)
            wst = [sb(es, kb, f"wost{i}", [128, D], F32) for i in range(2)]
            for k in range(8):
                kb.dma("sp", wst[k % 2][:], self.din['w_out'].t[li, k * 128:(k + 1) * 128, :], w=[wst[k % 2].b])
                kb.op("pool", lambda e: e.tensor_copy(out=Wo[:, k, :], in_=wst[k % 2][:]), r=[wst[k % 2].b], w=[Wo.b])
            gm = [sb(es, kb, f"gm{w_}", [128, D], F32) for w_ in range(2)]
            for w_ in range(2):
                kb.dma("sp", gm[w_][:], modrow.t[w_:w_ + 1, 2 * D:3 * D].partition_broadcast(128), r=[modrow.b], w=[gm[w_].b])
            mt = [sb(es, kb, f"mt{i}", [128, D], F32) for i in range(2)]
            mtb = [sb(es, kb, f"mtb{i}", [128, D], BF16) for i in range(2)]
            mT = [sb(es, kb, f"mT{i}", [128, 8, 128], BF16) for i in range(2)]
            xt = [sb(es, kb, f"xo{i}", [128, D], F32) for i in range(2)]
            tmp = [sb(es, kb, f"xtmp{i}", [128, D], F32) for i in range(2)]
            tp = [ps(es, kb, f"otp{i}", [128, 8, 128], BF16) for i in range(2)]
            po = [ps(es, kb, f"po{i}", [128, 512], F32) for i in range(4)]
            for tt in range(ntile):
                b = tt % 2
                ts = slice(tt * 128, (tt + 1) * 128)
                which = 0 if tt < 32 else 1
                if first:
                    xsrc = self.din['x'] if tt < 32 else self.din['ctx']
                    xs_ap = xsrc.t[ts, :] if tt < 32 else xsrc.t[(tt - 32) * 128:(tt - 31) * 128, :]
                else:
                    xsrc = S['xcur']
                    xs_ap = xsrc.t[ts, :]
                kb.dma("sp", mt[b][:], mix.t[ts, :], r=[mix.b], w=[mt[b].b])
                kb.dma("sp", xt[b][:], xs_ap, r=[xsrc.b], w=[xt[b].b])
                kb.op("pool", lambda e: e.tensor_copy(out=mtb[b][:], in_=mt[b][:]), r=[mt[b].b], w=[mtb[b].b])
                for k in range(8):
                    kb.op("pe", lambda e: e.transpose(tp[b][:, k, :], mtb[b][:, k * 128:(k + 1) * 128], identb[:]),
                          r=[mtb[b].b, identb.b], w=[tp[b].b])
                kb.op("act", lambda e: e.copy(out=mT[b][:], in_=tp[b][:]), r=[tp[b].b], w=[mT[b].b])
                for n in range(2):
                    p_ = po[(2 * tt + n) % 4]
                    for k in range(8):
                        kb.op("pe", lambda e: e.matmul(p_[:, :], lhsT=mT[b][:, k, :], rhs=Wo[:, k, n * 512:(n + 1) * 512],
                                                       start=(k == 0), stop=(k == 7)), r=[mT[b].b, Wo.b], w=[p_.b])
                    kb.op("dve", lambda e: e.tensor_tensor(out=tmp[b][:, n * 512:(n + 1) * 512], in0=p_[:, :],
                                                           in1=gm[which][:, n * 512:(n + 1) * 512], op=ALU.mult),
                          r=[p_.b, gm[which].b], w=[tmp[b].b])
                kb.op("pool", lambda e: e.tensor_tensor(out=xt[b][:], in0=xt[b][:], in1=tmp[b][:], op=ALU.add),
                      r=[tmp[b].b], w=[xt[b].b])
                if final_out is not None and tt < 32:
                    kb.dma("sp", final_out.t[ts, :], xt[b][:], r=[xt[b].b], w=[final_out.b])
                else:
                    kb.dma("sp", S['xcur'].t[ts, :], xt[b][:], r=[xt[b].b], w=[S['xcur'].b])
        barrier(kb)

    def phase_mla_prep(self, li):
        kb, nc = self.kb, self.nc
        S = self.scr
        zT = S['zT']
        inv96 = 1.0 / math.sqrt(96.0)
        with ExitStack() as es:
            identb = self.make_ident(es, BF16, "identb3")
            ones = sb(es, kb, "ones_c", [128, 1], F32)
            kb.op("pool", lambda e: e.memset(ones[:], 1.0), w=[ones.b])
            wq_st = sb(es, kb, "wq_st", [128, 2, 576], F32)
            wkv_st = sb(es, kb, "wkv_st", [128, 768], F32)
            qn = sb(es, kb, "qn_c", [128, 2], F32)
            kvn = sb(es, kb, "kvn_c", [128, 1], F32)
            Wq = sb(es, kb, "Wq", [128, 2, 576], BF16)
            Wkv = sb(es, kb, "Wkv", [128, 768], BF16)
            kb.dma("sp", wq_st[:], self.din['mla_w_uq'].t[li].rearrange("(k p) n -> p k n", p=128), w=[wq_st.b])
            kb.dma("sp", wkv_st[:], self.din['mla_w_ukv'].t[li], w=[wkv_st.b])
            kb.dma("sp", qn[:], self.din['mla_q_norm'].t[li].rearrange("(k p) -> p k", p=128), w=[qn.b],
                   allow_slow_non_contiguous=True)
            kb.dma("sp", kvn[:], self.din['mla_kv_norm'].t[li].rearrange("(k p) -> p k", p=128), w=[kvn.b],
                   allow_slow_non_contiguous=True)
            for k in range(2):
                kb.op("dve", lambda e: e.tensor_scalar(out=Wq[:, k, :], in0=wq_st[:, k, :], scalar1=qn[:, k:k + 1],
                                                       scalar2=None, op0=ALU.mult), r=[wq_st.b, qn.b], w=[Wq.b])
            kb.op("dve", lambda e: e.tensor_scalar(out=Wkv[:], in0=wkv_st[:], scalar1=kvn[:, 0:1], scalar2=None,
                                                   op0=ALU.mult), r=[wkv_st.b, kvn.b], w=[Wkv.b])
            qg = sb(es, kb, "qg_bc", [128, 96], F32)
            kg_ = sb(es, kb, "kg_bc", [128, 96], F32)
            kb.dma("sp", qg[:], self.din['mla_q_gain'].t[li:li + 1, :].partition_broadcast(128), w=[qg.b])
            kb.dma("sp", kg_[:], self.din['mla_k_gain'].t[li:li + 1, :].partition_broadcast(128), w=[kg_.b])
            kb.op("dve", lambda e: e.tensor_scalar(out=qg[:], in0=qg[:], scalar1=inv96, scalar2=None, op0=ALU.mult),
                  w=[qg.b])
            cq = sb(es, kb, "cq", [128, 2, T], F32)
            ckv = sb(es, kb, "ckv", [128, T], F32)
            cqb = sb(es, kb, "cqb", [128, 2, T], BF16)
            ckvb = sb(es, kb, "ckvb", [128, T], BF16)
            for k in range(2):
                kb.dma("sp", cq[:, k, :], zT.t[11 + k], r=[zT.b], w=[cq.b])
            kb.dma("sp", ckv[:], zT.t[13], r=[zT.b], w=[ckv.b])
            kb.op("pool", lambda e: e.tensor_copy(out=cqb[:], in_=cq[:]), r=[cq.b], w=[cqb.b])
            kb.op("pool", lambda e: e.tensor_copy(out=ckvb[:], in_=ckv[:]), r=[ckv.b], w=[ckvb.b])
            kb.op("act", lambda e: e.activation(out=cq[:], in_=cq[:], func=AF.Square), w=[cq.b])
            kb.op("act", lambda e: e.activation(out=ckv[:], in_=ckv[:], func=AF.Square), w=[ckv.b])
            pq = [ps(es, kb, f"mq{i}", [128, 512], F32) for i in range(2)]
            pkv = [ps(es, kb, f"mkv{i}", [128, 512], F32) for i in range(2)]
            pss = ps(es, kb, "mss", [128, 512], F32)
            ptr = [ps(es, kb, f"mtr{i}", [128, 1024], BF16) for i in range(2)]
            rs = sb(es, kb, "m_rs", [128, 2], F32)
            q = sb(es, kb, "m_q", [128, 6, 96], F32)
            kk_ = sb(es, kb, "m_k", [128, 6, 96], F32)
            kvt = sb(es, kb, "m_kv", [128, 6, 128], F32)
            sq = sb(es, kb, "m_sq", [128, 6, 96], F32)
            hs = sb(es, kb, "m_hs", [128, 6], F32)
            krt = sb(es, kb, "m_kr", [128, 32], F32)
            cs = sb(es, kb, "m_cs", [128, 32], F32)
            r1 = sb(es, kb, "m_r1", [128, 6, 2, 8], F32)
            r2 = sb(es, kb, "m_r2", [128, 6, 2, 8], F32)
            r3 = sb(es, kb, "m_r3", [128, 6, 2, 8], F32)
            qb = sb(es, kb, "m_qb", [128, 6, 96], BF16)
            kbb = sb(es, kb, "m_kb", [128, 6, 96], BF16)
            v1 = [sb(es, kb, f"m_v1{i}", [128, 6, 65], BF16) for i in range(2)]
            qTs = [sb(es, kb, f"m_qT{i}", [96, 6, 128], BF16) for i in range(2)]
            kTs = [sb(es, kb, f"m_kT{i}", [96, 6, 128], BF16) for i in range(2)]
            for i in range(2):
                kb.op("pool", lambda e: e.memset(v1[i][:], 1.0), w=[v1[i].b])

            def head_norm(x, gain):
                kb.op("pool", lambda e: e.tensor_tensor(out=sq[:], in0=x[:], in1=x[:], op=ALU.mult), r=[x.b], w=[sq.b])
                kb.op("dve", lambda e: e.tensor_reduce(out=hs[:], in_=sq[:], axis=AX.X, op=ALU.add), r=[sq.b], w=[hs.b])
                kb.op("dve", lambda e: e.tensor_scalar(out=hs[:], in0=hs[:], scalar1=1.0 / 96, scalar2=NORM_EPS,
                                                       op0=ALU.mult, op1=ALU.add), w=[hs.b])
                kb.op("act", lambda e: e.activation(out=hs[:], in_=hs[:], func=AF.Sqrt), w=[hs.b])
                kb.op("dve", lambda e: e.reciprocal(out=hs[:], in_=hs[:]), w=[hs.b])
                kb.op("dve", lambda e: e.tensor_tensor(out=x[:], in0=x[:], in1=hs[:].unsqueeze(2).to_broadcast([128, 6, 96]),
                                                       op=ALU.mult), r=[hs.b], w=[x.b])
                kb.op("dve", lambda e: e.tensor_tensor(out=x[:], in0=x[:], in1=gain[:].unsqueeze(1).to_broadcast([128, 6, 96]),
                                                       op=ALU.mult), r=[gain.b], w=[x.b])

            def rope(x):
                xr = x[:, :, 64:96].rearrange("p h (a f e) -> p h a f e", a=2, f=2)
                x1 = xr[:, :, :, 0, :]
                x2 = xr[:, :, :, 1, :]
                c_ = cs[:, 0:16].rearrange("p (a e) -> p a e", a=2).unsqueeze(1).to_broadcast([128, 6, 2, 8])
                s_ = cs[:, 16:32].rearrange("p (a e) -> p a e", a=2).unsqueeze(1).to_broadcast([128, 6, 2, 8])
                kb.op("dve", lambda e: e.tensor_tensor(out=r1[:], in0=x1, in1=c_, op=ALU.mult), r=[x.b, cs.b], w=[r1.b])
                kb.op("dve", lambda e: e.tensor_tensor(out=r2[:], in0=x2, in1=s_, op=ALU.mult), r=[x.b, cs.b], w=[r2.b])
                kb.op("dve", lambda e: e.tensor_tensor(out=r1[:], in0=r1[:], in1=r2[:], op=ALU.subtract), r=[r2.b], w=[r1.b])
                kb.op("dve", lambda e: e.tensor_tensor(out=r2[:], in0=x2, in1=c_, op=ALU.mult), r=[x.b, cs.b], w=[r2.b])
                kb.op("dve", lambda e: e.tensor_tensor(out=r3[:], in0=x1, in1=s_, op=ALU.mult), r=[x.b, cs.b], w=[r3.b])
                kb.op("dve", lambda e: e.tensor_tensor(out=x2, in0=r2[:], in1=r3[:], op=ALU.add), r=[r2.b, r3.b], w=[x.b])
                kb.op("dve", lambda e: e.tensor_copy(out=x1, in_=r1[:]), r=[r1.b], w=[x.b])

            for tt in range(NT):
                b = tt % 2
                ts = slice(tt * 128, (tt + 1) * 128)
                kb.dma("sp", krt[:], S['krope_tm'].t[ts, :], r=[S['krope_tm'].b], w=[krt.b])
                if tt < 32:
                    kb.dma("sp", cs[:], self.consts['rope_cs'].t[ts, :], w=[cs.b])
                for k in range(2):
                    kb.op("pe", lambda e: e.matmul(pss[:, 0:1], lhsT=cq[:, k, ts], rhs=ones[:, 0:1], start=(k == 0),
                                                   stop=(k == 1)), r=[cq.b, ones.b], w=[pss.b])
                kb.op("pe", lambda e: e.matmul(pss[:, 1:2], lhsT=ckv[:, ts], rhs=ones[:, 0:1], start=True, stop=True),
                      r=[ckv.b, ones.b], w=[pss.b])
                kb.op("dve", lambda e: e.tensor_scalar(out=rs[:, 0:1], in0=pss[:, 0:1], scalar1=1.0 / 256, scalar2=NORM_EPS,
                                                       op0=ALU.mult, op1=ALU.add), r=[pss.b], w=[rs.b])
                kb.op("dve", lambda e: e.tensor_scalar(out=rs[:, 1:2], in0=pss[:, 1:2], scalar1=1.0 / 128, scalar2=NORM_EPS,
                                                       op0=ALU.mult, op1=ALU.add), r=[pss.b], w=[rs.b])
                kb.op("act", lambda e: e.activation(out=rs[:], in_=rs[:], func=AF.Sqrt), w=[rs.b])
                kb.op("dve", lambda e: e.reciprocal(out=rs[:], in_=rs[:]), w=[rs.b])
                for n in range(2):
                    for k in range(2):
                        kb.op("pe", lambda e: e.matmul(pq[n][:, 0:288], lhsT=cqb[:, k, ts], rhs=Wq[:, k, n * 288:(n + 1) * 288],
                                                       start=(k == 0), stop=(k == 1)), r=[cqb.b, Wq.b], w=[pq[n].b])
                    kb.op("act", lambda e: e.activation(out=q[:, 3 * n:3 * n + 3, :].rearrange("p h c -> p (h c)"),
                                                        in_=pq[n][:, 0:288], func=AF.Copy, scale=rs[:, 0:1]),
                          r=[pq[n].b, rs.b], w=[q.b])
                for n in range(2):
                    kb.op("pe", lambda e: e.matmul(pkv[n][:, 0:384], lhsT=ckvb[:, ts], rhs=Wkv[:, n * 384:(n + 1) * 384],
                                                   start=True, stop=True), r=[ckvb.b, Wkv.b], w=[pkv[n].b])
                    kb.op("act", lambda e: e.activation(out=kvt[:, 3 * n:3 * n + 3, :].rearrange("p h c -> p (h c)"),
                                                        in_=pkv[n][:, 0:384], func=AF.Copy, scale=rs[:, 1:2]),
                          r=[pkv[n].b, rs.b], w=[kvt.b])
                kb.op("pool", lambda e: e.tensor_copy(out=kk_[:, :, 0:64], in_=kvt[:, :, 0:64]), r=[kvt.b], w=[kk_.b])
                kb.op("pool", lambda e: e.tensor_copy(out=kk_[:, :, 64:96],
                                                      in_=krt[:].unsqueeze(1).to_broadcast([128, 6, 32])),
                      r=[krt.b], w=[kk_.b])
                kb.op("pool", lambda e: e.tensor_copy(out=v1[b][:, :, 0:64], in_=kvt[:, :, 64:128]), r=[kvt.b], w=[v1[b].b])
                head_norm(q, qg)
                head_norm(kk_, kg_)
                if tt < 32:
                    rope(q)
                    rope(kk_)
                kb.op("pool", lambda e: e.tensor_copy(out=qb[:], in_=q[:]), r=[q.b], w=[qb.b])
                kb.op("pool", lambda e: e.tensor_copy(out=kbb[:], in_=kk_[:]), r=[kk_.b], w=[kbb.b])
                pt = ptr[b]
                for h in range(6):
                    kb.op("pe", lambda e: e.transpose(pt[0:96, h * 128:(h + 1) * 128], qb[:, h, :], identb[:]),
                          r=[qb.b, identb.b], w=[pt.b])
                kb.op("act", lambda e: e.copy(out=qTs[b][:].rearrange("p h t -> p (h t)"), in_=pt[0:96, 0:768]),
                      r=[pt.b], w=[qTs[b].b])
                for h in range(6):
                    kb.op("pe", lambda e: e.transpose(pt[0:96, h * 128:(h + 1) * 128], kbb[:, h, :], identb[:]),
                          r=[kbb.b, identb.b], w=[pt.b])
                kb.op("dve", lambda e: e.tensor_copy(out=kTs[b][:].rearrange("p h t -> p (h t)"), in_=pt[0:96, 0:768]),
                      r=[pt.b], w=[kTs[b].b])
                kb.dma("sp", S['qT'].t[:, :, ts], qTs[b][:], r=[qTs[b].b], w=[S['qT'].b])
                kb.dma("sp", S['kT'].t[:, :, ts], kTs[b][:], r=[kTs[b].b], w=[S['kT'].b])
                kb.dma("sp", S['V1'].t[ts, :, :], v1[b][:], r=[v1[b].b], w=[S['V1'].b])
        barrier(kb)

    def phase_mla_attn(self, li, need_ctx):
        kb, nc = self.kb, self.nc
        S = self.scr
        mix = S['mix_tm']
        with ExitStack() as es:
            kT = sb(es, kb, "a_kT", [96, 6, T], BF16)
            V1 = sb(es, kb, "a_V1", [128, NT, 6 * 65], BF16)
            kb.dma("sp", kT[:], S['kT'].t, r=[S['kT'].b], w=[kT.b])
            kb.dma("sp", V1[:], S['V1'].t.rearrange("(n p) h c -> p n (h c)", p=128), r=[S['V1'].b], w=[V1.b])
            qT = [sb(es, kb, f"a_qT{i}", [96, 6, 128], BF16) for i in range(2)]
            Pt = [sb(es, kb, f"a_P{i}", [128, 4, 128], BF16) for i in range(3)]
            psc = [ps(es, kb, f"a_sc{i}", [128, 512], F32) for i in range(3)]
            pov = [ps(es, kb, f"a_po{i}", [128, 512], F32) for i in range(2)]
            ot = [sb(es, kb, f"a_ot{i}", [128, 6, 64], F32) for i in range(2)]
            rcp = sb(es, kb, "a_rcp", [128, 1], F32)
            nsc = 0
            npo = 0
            qtiles = list(range(NT if need_ctx else 32))
            for qi, qt in enumerate(qtiles):
                b = qi % 2
                qs = slice(qt * 128, (qt + 1) * 128)
                kb.dma("sp", qT[b][:], S['qT'].t[:, :, qs], r=[S['qT'].b], w=[qT[b].b])
                ktiles = list(range(NT)) if qt < 32 else [32, 33]
                for h in range(6):
                    po = pov[npo % 2]
                    npo += 1
                    for g0 in range(0, len(ktiles), 4):
                        grp = ktiles[g0:g0 + 4]
                        sc = psc[nsc % 3]
                        P_ = Pt[nsc % 3]
                        nsc += 1
                        for j, kt_ in enumerate(grp):
                            kb.op("pe", lambda e: e.matmul(sc[:, j * 128:(j + 1) * 128], lhsT=kT[:, h, kt_ * 128:(kt_ + 1) * 128],
                                                           rhs=qT[b][:, h, :], start=True, stop=True),
                                  r=[kT.b, qT[b].b], w=[sc.b])
                        n = len(grp)
                        kb.op("act", lambda e: e.activation(out=P_[:, 0:n, :].rearrange("p j q -> p (j q)"),
                                                            in_=sc[:, 0:n * 128], func=AF.Exp), r=[sc.b], w=[P_.b])
                        for j, kt_ in enumerate(grp):
                            first = (g0 == 0 and j == 0)
                            last = (g0 + j == len(ktiles) - 1)
                            kb.op("pe", lambda e: e.matmul(po[:, 0:65], lhsT=P_[:, j, :], rhs=V1[:, kt_, h * 65:(h + 1) * 65],
                                                           start=first, stop=last), r=[P_.b, V1.b], w=[po.b])
                    kb.op("dve", lambda e: e.reciprocal(out=rcp[:], in_=po[:, 64:65]), r=[po.b], w=[rcp.b])
                    kb.op("dve", lambda e: e.tensor_scalar(out=ot[b][:, h, :], in0=po[:, 0:64], scalar1=rcp[:, 0:1],
                                                           scalar2=None, op0=ALU.mult), r=[po.b, rcp.b], w=[ot[b].b])
                kb.dma("sp", mix.t[qs, 384:768], ot[b][:].rearrange("p h c -> p (h c)"), r=[ot[b].b], w=[mix.b])
        barrier(kb)

    def phase_hy_filter(self, li, L, Hd, suffix):
        kb, nc = self.kb, self.nc
        TWO_PI = 2.0 * math.pi
        C = self.consts
        with ExitStack() as es:
            w1 = sb(es, kb, "hw1", [33, 64], F32)
            w2 = sb(es, kb, "hw2", [64, 64], F32)
            w3 = sb(es, kb, "hw3", [64, 1024], F32)
            b1 = sb(es, kb, "hb1", [64, 1], F32)
            f1 = sb(es, kb, "hf1", [64, 1], F32)
            b2 = sb(es, kb, "hb2", [64, 1], F32)
            f2 = sb(es, kb, "hf2", [64, 1], F32)
            b3 = sb(es, kb, "hb3", [128, 8], F32)
            nrate = sb(es, kb, "hnrate", [128, 2], F32)
            kb.dma("sp", w1[:], self.din['hy_w1'].t[li], w=[w1.b])
            kb.dma("sp", w2[:], self.din['hy_w2'].t[li], w=[w2.b])
            kb.dma("sp", w3[:], self.din['hy_w3'].t[li], w=[w3.b])
            for (tl, nm) in ((b1, 'hy_b1'), (f1, 'hy_freq1'), (b2, 'hy_b2'), (f2, 'hy_freq2')):
                kb.dma("sp", tl[:], self.din[nm].t[li].rearrange("(p o) -> p o", o=1), w=[tl.b])
            kb.dma("sp", b3[:], self.din['hy_b3'].t[li].rearrange("(k p) -> p k", p=128), w=[b3.b],
                   allow_slow_non_contiguous=True)
            kb.dma("sp", nrate[:], C['hy_nrate'].t.rearrange("(k p) -> p k", p=128), w=[nrate.b],
                   allow_slow_non_contiguous=True)
            zp = sb(es, kb, "hzp", [33, L], F32)
            tnb = sb(es, kb, "htn", [128, L], F32)
            h1 = sb(es, kb, "hh1", [64, L], F32)
            h2 = sb(es, kb, "hh2", [64, 2, L], F32)
            arg = sb(es, kb, "harg", [64, 512], F32)
            ki = sb(es, kb, "hki", [64, 512], I32)
            kf = sb(es, kb, "hkf", [64, 512], F32)
            msk = sb(es, kb, "hmsk", [64, 512], F32)
            dec = sb(es, kb, "hdec", [128, L], F32)
            H = sb(es, kb, "hH", [128, 8192], F32)
            Hb = sb(es, kb, "hHb", [128, 8192], BF16)
            junk = Hb
            ssq = sb(es, kb, "hssq", [128, 1], F32)
            pp = [ps(es, kb, f"hp{i}", [128, 512], F32) for i in range(3)]
            npp = [0]
            nb = max(1, L // 512)
            bw = min(512, L)

            def sin_layer(dstf, w, srcf, srcb, bcol, fcol, K):
                for d in [DCUR[0]]:
                    for blk in range(nb):
                        cs_ = slice(blk * bw, (blk + 1) * bw)
                        p_ = pp[npp[0] % 3]
                        npp[0] += 1
                        kb.op("pe", lambda e: e.matmul(p_[0:64, 0:bw], lhsT=w[0:K, :], rhs=srcf(cs_), start=True, stop=True),
                              r=[w.b, srcb], w=[p_.b])
                        a_ = arg[:, 0:bw]
                        kb.op("dve", lambda e: e.tensor_scalar(out=a_, in0=p_[0:64, 0:bw], scalar1=bcol[:, 0:1], scalar2=fcol[:, 0:1],
                                                               op0=ALU.add, op1=ALU.mult), r=[p_.b, bcol.b, fcol.b], w=[arg.b])
                        kb.op("dve", lambda e: e.tensor_scalar(out=ki[:, 0:bw], in0=a_, scalar1=1.0 / TWO_PI, scalar2=None,
                                                               op0=ALU.mult), r=[arg.b], w=[ki.b])
                        kb.op("dve", lambda e: e.tensor_copy(out=kf[:, 0:bw], in_=ki[:, 0:bw]), r=[ki.b], w=[kf.b])
                        kb.op("dve", lambda e: e.scalar_tensor_tensor(out=a_, in0=kf[:, 0:bw], scalar=-TWO_PI, in1=a_,
                                                                      op0=ALU.mult, op1=ALU.add), r=[kf.b], w=[arg.b])
                        kb.op("dve", lambda e: e.tensor_scalar(out=msk[:, 0:bw], in0=a_, scalar1=math.pi, scalar2=-TWO_PI,
                                                               op0=ALU.is_gt, op1=ALU.mult), r=[arg.b], w=[msk.b])
                        kb.op("dve", lambda e: e.tensor_tensor(out=a_, in0=a_, in1=msk[:, 0:bw], op=ALU.add), r=[msk.b], w=[arg.b])
                        kb.op("dve", lambda e: e.tensor_scalar(out=msk[:, 0:bw], in0=a_, scalar1=-math.pi, scalar2=TWO_PI,
                                                               op0=ALU.is_lt, op1=ALU.mult), r=[arg.b], w=[msk.b])
                        kb.op("dve", lambda e: e.tensor_tensor(out=a_, in0=a_, in1=msk[:, 0:bw], op=ALU.add), r=[msk.b], w=[arg.b])
                        kb.op("dve", lambda e: e.tensor_scalar(out=a_, in0=a_, scalar1=3.14159, scalar2=-3.14159,
                                                               op0=ALU.min, op1=ALU.max), w=[arg.b])
                        kb.op("act", lambda e: e.activation(out=dstf(cs_)[0], in_=a_, func=AF.Sin), r=[arg.b], w=[dstf(cs_)[1]])

            DCUR = [0]
            for d in range(2):
                DCUR[0] = d
                kb.dma("sp", zp[:], C['hy_z' + suffix].t[d], w=[zp.b])
                sin_layer(lambda cs_: (h1[:, cs_], h1.b), w1, lambda cs_: zp[0:33, cs_], zp.b, b1, f1, 33)
                sin_layer(lambda cs_: (h2[:, d, cs_], h2.b), w2, lambda cs_: h1[:, cs_], h1.b, b2, f2, 64)
            for o in range(2):
                for c in range(2):
                    kb.op("pool", lambda e: e.memset(H[:], 0.0), w=[H.b])
                    for d in range(2):
                        col0 = o * 512 + d * 256 + c * 128
                        kcol = col0 // 128
                        kb.dma("sp", tnb[:], C['hy_tn' + suffix].t[d:d + 1, :].partition_broadcast(128), w=[tnb.b])
                        kb.op("act", lambda e: e.activation(out=dec[:], in_=tnb[:], func=AF.Exp, scale=nrate[:, c:c + 1]),
                              r=[tnb.b, nrate.b], w=[dec.b])
                        for blk in range(nb):
                            cs_ = slice(blk * bw, (blk + 1) * bw)
                            p_ = pp[npp[0] % 3]
                            npp[0] += 1
                            kb.op("pe", lambda e: e.matmul(p_[:, 0:bw], lhsT=w3[:, col0:col0 + 128], rhs=h2[:, d, cs_],
                                                           start=True, stop=True), r=[w3.b, h2.b], w=[p_.b])
                            if d == 0:
                                n0 = 4096 + blk * bw
                                width = bw
                            else:
                                n0 = 4096 - L + 1 + blk * bw
                                width = bw if blk < nb - 1 else bw - 1
                            kb.op("dve", lambda e: e.scalar_tensor_tensor(out=H[:, n0:n0 + width], in0=p_[:, 0:width],
                                                                          scalar=b3[:, kcol:kcol + 1],
                                                                          in1=dec[:, blk * bw:blk * bw + width],
                                                                          op0=ALU.add, op1=ALU.mult),
                                  r=[p_.b, b3.b, dec.b], w=[H.b])
                    kb.op("act", lambda e: e.activation(out=junk[:], in_=H[:], func=AF.Square, accum_out=ssq[:]),
                          r=[H.b], w=[junk.b, ssq.b])
                    kb.op("act", lambda e: e.activation(out=ssq[:], in_=ssq[:], func=AF.Sqrt), w=[ssq.b])
                    kb.op("dve", lambda e: e.reciprocal(out=ssq[:], in_=ssq[:]), w=[ssq.b])
                    kb.op("dve", lambda e: e.tensor_scalar(out=Hb[:], in0=H[:], scalar1=ssq[:, 0:1], scalar2=None, op0=ALU.mult),
                          r=[H.b, ssq.b], w=[Hb.b])
                    kb.dma("sp", Hd.t[o, c * 128:(c + 1) * 128, :], Hb[:], r=[Hb.b], w=[Hd.b])
        barrier(kb)

    def phase_hyena(self, li, need_ctx):
        kb, nc = self.kb, self.nc
        S = self.scr
        zT = S['zT']
        mix = S['mix_tm']
        C = self.consts
        with ExitStack() as es:
            identf = self.make_ident(es, F32, "identf_h")
            src = [sb(es, kb, f"hsrc{i}", [128, T], F32) for i in range(2)]
            stg = [sb(es, kb, f"hstg{i}", [128, NT, 128], F32) for i in range(2)]
            tpp = [ps(es, kb, f"htq{i}", [128, 4, 128], F32) for i in range(2)]
            st = [0]
            for i in range(6):
                s_ = src[i % 2]
                kb.dma("sp", s_[:], zT.t[14 + i], r=[zT.b], w=[s_.b])
                self.fm_to_tm(s_, None, identf, stg[i % 2], tpp, NT, st)
                kb.dma("sp", S['hy_tm'].t[i // 2].rearrange("(n p) f -> p n f", p=128)[:, :, (i % 2) * 128:(i % 2 + 1) * 128],
                       stg[i % 2][:], r=[stg[i % 2].b], w=[S['hy_tm'].b])
        barrier(kb)
        segs = [(0, 32, S['Hd_l'], 31)]
        if need_ctx:
            segs.append((32, 2, S['Hd_c'], 1))
        with ExitStack() as es:
            J = sb(es, kb, "hJ", [128, 128], F32)
            kb.dma("sp", J[:], C['antiident'].t, w=[J.b])
            bias = sb(es, kb, "hbias", [128, 2, 256], F32)
            for o in range(2):
                kb.dma("sp", bias[:, o, :], self.din['hy_bias'].t[li, o:o + 1, :].partition_broadcast(128), w=[bias.b])
            U = sb(es, kb, "hU", [128, NT, 256], F32)
            G = sb(es, kb, "hG", [128, NT, 256], F32)
            Y = sb(es, kb, "hY", [128, NT, 256], F32)
            Uf = sb(es, kb, "hUf", [128, NT, 256], BF16)
            At = [sb(es, kb, f"hAt{i}", [128, 8064], BF16) for i in range(3)]
            pf = [ps(es, kb, f"hpf{i}", [128, 512], F32) for i in range(2)]
            pc = [ps(es, kb, f"hpc{i}", [128, 512], F32) for i in range(2)]
            nat = 0
            kb.dma("sp", U[:], S['hy_tm'].t[2].rearrange("(n p) f -> p n f", p=128), r=[S['hy_tm'].b], w=[U.b])
            for o in range(2):
                kb.dma("sp", G[:], S['hy_tm'].t[o].rearrange("(n p) f -> p n f", p=128), r=[S['hy_tm'].b], w=[G.b])
                ntl = NT if need_ctx else 32
                for tt in range(ntl):
                    p_ = pf[tt % 2]
                    kb.op("pe", lambda e: e.matmul(p_[:, 0:256], lhsT=J[:], rhs=U[:, tt, :], start=True, stop=True),
                          r=[J.b, U.b], w=[p_.b])
                    kb.op("act", lambda e: e.copy(out=Uf[:, tt, :], in_=p_[:, 0:256]), r=[p_.b], w=[Uf.b])
                for (tile0, nti, Hd, mmax) in segs:
                    ncol = (2 * mmax + 1) * 128
                    x0 = (31 - mmax) * 128
                    for cg in range(16):
                        pcb = pc[cg % 2]
                        for cc in range(16):
                            ch = cg * 16 + cc
                            A = At[nat % 3]
                            nat += 1
                            src_ap = bass.AP(Hd.t.tensor, Hd.t.offset + (o * 256 + ch) * 8192 + 1 + x0, [[1, 128], [1, ncol]])
                            kb.dma("sp", A[:, 0:ncol], src_ap, r=[Hd.b], w=[A.b])
                            outv = pcb[:, cc * 32:cc * 32 + nti]
                            order = [0] + [m for m in range(-mmax, mmax + 1) if m != 0]
                            for mi, m in enumerate(order):
                                i0 = max(0, m)
                                i1 = min(nti - 1, nti - 1 + m)
                                if i1 < i0:
                                    continue
                                n = i1 - i0 + 1
                                j0 = i0 - m
                                kb.op("pe", lambda e: e.matmul(pcb[:, cc * 32 + i0:cc * 32 + i0 + n],
                                                               lhsT=A[:, (m + mmax) * 128:(m + mmax + 1) * 128],
                                                               rhs=Uf[:, tile0 + j0:tile0 + j0 + n, ch],
                                                               start=(mi == 0), stop=(mi == len(order) - 1)),
                                      r=[A.b, Uf.b], w=[pcb.b])
                        kb.op("act", lambda e: e.copy(out=Y[:, tile0:tile0 + nti, cg * 16:(cg + 1) * 16].rearrange("p i c -> p c i"),
                                                      in_=pcb[:, :].rearrange("p (c i) -> p c i", c=16)[:, :, 0:nti]),
                              r=[pcb.b], w=[Y.b])
                bb = bias[:, o, :].unsqueeze(1).to_broadcast([128, ntl, 256])
                kb.op("pool", lambda e: e.tensor_tensor(out=U[:, 0:ntl, :], in0=U[:, 0:ntl, :], in1=bb, op=ALU.mult),
                      r=[bias.b], w=[U.b])
                kb.op("dve", lambda e: e.tensor_tensor(out=U[:, 0:ntl, :], in0=U[:, 0:ntl, :], in1=Y[:, 0:ntl, :], op=ALU.add),
                      r=[Y.b], w=[U.b])
                kb.op("dve", lambda e: e.tensor_tensor(out=U[:, 0:ntl, :], in0=U[:, 0:ntl, :], in1=G[:, 0:ntl, :], op=ALU.mult),
                      r=[G.b], w=[U.b])
            ntl = NT if need_ctx else 32
            kb.dma("sp", mix.t.rearrange("(n p) f -> p n f", p=128)[:, 0:ntl, 768:1024], U[:, 0:ntl, :], r=[U.b], w=[mix.b])
        barrier(kb)

    def phase_peer(self, li, need_ctx, final_out=None):
        kb, nc = self.kb, self.nc
        S = self.scr
        modrow = S['modrow']
        xcur = S['xcur']
        ntile = NT if need_ctx else 32
        puv = S['puv_bf'].t.rearrange("e j d -> e (j d)")
        with ExitStack() as es0:
            cbuf = [sb(es0, kb, f"pcv{i}", [128, 8192], BF16) for i in range(3)]
            n = 0
            for (src, j) in ((self.din['peer_u'], 0), (self.din['peer_v'], 1)):
                dst = S['puv_bf']
                for c in range(16):
                    cb_ = cbuf[n % 3]
                    n += 1
                    kb.dma("pool", cb_[:], src.t[li, c * 1024:(c + 1) * 1024, :].rearrange("(p r) d -> p (r d)", p=128),
                           w=[cb_.b])
                    r0 = li * 16384 + c * 1024
                    kb.dma("sp", dst.t[r0:r0 + 1024, j, :].rearrange("(p r) d -> p r d", p=128),
                           cb_[:].rearrange("p (r d) -> p r d", r=8), r=[cb_.b], w=[dst.b])
        barrier(kb)
        if not hasattr(self, '_bc_reg'):
            self._bc_reg = nc.gpsimd.alloc_register("peer_bc")
            nc.gpsimd.reg_mov(self._bc_reg, 32767)
        bc_reg = self._bc_reg
        NB = 12
        with ExitStack() as es:
            identb = self.make_ident(es, BF16, "identb_p")
            identf = self.make_ident(es, F32, "identf_p")
            Wq = sb(es, kb, "pWq", [128, 8, 2048], BF16)
            keysT = sb(es, kb, "pkeysT", [128, 16, 128], BF16)
            es_w = ExitStack()
            wst = [sb(es_w, kb, f"pwst{i}", [128, 2048], F32) for i in range(2)]
            for k in range(8):
                kb.dma("sp", wst[k % 2][:], self.din['peer_wq'].t[li, k * 128:(k + 1) * 128, :], w=[wst[k % 2].b])
                kb.op("pool", lambda e: e.tensor_copy(out=Wq[:, k, :], in_=wst[k % 2][:]), r=[wst[k % 2].b], w=[Wq.b])
            pq = [ps(es, kb, f"ppq{i}", [128, 4, 128], F32) for i in range(2)]
            psr = [ps(es, kb, f"ppsr{i}", [128, 4, 128], F32) for i in range(2)]
            tp = [ps(es, kb, f"pptp{i}", [128, 8, 128], BF16) for i in range(2)]
            for c4 in range(4):
                kst = wst[c4 % 2]
                kb.dma("sp", kst[:, 0:512].rearrange("n (c d) -> n c d", c=4),
                       self.din['peer_keys'].t[li].rearrange("h p n d -> n (h p) d")[:, c4 * 4:(c4 + 1) * 4, :], w=[kst.b])
                p_ = pq[c4 % 2]
                for j in range(4):
                    kb.op("pe", lambda e: e.transpose(p_[:, j, :], kst[:, j * 128:(j + 1) * 128], identf[:]),
                          r=[kst.b, identf.b], w=[p_.b])
                kb.op("act", lambda e: e.copy(out=keysT[:, c4 * 4:(c4 + 1) * 4, :], in_=p_[:]), r=[p_.b], w=[keysT.b])
            barrier(kb)
            es_w.close()
            G = [sb(es, kb, f"pG{w_}", [128, D], F32) for w_ in range(2)]
            SH = [sb(es, kb, f"pSH{w_}", [128, D], F32) for w_ in range(2)]
            GF = [sb(es, kb, f"pGF{w_}", [128, D], F32) for w_ in range(2)]
            gn = sb(es, kb, "pgn", [128, D], F32)
            kb.dma("sp", gn[:], self.din['ffn_norm'].t[li:li + 1, :].partition_broadcast(128), w=[gn.b])
            for w_ in range(2 if need_ctx else 1):
                kb.dma("sp", G[w_][:], modrow.t[w_:w_ + 1, 4 * D:5 * D].partition_broadcast(128), r=[modrow.b], w=[G[w_].b])
                kb.dma("sp", SH[w_][:], modrow.t[w_:w_ + 1, 3 * D:4 * D].partition_broadcast(128), r=[modrow.b], w=[SH[w_].b])
                kb.dma("sp", GF[w_][:], modrow.t[w_:w_ + 1, 5 * D:6 * D].partition_broadcast(128), r=[modrow.b], w=[GF[w_].b])
                kb.op("dve", lambda e: e.scalar_tensor_tensor(out=G[w_][:], in0=G[w_][:], scalar=1.0, in1=gn[:],
                                                              op0=ALU.add, op1=ALU.mult), r=[gn.b], w=[G[w_].b])
            xt = [sb(es, kb, f"pxt{i}", [128, D], F32) for i in range(2)]
            hn = [sb(es, kb, f"phn{i}", [128, D], F32) for i in range(2)]
            hnb2 = [sb(es, kb, f"phnb{i}", [128, D], BF16) for i in range(2)]
            junk = sb(es, kb, "pjunk", [128, D], F32)
            junkb = sb(es, kb, "pjunkb", [128, D], BF16)
            ss = sb(es, kb, "pss", [128, 1], F32)
            hnT = sb(es, kb, "phnT", [128, 8, 128], BF16)
            qTs = sb(es, kb, "pqTs", [128, 16, 128], BF16)
            s_sb = sb(es, kb, "ps_sb", [128, 16, 128], F32)
            tmpc = sb(es, kb, "ptmpc", [128, 256], F32)
            sv = sb(es, kb, "psv", [128, 16, 16], F32)
            siu = sb(es, kb, "psiu", [128, 16, 16], U32)
            sif = sb(es, kb, "psif", [128, 16, 16], F32)
            s1x = sb(es, kb, "ps1x", [128, 8, 16], F32)
            cand = sb(es, kb, "pcand", [128, 8, 256], F32)
            ci = sb(es, kb, "pci", [128, 8, 256], F32)
            tsv = sb(es, kb, "ptsv", [128, 8, 16], F32)
            posu = sb(es, kb, "pposu", [128, 8, 16], U32)
            pau = sb(es, kb, "ppau", [128, 8, 16], U32)
            pbu = sb(es, kb, "ppbu", [128, 8, 16], U32)
            paf = sb(es, kb, "ppaf", [128, 8, 16], F32)
            pbf = sb(es, kb, "ppbf", [128, 8, 16], F32)
            oh = sb(es, kb, "poh", [128, 16, 16], F32)
            eidb = sb(es, kb, "peidb", [128, 8, 16], F32)
            iota16 = sb(es, kb, "piota", [128, 16], F32)
            kb.dma("sp", iota16[:], self.consts['iota16'].t[0:1, :].partition_broadcast(128), w=[iota16.b])
            eidf = sb(es, kb, "peidf", [128, 8, 16], F32)
            eidx = [sb(es, kb, f"peidx{i}", [128, 128], I32) for i in range(2)]
            gate2 = [sb(es, kb, f"pgate{i}", [128, 8, 16], F32) for i in range(2)]
            gsum = sb(es, kb, "pgsum", [128, 8], F32)
            act = sb(es, kb, "pact", [128, 128], F32, nsub=128)
            wgt = sb(es, kb, "pwgt", [128, 128], F32, nsub=128)
            acc = sb(es, kb, "pacc", [128, D], F32)
            dgs = [sb(es, kb, f"pdg{i}", [128, 128], BF16) for i in range(4)]
            pacc = [ps(es, kb, f"ppacc{i}", [128, 512], F32) for i in range(2)]
            ug = [sb(es, kb, f"pug{i}", [128, 2 * D], BF16) for i in range(NB)]
            nev = [0]

            def stage_a(tt):
                b = tt % 2
                ts = slice(tt * 128, (tt + 1) * 128)
                which = 0 if tt < 32 else 1
                x_, h_ = xt[b], hn[b]
                hnb = hnb2[b]
                gate = gate2[b]
                kb.dma("sp", x_[:], xcur.t[ts, :], r=[xcur.b], w=[x_.b])
                kb.op("act", lambda e: e.activation(out=junk[:], in_=x_[:], func=AF.Square, accum_out=ss[:]),
                      r=[x_.b], w=[junk.b, ss.b])
                kb.op("dve", lambda e: e.tensor_scalar(out=ss[:], in0=ss[:], scalar1=1.0 / D, scalar2=NORM_EPS,
                                                       op0=ALU.mult, op1=ALU.add), w=[ss.b])
                kb.op("act", lambda e: e.activation(out=ss[:], in_=ss[:], func=AF.Sqrt), w=[ss.b])
                kb.op("dve", lambda e: e.reciprocal(out=ss[:], in_=ss[:]), w=[ss.b])
                kb.op("dve", lambda e: e.scalar_tensor_tensor(out=h_[:], in0=x_[:], scalar=ss[:, 0:1], in1=G[which][:],
                                                              op0=ALU.mult, op1=ALU.mult), r=[x_.b, ss.b, G[which].b], w=[h_.b])
                kb.op("dve", lambda e: e.tensor_tensor(out=h_[:], in0=h_[:], in1=SH[which][:], op=ALU.add),
                      r=[SH[which].b], w=[h_.b])
                kb.op("act", lambda e: e.copy(out=hnb[:], in_=h_[:]), r=[h_.b], w=[hnb.b])
                for k in range(8):
                    kb.op("pe", lambda e: e.transpose(tp[b][:, k, :], hnb[:, k * 128:(k + 1) * 128], identb[:]),
                          r=[hnb.b, identb.b], w=[tp[b].b])
                kb.op("act", lambda e: e.copy(out=hnT[:], in_=tp[b][:]), r=[tp[b].b], w=[hnT.b])
                for c4 in range(4):
                    p_ = pq[c4 % 2]
                    for j in range(4):
                        c = c4 * 4 + j
                        for k in range(8):
                            kb.op("pe", lambda e: e.matmul(p_[:, j, :], lhsT=Wq[:, k, c * 128:(c + 1) * 128], rhs=hnT[:, k, :],
                                                           start=(k == 0), stop=(k == 7)), r=[Wq.b, hnT.b], w=[p_.b])
                    nev[0] += 1
                    if nev[0] % 2:
                        kb.op("act", lambda e: e.copy(out=qTs[:, c4 * 4:(c4 + 1) * 4, :], in_=p_[:]), r=[p_.b], w=[qTs.b])
                    else:
                        kb.op("dve", lambda e: e.tensor_copy(out=qTs[:, c4 * 4:(c4 + 1) * 4, :], in_=p_[:]), r=[p_.b], w=[qTs.b])
                for c4 in range(4):
                    p_ = psr[c4 % 2]
                    for j in range(4):
                        c = c4 * 4 + j
                        kb.op("pe", lambda e: e.matmul(p_[:, j, :], lhsT=qTs[:, c, :], rhs=keysT[:, c, :], start=True, stop=True),
                              r=[qTs.b, keysT.b], w=[p_.b])
                    kb.op("act", lambda e: e.copy(out=s_sb[:, c4 * 4:(c4 + 1) * 4, :], in_=p_[:]), r=[p_.b], w=[s_sb.b])
                for c in range(16):
                    kb.op("dve", lambda e: e.max(out=sv[:, c, 0:8], in_=s_sb[:, c, :]), r=[s_sb.b], w=[sv.b])
                    kb.op("dve", lambda e: e.max_index(out=siu[:, c, 0:8], in_max=sv[:, c, 0:8], in_values=s_sb[:, c, :]),
                          r=[s_sb.b, sv.b], w=[siu.b])
                    kb.op("dve", lambda e: e.match_replace(out=tmpc[:, 0:128], in_to_replace=sv[:, c, 0:8], in_values=s_sb[:, c, :],
                                                           imm_value=-1e30), r=[s_sb.b, sv.b], w=[tmpc.b])
                    kb.op("dve", lambda e: e.max(out=sv[:, c, 8:16], in_=tmpc[:, 0:128]), r=[tmpc.b], w=[sv.b])
                    kb.op("dve", lambda e: e.max_index(out=siu[:, c, 8:16], in_max=sv[:, c, 8:16], in_values=tmpc[:, 0:128]),
                          r=[tmpc.b, sv.b], w=[siu.b])
                kb.op("dve", lambda e: e.tensor_copy(out=sif[:], in_=siu[:]), r=[siu.b], w=[sif.b])
                sv4 = sv[:].rearrange("p (h q) k -> p h q k", q=2)
                si4 = sif[:].rearrange("p (h q) k -> p h q k", q=2)
                kb.op("dve", lambda e: e.tensor_scalar(out=s1x[:], in0=si4[:, :, 0, :], scalar1=128.0, scalar2=None, op0=ALU.mult),
                      r=[sif.b], w=[s1x.b])
                c4v = cand[:].rearrange("p h (a b) -> p h a b", a=16)
                i4v = ci[:].rearrange("p h (a b) -> p h a b", a=16)
                kb.op("dve", lambda e: e.tensor_tensor(out=c4v, in0=sv4[:, :, 0, :].unsqueeze(3).to_broadcast([128, 8, 16, 16]),
                                                       in1=sv4[:, :, 1, :].unsqueeze(2).to_broadcast([128, 8, 16, 16]), op=ALU.add),
                      r=[sv.b], w=[cand.b])
                kb.op("dve", lambda e: e.tensor_tensor(out=i4v, in0=s1x[:].unsqueeze(3).to_broadcast([128, 8, 16, 16]),
                                                       in1=si4[:, :, 1, :].unsqueeze(2).to_broadcast([128, 8, 16, 16]), op=ALU.add),
                      r=[s1x.b, sif.b], w=[ci.b])
                for h in range(8):
                    kb.op("dve", lambda e: e.max(out=tsv[:, h, 0:8], in_=cand[:, h, :]), r=[cand.b], w=[tsv.b])
                    kb.op("dve", lambda e: e.max_index(out=posu[:, h, 0:8], in_max=tsv[:, h, 0:8], in_values=cand[:, h, :]),
                          r=[cand.b, tsv.b], w=[posu.b])
                    kb.op("dve", lambda e: e.match_replace(out=tmpc[:], in_to_replace=tsv[:, h, 0:8], in_values=cand[:, h, :],
                                                           imm_value=-1e30), r=[cand.b, tsv.b], w=[tmpc.b])
                    kb.op("dve", lambda e: e.max(out=tsv[:, h, 8:16], in_=tmpc[:]), r=[tmpc.b], w=[tsv.b])
                    kb.op("dve", lambda e: e.max_index(out=posu[:, h, 8:16], in_max=tsv[:, h, 8:16], in_values=tmpc[:]),
                          r=[tmpc.b, tsv.b], w=[posu.b])
                kb.op("dve", lambda e: e.tensor_single_scalar(out=pau[:], in_=posu[:], scalar=4, op=ALU.logical_shift_right),
                      r=[posu.b], w=[pau.b])
                kb.op("dve", lambda e: e.tensor_single_scalar(out=pbu[:], in_=posu[:], scalar=15, op=ALU.bitwise_and),
                      r=[posu.b], w=[pbu.b])
                kb.op("dve", lambda e: e.tensor_copy(out=paf[:], in_=pau[:]), r=[pau.b], w=[paf.b])
                kb.op("dve", lambda e: e.tensor_copy(out=pbf[:], in_=pbu[:]), r=[pbu.b], w=[pbf.b])
                for h in range(8):
                    for (pf, src, first) in ((paf, s1x[:, h, :], True), (pbf, si4[:, h, 1, :], False)):
                        kb.op("dve", lambda e: e.tensor_tensor(out=oh[:], in0=pf[:, h, :].unsqueeze(2).to_broadcast([128, 16, 16]),
                                                               in1=iota16[:].unsqueeze(1).to_broadcast([128, 16, 16]),
                                                               op=ALU.is_equal), r=[pf.b, iota16.b], w=[oh.b])
                        kb.op("dve", lambda e: e.tensor_tensor(out=oh[:], in0=oh[:], in1=src.unsqueeze(1).to_broadcast([128, 16, 16]),
                                                               op=ALU.mult), r=[s1x.b, sif.b], w=[oh.b])
                        dst = eidf if first else eidb
                        kb.op("dve", lambda e: e.tensor_reduce(out=dst[:, h, :], in_=oh[:], axis=AX.X, op=ALU.add), r=[oh.b], w=[dst.b])
                kb.op("dve", lambda e: e.tensor_tensor(out=eidf[:], in0=eidf[:], in1=eidb[:], op=ALU.add), r=[eidb.b], w=[eidf.b])
                ei = eidx[b]
                if li > 0:
                    kb.op("dve", lambda e: e.tensor_scalar(out=eidf[:], in0=eidf[:], scalar1=float(li * 16384), scalar2=None,
                                                           op0=ALU.add), w=[eidf.b])
                kb.op("dve", lambda e: e.tensor_copy(out=ei[:], in_=eidf[:].rearrange("p h k -> p (h k)")), r=[eidf.b], w=[ei.b])
                kb.op("dve", lambda e: e.tensor_tensor(out=gate[:], in0=tsv[:], in1=tsv[:, :, 0:1].to_broadcast([128, 8, 16]),
                                                       op=ALU.subtract), r=[tsv.b], w=[gate.b])
                kb.op("act", lambda e: e.activation(out=gate[:], in_=gate[:], func=AF.Exp), w=[gate.b])
                kb.op("dve", lambda e: e.tensor_reduce(out=gsum[:], in_=gate[:], axis=AX.X, op=ALU.add), r=[gate.b], w=[gsum.b])
                kb.op("dve", lambda e: e.reciprocal(out=gsum[:], in_=gsum[:]), w=[gsum.b])
                kb.op("dve", lambda e: e.tensor_tensor(out=gate[:], in0=gate[:], in1=gsum[:].unsqueeze(2).to_broadcast([128, 8, 16]),
                                                       op=ALU.mult), r=[gsum.b], w=[gate.b])

            def stage_b(tt):
                b = tt % 2
                ts = slice(tt * 128, (tt + 1) * 128)
                which = 0 if tt < 32 else 1
                x_, h_ = xt[b], hn[b]
                hnb = hnb2[b]
                gate = gate2[b]
                ei = eidx[b]
                kb.op("dve", lambda e: e.memset(act[:], 0.0), w=list(act.sub))
                gflat = gate[:].rearrange("p h k -> p (h k)")
                for slot in range(128):
                    uv_ = ug[slot % NB]
                    kb.idma(uv_[:, :], puv, bass.IndirectOffsetOnAxis(ap=ei[:, slot:slot + 1], axis=0), r=[ei.b, S['puv_bf'].b],
                            w=[uv_.b], bounds_check=bc_reg, oob_is_err=False)
                    kb.op("dve", lambda e: e.scalar_tensor_tensor(out=junkb[:], in0=uv_[:, 0:D], scalar=1.0, in1=hnb[:],
                                                                  op0=ALU.mult, op1=ALU.mult, accum_out=act[:, slot:slot + 1]),
                          r=[uv_.b, hnb.b], w=[junkb.b, act.sub[slot]])
                    kb.op("act", lambda e: e.activation(out=wgt[:, slot:slot + 1], in_=act[:, slot:slot + 1], func=AF.Gelu),
                          r=[act.sub[slot]], w=[wgt.sub[slot]])
                    dg = dgs[slot % 4]
                    kb.op("dve", lambda e: e.tensor_scalar(out=dg[:], in0=identb[:], scalar1=wgt[:, slot:slot + 1],
                                                           scalar2=gflat[:, slot:slot + 1], op0=ALU.mult, op1=ALU.mult),
                          r=[identb.b, wgt.sub[slot], gate.b], w=[dg.b])
                    for n in range(2):
                        kb.op("pe", lambda e: e.matmul(pacc[n][:, :], lhsT=dg[:], rhs=uv_[:, D + n * 512:D + (n + 1) * 512],
                                                       start=(slot == 0), stop=(slot == 127)), r=[dg.b, uv_.b], w=[pacc[n].b])
                for n in range(2):
                    kb.op("dve", lambda e: e.tensor_tensor(out=acc[:, n * 512:(n + 1) * 512], in0=pacc[n][:, :],
                                                           in1=GF[which][:, n * 512:(n + 1) * 512], op=ALU.mult),
                          r=[pacc[n].b, GF[which].b], w=[acc.b])
                kb.op("dve", lambda e: e.tensor_tensor(out=x_[:], in0=x_[:], in1=acc[:], op=ALU.add), r=[acc.b], w=[x_.b])
                if final_out is not None and tt < 32:
                    kb.dma("sp", final_out.t[ts, :], x_[:], r=[x_.b], w=[final_out.b])
                else:
                    kb.dma("sp", xcur.t[ts, :], x_[:], r=[x_.b], w=[xcur.b])

            stage_a(0)
            for tt in range(ntile):
                if tt + 1 < ntile:
                    stage_a(tt + 1)
                stage_b(tt)
        barrier(kb)

    def zero_dram(self, tl, nelem):
        kb = self.kb
        with ExitStack() as es:
            z = sb(es, kb, "zeros2", [128, 8192], F32)
            kb.op("pool", lambda e: e.memset(z[:], 0.0), w=[z.b])
            nd = len(tl.t.shape)
            names = " ".join(f"a{i}" for i in range(nd))
            flat = tl.t.rearrange(f"{names} -> ({names})")
            per = 128 * 8192
            off = 0
            while off < nelem:
                n = min(per, nelem - off)
                cols = n // 128
                kb.dma("sp", flat[off:off + n].rearrange("(p c) -> p c", p=128), z[:, :cols], r=[z.b], w=[tl.b])
                off += n
        barrier(kb)

USE_HY = True
USE_SCAN2 = True
USE_PEER = True


def build(dbg=None, nlayers=DEPTH, scan_steps=None):
    P = Prog(dbg, nlayers)
    kb = P.kb
    P._prep_buf = Buf("prep")
    P.const_in('ident_f32', [128, 128])
    P.const_in('blockones', [128, 128])
    P.scratch('modrow', [2, 6 * D])
    P.scratch('zT', [20, 128, T])
    P.scratch('krope_tm', [T, 32])
    P.scratch('xcur', [T, D])
    P.scratch('w_fm', [2, 3, 128, T])
    P.scratch('kk_fm', [3, 128, T], BF16)
    P.scratch('rb_fm', [3, 128, T], BF16)
    P.scratch('kk6_tm', [6, T, 128], BF16)
    P.scratch('akk6_tm', [2, 6, T, 384], BF16)
    P.scratch('krep6_tm', [2, 6, T, 128], BF16)
    P.scratch('v6_tm', [6, T, 192], BF16)
    P.scratch('L4_tm', [2, 4, T, 384], BF16)
    P.scratch('V2_tm', [2, T, 3, 64], BF16)
    P.scratch('y_fm', [2, 3, 2, 64, T])
    for n in ('v_tm', 'r_tm', 'k_tm', 'g_tm'):
        P.scratch(n, [T, 384])
    P.scratch('mix_tm', [T, D])
    P.const_in('rope_cs', [SEQ, 32])
    P.const_in('antiident', [128, 128])
    P.const_in('iota16', [1, 16])
    P.const_in('hy_nrate', [256])
    P.const_in('hy_z_l', [2, 33, SEQ])
    P.const_in('hy_tn_l', [2, SEQ])
    P.const_in('hy_z_c', [2, 33, CTX])
    P.const_in('hy_tn_c', [2, CTX])
    P.scratch('Hd_l', [2, 256, 8192], BF16)
    P.scratch('Hd_c', [2, 256, 8192], BF16)
    P.scratch('hy_tm', [3, T, 256])
    P.scratch('puv_bf', [2 * 16384, 2, D], BF16)
    P.scratch('qT', [96, 6, T], BF16)
    P.scratch('kT', [96, 6, T], BF16)
    P.scratch('V1', [T, 6, 65], BF16)

    def dump(names):
        outs = []
        for n in names:
            src = P.scr[n]
            o = P.out_tensor("o_" + n, list(src.t.shape), src.t.dtype)
            kb.dma("sp", o.t, src.t, r=[src.b, P._prep_buf], w=[o.b])
            outs.append(o.b)
        kb.wait_all("sp", outs)

    out = None
    if dbg is None:
        out = P.out_tensor("out", [SEQ, D])
    P.phase_zero_init()
    P.zero_dram(P.scr['mix_tm'], T * D)
    for li in range(nlayers):
        first = (li == 0)
        need_ctx = li < DEPTH - 1
        last = (li == nlayers - 1)
        P.phase_mod(li)
        P.phase_inproj(li, first)
        if dbg == 'inproj':
            dump(['zT', 'krope_tm', 'modrow'])
            return P
        P.phase_rw_prep(li)
        if dbg == 'rwprep':
            dump(['w_fm', 'kk_fm', 'L4_tm', 'V2_tm', 'v_tm', 'r_tm', 'k_tm', 'g_tm'])
            return P
        if USE_SCAN2:
            P.phase_rw_scan2(li, scan_steps)
        else:
            P.phase_rw_scan(li, scan_steps)
        if dbg == 'rwscan':
            dump(['y_fm', 'w_fm', 'kk_fm', 'L4_tm', 'V2_tm', 'v_tm', 'r_tm', 'k_tm', 'g_tm'])
            return P
        P.phase_rw_readout(li, need_ctx)
        if dbg == 'readout':
            dump(['mix_tm', 'y_fm'])
            return P
        P.phase_mla_prep(li)
        if dbg == 'mlaprep':
            dump(['qT', 'kT', 'V1'])
            return P
        P.phase_mla_attn(li, need_ctx)
        if dbg == 'mla':
            dump(['mix_tm'])
            return P
        if USE_HY:
            P.phase_hy_filter(li, SEQ, P.scr['Hd_l'], '_l')
            if need_ctx:
                P.phase_hy_filter(li, CTX, P.scr['Hd_c'], '_c')
            if dbg == 'hyfilt':
                dump(['Hd_l', 'Hd_c'])
                return P
            P.phase_hyena(li, need_ctx)
            if dbg == 'hyena':
                dump(['mix_tm'])
                return P
        P.phase_outproj(li, first, need_ctx, final_out=(out if (last and dbg is None and not USE_PEER) else None))
        if dbg == 'outproj':
            dump(['xcur', 'mix_tm'])
            return P
        if USE_PEER:
            P.phase_peer(li, need_ctx, final_out=(out if (last and dbg is None) else None))
            if dbg == 'peer':
                dump(['xcur'])
                return P
    kb.wait_all("sp", [out.b])
    return P


def host_consts():
    c = {}
    c['ident_f32'] = np.eye(128, dtype=np.float32)
    bo = np.zeros((128, 128), np.float32)
    bo[:64, :64] = 1.0
    bo[64:, 64:] = 1.0
    c['blockones'] = bo
    rows = SEQ // 64
    row, col = np.meshgrid(np.arange(rows), np.arange(64), indexing='ij')
    inv = (10000.0 ** (-np.arange(0, 16, 2, dtype=np.float32) / 16)).astype(np.float32)
    pos = np.stack([row.reshape(-1), col.reshape(-1)], axis=-1).astype(np.float32)
    ang = (pos[:, :, None] * inv[None, None, :]).astype(np.float32)
    c['iota16'] = np.arange(16, dtype=np.float32).reshape(1, 16)
    c['antiident'] = np.ascontiguousarray(np.eye(128, dtype=np.float32)[::-1])
    rates = np.abs(np.linspace(math.log(1e-2) / 1.5, math.log(1e-2) / 0.3, 256, dtype=np.float32)).astype(np.float32)
    c['hy_nrate'] = (-rates).astype(np.float32)
    for L_, suf in ((SEQ, '_l'), (CTX, '_c')):
        tn = (np.arange(L_, dtype=np.float32) / np.float32(L_)).astype(np.float32)
        bands = np.arange(1, 17, dtype=np.float32)
        angh = (np.float32(2.0 * math.pi) * tn[:, None] * bands[None, :]).astype(np.float32)
        z = np.concatenate([tn[:, None], np.cos(angh), np.sin(angh)], axis=-1).astype(np.float32)
        c['hy_z' + suf] = np.ascontiguousarray(np.stack([z.T, z[::-1].T]).astype(np.float32))
        c['hy_tn' + suf] = np.ascontiguousarray(np.stack([tn, tn[::-1]]).astype(np.float32))
    c['rope_cs'] = np.concatenate([np.cos(ang).reshape(SEQ, 16), np.sin(ang).reshape(SEQ, 16)], axis=1).astype(np.float32)
    return c


_PROG = {}


def kernel(**inputs):
    if 'p' not in _PROG:
        _PROG['p'] = build(None)
    P = _PROG['p']
    hc = host_consts()
    in_maps = []
    for b in range(8):
        m = {'x': np.ascontiguousarray(inputs['x'][b], dtype=np.float32),
             'ctx': np.ascontiguousarray(inputs['ctx'][b], dtype=np.float32),
             'c2': np.ascontiguousarray(np.stack([inputs['c'][b], inputs['c_ctx']]), dtype=np.float32)}
        for n in INPUT_NAMES:
            m[n] = np.ascontiguousarray(inputs[n], dtype=np.float32)
        for k in P.consts:
            m[k] = hc[k]
        in_maps.append(m)
    res = run_bass_kernel_spmd(P.nc, in_maps, core_ids=list(range(8)))
    return np.stack([np.asarray(r['out'], dtype=np.float32) for r in res.results], axis=0)
```

```python
import math
from contextlib import ExitStack
import numpy as np
import concourse.bass as bass
import concourse.mybir as mybir
from concourse.bass_utils import run_bass_kernel_spmd

F32 = mybir.dt.float32
BF16 = mybir.dt.bfloat16
I32 = mybir.dt.int32
U32 = mybir.dt.uint32
AF = mybir.ActivationFunctionType
ALU = mybir.AluOpType
AX = mybir.AxisListType

D = 1024
SEQ = 4096
CTX = 256
T = SEQ + CTX
NT = T // 128
DEPTH = 2
RW_PROJ, MLA_PROJ, HY_PROJ = 1408, 416, 768
IN_PROJ = 2592
NORM_EPS = 1e-6

SEM_CAP = 30000


class Buf:
    __slots__ = ("w", "r", "name")

    def __init__(self, name=""):
        self.w = None
        self.r = {}
        self.name = name


class KB:
    def __init__(self):
        self.nc = bass.Bass("TRN2", target_bir_lowering=False)
        nc = self.nc
        self.engs = {"pe": nc.tensor, "act": nc.scalar, "dve": nc.vector,
                     "pool": nc.gpsimd, "sp": nc.sync}
        self.sem = {}
        self.cnt = {}
        self.seen = {e: {} for e in self.engs}
        self.nsem = 0
        self.semh = {}
        self.dma_pool = {}
        self.dma_rr = {}
        self.ndma_sems = {"sp": 12, "pool": 16, "act": 4}
        self.n_ins = 0
        self.same_engine_sync = True

    def new_sem(self, name):
        h = self.nc.alloc_semaphore(name=f"{name}_{self.nsem}")
        sid = self.nsem
        self.nsem += 1
        self.semh[sid] = h
        return sid

    def _wait(self, e, toks):
        seen = self.seen[e]
        best = {}
        for (sid, val, src) in toks:
            if src == "pe" and e == "pe":
                continue
            if (not self.same_engine_sync) and src == e:
                continue
            if seen.get(sid, 0) >= val:
                continue
            if best.get(sid, 0) < val:
                best[sid] = val
        for sid, val in best.items():
            self.engs[e].wait_ge(self.semh[sid], val)
            seen[sid] = val
            self.n_ins += 1

    def _deps(self, r, w):
        toks = []
        for b in r:
            if b.w is not None:
                toks.append(b.w)
        for b in w:
            if b.w is not None:
                toks.append(b.w)
            toks.extend(b.r.values())
        return toks

    def _record(self, tok, r, w):
        for b in r:
            old = b.r.get(tok[0])
            if old is None or old[1] < tok[1]:
                b.r[tok[0]] = tok
        for b in w:
            b.w = tok
            b.r = {}

    def op(self, e, fn, r=(), w=()):
        self._wait(e, self._deps(r, w))
        ins = fn(self.engs[e])
        if e not in self.sem or self.cnt[e] >= SEM_CAP:
            self.sem[e] = self.new_sem(e)
            self.cnt[e] = 0
        self.cnt[e] += 1
        ins.then_inc(self.semh[self.sem[e]], 1)
        tok = (self.sem[e], self.cnt[e], e)
        self._record(tok, r, w)
        self.n_ins += 1
        return tok

    def dma(self, q, out, in_, r=(), w=(), **kw):
        toks = self._deps(r, w)
        if q not in self.dma_pool:
            self.dma_pool[q] = [[self.new_sem(f"dma{q}"), 0] for _ in range(self.ndma_sems[q])]
            self.dma_rr[q] = 0
        slot = self.dma_rr[q]
        self.dma_rr[q] = (slot + 1) % len(self.dma_pool[q])
        ent = self.dma_pool[q][slot]
        if ent[1] + 16 > SEM_CAP:
            ent[0] = self.new_sem(f"dma{q}")
            ent[1] = 0
        if ent[1] > 0:
            toks.append((ent[0], ent[1], "dma"))
        self._wait(q, toks)
        ins = self.engs[q].dma_start(out=out, in_=in_, **kw)
        ent[1] += 16
        ins.then_inc(self.semh[ent[0]], 16)
        tok = (ent[0], ent[1], "dma")
        self._record(tok, r, w)
        self.n_ins += 1
        return tok

    def idma(self, out, in_, in_off, r=(), w=(), **kw):
        q = "pool"
        toks = self._deps(r, w)
        if q not in self.dma_pool:
            self.dma_pool[q] = [[self.new_sem(f"dma{q}"), 0] for _ in range(self.ndma_sems[q])]
            self.dma_rr[q] = 0
        slot = self.dma_rr[q]
        self.dma_rr[q] = (slot + 1) % len(self.dma_pool[q])
        ent = self.dma_pool[q][slot]
        if ent[1] + 16 > SEM_CAP:
            ent[0] = self.new_sem(f"dma{q}")
            ent[1] = 0
        if ent[1] > 0:
            toks.append((ent[0], ent[1], "dma"))
        self._wait(q, toks)
        ins = self.nc.gpsimd.indirect_dma_start(out=out, out_offset=None, in_=in_, in_offset=in_off, **kw)
        ent[1] += 16
        ins.then_inc(self.semh[ent[0]], 16)
        tok = (ent[0], ent[1], "dma")
        self._record(tok, r, w)
        self.n_ins += 1
        return tok

    def wait_all(self, e, bufs):
        toks = []
        for b in bufs:
            if b.w is not None:
                toks.append(b.w)
            toks.extend(b.r.values())
        self._wait(e, toks)


class Tl:
    def __init__(self, t, name, nsub=1):
        self.t = t
        self.b = Buf(name)
        self.sub = [Buf(f"{name}{i}") for i in range(nsub)] if nsub > 1 else None

    def __getitem__(self, idx):
        return self.t[idx]


_UID = [0]


def sb(es, kb, name, shape, dt, nsub=1):
    _UID[0] += 1
    name = f"{name}_{_UID[0]}"
    return Tl(es.enter_context(kb.nc.sbuf_tensor(name, list(shape), dt)), name, nsub)


def ps(es, kb, name, shape, dt, nsub=1):
    _UID[0] += 1
    name = f"{name}_{_UID[0]}"
    return Tl(es.enter_context(kb.nc.psum_tensor(name, list(shape), dt)), name, nsub)


def barrier(kb):
    toks = []
    for e in kb.sem:
        toks.append((kb.sem[e], kb.cnt[e], "bar"))
    for q in kb.dma_pool:
        for ent in kb.dma_pool[q]:
            if ent[1] > 0:
                toks.append((ent[0], ent[1], "bar"))
    for e in kb.engs:
        kb._wait(e, toks)


INPUT_NAMES = ['mod_w', 'mod_b', 'mix_norm', 'w_in', 'w_out',
               'rw_conv', 'rw_decay_up', 'rw_decay0', 'rw_a_up', 'rw_a0', 'rw_gate_up', 'rw_k_k', 'rw_k_a',
               'rw_r_k', 'rw_gn_g', 'rw_gn_b',
               'mla_q_norm', 'mla_w_uq', 'mla_kv_norm', 'mla_w_ukv', 'mla_q_gain', 'mla_k_gain',
               'hy_conv', 'hy_w1', 'hy_b1', 'hy_freq1', 'hy_w2', 'hy_b2', 'hy_freq2', 'hy_w3', 'hy_b3', 'hy_bias',
               'ffn_norm', 'peer_wq', 'peer_keys', 'peer_u', 'peer_v']

WEIGHT_SHAPES = {
    'mod_w': (2, 1024, 6144), 'mod_b': (2, 6144), 'mix_norm': (2, 1024), 'w_in': (2, 1024, 2592),
    'w_out': (2, 1024, 1024), 'rw_conv': (2, 3, 1408), 'rw_decay_up': (2, 2, 64, 384), 'rw_decay0': (2, 2, 384),
    'rw_a_up': (2, 2, 64, 384), 'rw_a0': (2, 2, 384), 'rw_gate_up': (2, 128, 384), 'rw_k_k': (2, 384),
    'rw_k_a': (2, 384), 'rw_r_k': (2, 384), 'rw_gn_g': (2, 384), 'rw_gn_b': (2, 384),
    'mla_q_norm': (2, 256), 'mla_w_uq': (2, 256, 576), 'mla_kv_norm': (2, 128), 'mla_w_ukv': (2, 128, 768),
    'mla_q_gain': (2, 96), 'mla_k_gain': (2, 96), 'hy_conv': (2, 3, 768), 'hy_w1': (2, 33, 64), 'hy_b1': (2, 64),
    'hy_freq1': (2, 64), 'hy_w2': (2, 64, 64), 'hy_b2': (2, 64), 'hy_freq2': (2, 64), 'hy_w3': (2, 64, 1024),
    'hy_b3': (2, 1024), 'hy_bias': (2, 2, 256), 'ffn_norm': (2, 1024), 'peer_wq': (2, 1024, 2048),
    'peer_keys': (2, 8, 2, 128, 128), 'peer_u': (2, 16384, 1024), 'peer_v': (2, 16384, 1024),
}


class Prog:
    def __init__(self, dbg=None, nlayers=DEPTH):
        self.kb = KB()
        self.nc = self.kb.nc
        self.dbg = dbg
        self.nlayers = nlayers
        nc = self.nc
        self.din = {}
        self.din['x'] = Tl(nc.dram_tensor("x", [SEQ, D], F32, kind="ExternalInput").ap(), "x")
        self.din['ctx'] = Tl(nc.dram_tensor("ctx", [CTX, D], F32, kind="ExternalInput").ap(), "ctx")
        self.din['c2'] = Tl(nc.dram_tensor("c2", [2, D], F32, kind="ExternalInput").ap(), "c2")
        for n in INPUT_NAMES:
            self.din[n] = Tl(nc.dram_tensor(n, list(WEIGHT_SHAPES[n]), F32, kind="ExternalInput").ap(), n)
        self.consts = {}
        self.scr = {}

    def const_in(self, name, shape, dt=F32):
        t = Tl(self.nc.dram_tensor(name, list(shape), dt, kind="ExternalInput").ap(), name)
        self.consts[name] = t
        return t

    def scratch(self, name, shape, dt=F32):
        t = Tl(self.nc.dram_tensor(name, list(shape), dt, kind="Internal").ap(), name)
        self.scr[name] = t
        return t

    def out_tensor(self, name, shape, dt=F32):
        t = Tl(self.nc.dram_tensor(name, list(shape), dt, kind="ExternalOutput").ap(), name)
        return t

    def phase_mod(self, li):
        kb, nc = self.kb, self.nc
        modrow = self.scr['modrow']
        with ExitStack() as es:
            cT = sb(es, kb, "cT", [128, 8, 2], F32)
            scT = sb(es, kb, "scT", [128, 8, 2], F32)
            sig = sb(es, kb, "sigT", [128, 8, 2], F32)
            mw = [sb(es, kb, f"mw{i}", [128, 3072], F32) for i in range(2)]
            modb = sb(es, kb, "modb", [2, 6144], F32)
            modsb = sb(es, kb, "modsb", [2, 6144], F32)
            pm = ps(es, kb, "pm", [2, 3072], F32)
            c2 = self.din['c2']
            for r_ in range(2):
                kb.dma("sp", cT[:, :, r_], c2.t[r_].rearrange("(k p) -> p k", p=128), r=[c2.b], w=[cT.b],
                       allow_slow_non_contiguous=True)
            for r_ in range(2):
                kb.dma("sp", modb[r_:r_ + 1, :], self.din['mod_b'].t[li:li + 1, :], w=[modb.b])
            kb.op("act", lambda e: e.activation(out=sig[:], in_=cT[:], func=AF.Sigmoid), r=[cT.b], w=[sig.b])
            kb.op("dve", lambda e: e.tensor_tensor(out=scT[:], in0=cT[:], in1=sig[:], op=ALU.mult),
                  r=[cT.b, sig.b], w=[scT.b])
            it = 0
            for half in range(2):
                for k in range(8):
                    m = mw[it % 2]
                    it += 1
                    kb.dma("sp", m[:], self.din['mod_w'].t[li, k * 128:(k + 1) * 128, half * 3072:(half + 1) * 3072],
                           w=[m.b])
                    for n in range(6):
                        kb.op("pe", lambda e: e.matmul(pm[:, n * 512:(n + 1) * 512], lhsT=scT[:, k, :],
                                                       rhs=m[:, n * 512:(n + 1) * 512], start=(k == 0), stop=(k == 7)),
                              r=[scT.b, m.b], w=[pm.b])
                kb.op("dve", lambda e: e.tensor_tensor(out=modsb[:, half * 3072:(half + 1) * 3072], in0=pm[:],
                                                       in1=modb[:, half * 3072:(half + 1) * 3072], op=ALU.add),
                      r=[pm.b, modb.b], w=[modsb.b])
            kb.dma("sp", modrow[:, :], modsb[:], r=[modsb.b], w=[modrow.b])
        barrier(kb)

    def norm_to_xsT(self, es, li, xsT, tiles, norm_name, sc_idx, sh_idx, ident):
        kb, nc = self.kb, self.nc
        modrow = self.scr['modrow']
        with ExitStack() as es2:
            G = [sb(es2, kb, f"Gbc{w_}", [128, D], F32) for w_ in range(2)]
            SH = [sb(es2, kb, f"SHbc{w_}", [128, D], F32) for w_ in range(2)]
            gn = sb(es2, kb, "gnbc", [128, D], F32)
            kb.dma("sp", gn[:], self.din[norm_name].t[li:li + 1, :].partition_broadcast(128), w=[gn.b])
            for w_ in range(2):
                kb.dma("sp", G[w_][:], modrow.t[w_:w_ + 1, sc_idx * D:(sc_idx + 1) * D].partition_broadcast(128),
                       r=[modrow.b], w=[G[w_].b])
                kb.dma("sp", SH[w_][:], modrow.t[w_:w_ + 1, sh_idx * D:(sh_idx + 1) * D].partition_broadcast(128),
                       r=[modrow.b], w=[SH[w_].b])
                kb.op("dve", lambda e: e.scalar_tensor_tensor(out=G[w_][:], in0=G[w_][:], scalar=1.0, in1=gn[:],
                                                              op0=ALU.add, op1=ALU.mult),
                      r=[gn.b], w=[G[w_].b])
            xt = [sb(es2, kb, f"xt{i}", [128, D], F32) for i in range(2)]
            junk = sb(es2, kb, "junk", [128, D], F32)
            ss = [sb(es2, kb, f"ss{i}", [128, 1], F32) for i in range(2)]
            xs = [sb(es2, kb, f"xs{i}", [128, D], F32) for i in range(2)]
            xsb = [sb(es2, kb, f"xsb{i}", [128, D], BF16) for i in range(2)]
            tp = [ps(es2, kb, f"tp{i}", [128, 8, 128], BF16) for i in range(2)]
            for i, (src, r0, c0, which) in enumerate(tiles):
                b = i % 2
                kb.dma("sp", xt[b][:], src.t[r0:r0 + 128, :], r=[src.b], w=[xt[b].b])
                kb.op("act", lambda e: e.activation(out=junk[:], in_=xt[b][:], func=AF.Square, accum_out=ss[b][:]),
                      r=[xt[b].b], w=[junk.b, ss[b].b])
                kb.op("dve", lambda e: e.tensor_scalar(out=ss[b][:], in0=ss[b][:], scalar1=1.0 / D, scalar2=NORM_EPS,
                                                       op0=ALU.mult, op1=ALU.add), w=[ss[b].b])
                kb.op("act", lambda e: e.activation(out=ss[b][:], in_=ss[b][:], func=AF.Sqrt), w=[ss[b].b])
                kb.op("dve", lambda e: e.reciprocal(out=ss[b][:], in_=ss[b][:]), w=[ss[b].b])
                kb.op("dve", lambda e: e.scalar_tensor_tensor(out=xs[b][:], in0=xt[b][:], scalar=ss[b][:, 0:1],
                                                              in1=G[which][:], op0=ALU.mult, op1=ALU.mult),
                      r=[xt[b].b, ss[b].b, G[which].b], w=[xs[b].b])
                kb.op("pool", lambda e: e.tensor_tensor(out=xsb[b][:], in0=xs[b][:], in1=SH[which][:], op=ALU.add),
                      r=[xs[b].b, SH[which].b], w=[xsb[b].b])
                for k in range(8):
                    kb.op("pe", lambda e: e.transpose(tp[b][:, k, :], xsb[b][:, k * 128:(k + 1) * 128], ident[:]),
                          r=[xsb[b].b, ident.b], w=[tp[b].b])
                kb.op("act", lambda e: e.copy(out=xsT[:, :, c0:c0 + 128], in_=tp[b][:]),
                      r=[tp[b].b], w=[xsT.b])

    def make_ident(self, es, dt, name):
        kb = self.kb
        ident = sb(es, kb, name, [128, 128], dt)
        src = self.consts['ident_f32']
        if dt == F32:
            kb.dma("sp", ident[:], src.t[:, :], w=[ident.b])
        else:
            with ExitStack() as es2:
                tmp = sb(es2, kb, name + "_tmp", [128, 128], F32)
                kb.dma("sp", tmp[:], src.t[:, :], w=[tmp.b])
                kb.op("dve", lambda e: e.tensor_copy(out=ident[:], in_=tmp[:]), r=[tmp.b], w=[ident.b])
                kb.wait_all("sp", [tmp.b])
                barrier(kb)
        return ident

    def phase_inproj(self, li, first):
        kb, nc = self.kb, self.nc
        zT = self.scr['zT']
        krope = self.scr['krope_tm']
        xsrc_l = self.din['x'] if first else self.scr['xcur']
        xsrc_c = self.din['ctx'] if first else self.scr['xcur']
        tiles = []
        for i in range(32):
            tiles.append((xsrc_l, i * 128, i * 128, 0))
        for i in range(2):
            tiles.append((xsrc_c, (i * 128) if first else (SEQ + i * 128), SEQ + i * 128, 1))
        with ExitStack() as es:
            identb = self.make_ident(es, BF16, "identb")
            xsT = sb(es, kb, "xsT", [128, 8, T], BF16)
            self.norm_to_xsT(es, li, xsT, tiles, 'mix_norm', 1, 0, identb)
            barrier(kb)
            W = sb(es, kb, "Win", [128, 8, IN_PROJ], BF16)
            wst = [sb(es, kb, f"wst{i}", [128, IN_PROJ], F32) for i in range(2)]
            for k in range(8):
                kb.dma("sp", wst[k % 2][:], self.din['w_in'].t[li, k * 128:(k + 1) * 128, :], w=[wst[k % 2].b])
                kb.op("pool", lambda e: e.tensor_copy(out=W[:, k, :], in_=wst[k % 2][:]), r=[wst[k % 2].b], w=[W.b])
            cw_rw = sb(es, kb, "cw_rw", [128, 11, 3], F32)
            cw_hy = sb(es, kb, "cw_hy", [128, 6, 3], F32)
            for j in range(3):
                kb.dma("sp", cw_rw[:, :, j], self.din['rw_conv'].t[li, j].rearrange("(c p) -> p c", p=128),
                       w=[cw_rw.b], allow_slow_non_contiguous=True)
                kb.dma("sp", cw_hy[:, :, j], self.din['hy_conv'].t[li, j].rearrange("(c p) -> p c", p=128),
                       w=[cw_hy.b], allow_slow_non_contiguous=True)
            chunks = []
            for c in range(11):
                chunks.append((c * 128, 128, cw_rw, c))
            for c in range(3):
                chunks.append((RW_PROJ + c * 128, 128, None, None))
            for c in range(6):
                chunks.append((RW_PROJ + MLA_PROJ + c * 128, 128, cw_hy, c))
            PW = T + 3
            prow = [sb(es, kb, f"prow{i}", [128, PW], F32) for i in range(2)]
            zrow = [sb(es, kb, f"zrow{i}", [128, PW], F32) for i in range(2)]
            pp = [ps(es, kb, f"pp{i}", [128, 512], F32) for i in range(4)]
            for i in range(2):
                kb.op("pool", lambda e: e.memset(prow[i][:], 0.0), w=[prow[i].b])
            npp = 0
            for ci, (c0, csz, cw, cidx) in enumerate(chunks):
                pr = prow[ci % 2]
                zr = zrow[ci % 2]
                for tb in range(9):
                    t0 = tb * 512
                    n = 512 if tb < 8 else CTX
                    p_ = pp[npp % 4]
                    npp += 1
                    for k in range(8):
                        kb.op("pe", lambda e: e.matmul(p_[:csz, :n], lhsT=W[:, k, c0:c0 + csz],
                                                       rhs=xsT[:, k, t0:t0 + n], start=(k == 0), stop=(k == 7)),
                              r=[W.b, xsT.b], w=[p_.b])
                    o0 = 1 + t0 if tb < 8 else SEQ + 2
                    if cw is None:
                        kb.op("act", lambda e: e.copy(out=zr[:csz, o0:o0 + n], in_=p_[:csz, :n]), r=[p_.b], w=[zr.b])
                    else:
                        kb.op("act", lambda e: e.copy(out=pr[:csz, o0:o0 + n], in_=p_[:csz, :n]), r=[p_.b], w=[pr.b])
                if cw is not None:
                    kb.op("act", lambda e: e.activation(out=zr[:, :], in_=pr[:, :], func=AF.Copy,
                                                        scale=cw[:, cidx, 1:2]), r=[pr.b, cw.b], w=[zr.b])
                    kb.op("dve", lambda e: e.scalar_tensor_tensor(out=zr[:, 1:PW], in0=pr[:, 0:PW - 1],
                                                                  scalar=cw[:, cidx, 0:1], in1=zr[:, 1:PW],
                                                                  op0=ALU.mult, op1=ALU.add),
                          r=[pr.b, cw.b], w=[zr.b])
                    kb.op("dve", lambda e: e.scalar_tensor_tensor(out=zr[:, 0:PW - 1], in0=pr[:, 1:PW],
                                                                  scalar=cw[:, cidx, 2:3], in1=zr[:, 0:PW - 1],
                                                                  op0=ALU.mult, op1=ALU.add),
                          r=[pr.b, cw.b], w=[zr.b])
                kb.dma("sp", zT.t[ci, :, 0:SEQ], zr[:, 1:1 + SEQ], r=[zr.b], w=[zT.b])
                kb.dma("sp", zT.t[ci, :, SEQ:T], zr[:, SEQ + 2:SEQ + 2 + CTX], r=[zr.b], w=[zT.b])
            kr = sb(es, kb, "kr_sb", [128, NT, 32], F32)
            c0 = RW_PROJ + 384
            for tt in range(NT):
                p_ = pp[npp % 4]
                npp += 1
                for k in range(8):
                    kb.op("pe", lambda e: e.matmul(p_[:, :32], lhsT=xsT[:, k, tt * 128:(tt + 1) * 128],
                                                   rhs=W[:, k, c0:c0 + 32], start=(k == 0), stop=(k == 7)),
                          r=[W.b, xsT.b], w=[p_.b])
                kb.op("act", lambda e: e.copy(out=kr[:, tt, :], in_=p_[:, :32]), r=[p_.b], w=[kr.b])
            kb.dma("sp", krope.t.rearrange("(n p) f -> p n f", p=128), kr[:], r=[kr.b], w=[krope.b])
        barrier(kb)


    def fm_to_tm(self, src, dst_ap_pnf, identf, stg, tpp, ntiles, state):
        kb = self.kb
        for n0 in range(0, ntiles, 4):
            nn = min(4, ntiles - n0)
            tp = tpp[state[0] % 2]
            state[0] += 1
            for j in range(nn):
                kb.op("pe", lambda e: e.transpose(tp[:, j, :], src[:, (n0 + j) * 128:(n0 + j + 1) * 128], identf[:]),
                      r=[src.b, identf.b], w=[tp.b])
            eng = "act" if (state[0] % 2) else "dve"
            if eng == "act":
                kb.op("act", lambda e: e.copy(out=stg[:, n0:n0 + nn, :], in_=tp[:, 0:nn, :]), r=[tp.b], w=[stg.b])
            else:
                kb.op("dve", lambda e: e.tensor_copy(out=stg[:, n0:n0 + nn, :], in_=tp[:, 0:nn, :]), r=[tp.b], w=[stg.b])

    def phase_rw_prep(self, li):
        kb, nc = self.kb, self.nc
        zT = self.scr['zT']
        S = self.scr
        with ExitStack() as es:
            identf = self.make_ident(es, F32, "identf")
            bones = sb(es, kb, "bones", [128, 128], F32)
            kb.dma("sp", bones[:], self.consts['blockones'].t[:, :], w=[bones.b])
            dup = sb(es, kb, "dup", [64, 2, 384], F32)
            aup = sb(es, kb, "aup", [64, 2, 384], F32)
            alo = sb(es, kb, "alo", [64, T], F32)
            gup = sb(es, kb, "gup", [128, 384], F32)
            kb.dma("sp", dup[:], self.din['rw_decay_up'].t[li].rearrange("d r c -> r d c"), w=[dup.b])
            kb.dma("sp", aup[:, :, :], self.din['rw_a_up'].t[li].rearrange("d r c -> r d c"), w=[aup.b])
            kb.dma("sp", gup[:], self.din['rw_gate_up'].t[li], w=[gup.b])
            d0c = sb(es, kb, "d0c", [128, 2, 3], F32)
            a0c = sb(es, kb, "a0c", [128, 2, 3], F32)
            kkc = sb(es, kb, "kkc", [128, 3], F32)
            kac = sb(es, kb, "kac", [128, 3], F32)
            omka = sb(es, kb, "omka", [128, 3], F32)
            for d in range(2):
                kb.dma("sp", d0c[:, d, :], self.din['rw_decay0'].t[li, d].rearrange("(g p) -> p g", p=128),
                       w=[d0c.b], allow_slow_non_contiguous=True)
                kb.dma("sp", a0c[:, d, :], self.din['rw_a0'].t[li, d].rearrange("(g p) -> p g", p=128),
                       w=[a0c.b], allow_slow_non_contiguous=True)
            kb.dma("sp", kkc[:], self.din['rw_k_k'].t[li].rearrange("(g p) -> p g", p=128), w=[kkc.b],
                   allow_slow_non_contiguous=True)
            kb.dma("sp", kac[:], self.din['rw_k_a'].t[li].rearrange("(g p) -> p g", p=128), w=[kac.b],
                   allow_slow_non_contiguous=True)
            kb.op("dve", lambda e: e.tensor_scalar(out=omka[:], in0=kac[:], scalar1=-1.0, scalar2=1.0,
                                                   op0=ALU.mult, op1=ALU.add), r=[kac.b], w=[omka.b])
            da = sb(es, kb, "da", [64, T], F32)
            gs = sb(es, kb, "gs", [128, T], F32)
            kg = sb(es, kb, "kg", [128, T], F32)
            t1 = sb(es, kb, "t1", [128, T], F32)
            kkg = sb(es, kb, "kkg", [128, T], F32)
            adg = sb(es, kb, "adg", [128, T], F32)
            tA = sb(es, kb, "tA", [128, T], F32)
            tB = sb(es, kb, "tB", [128, T], F32)
            stg = [sb(es, kb, f"stg{i}", [128, NT, 128], F32) for i in range(2)]
            pp = [ps(es, kb, f"pq{i}", [128, 512], F32) for i in range(4)]
            tpp = [ps(es, kb, f"tq{i}", [128, 4, 128], F32) for i in range(2)]
            st = [0]
            nst = [0]
            npp = [0]

            def to_tm(src, dst_tl, col0, extra=None):
                sg = stg[nst[0] % 2]
                nst[0] += 1
                self.fm_to_tm(src, None, identf, sg, tpp, NT, st)
                if dst_tl is not None:
                    kb.dma("sp", dst_tl.rearrange("(n p) f -> p n f", p=128)[:, :, col0:col0 + 128], sg[:],
                           r=[sg.b], w=[self._prep_buf])
                if extra is not None:
                    for (dst_ap, c0, c1) in extra:
                        kb.dma("pool", dst_ap, sg[:, :, c0:c1], r=[sg.b], w=[self._prep_buf])

            def rows_dst(ap_T_384, g, c0, c1):
                return ap_T_384.rearrange("(n p) (g c) -> p n g c", p=128, c=128)[:, :, g, c0:c1]

            def blocks(fn):
                for tb in range(9):
                    t0 = tb * 512
                    n = 512 if tb < 8 else CTX
                    p_ = pp[npp[0] % 4]
                    npp[0] += 1
                    fn(p_, t0, n)

            kb.dma("sp", da[0:64, :], zT.t[9, 0:64, :], r=[zT.b], w=[da.b])
            kb.dma("sp", alo[:, :], zT.t[9, 64:128, :], r=[zT.b], w=[alo.b])
            kb.dma("sp", gs[:], zT.t[10], r=[zT.b], w=[gs.b])
            kb.op("act", lambda e: e.activation(out=da[0:64, :], in_=da[0:64, :], func=AF.Tanh), w=[da.b])
            kb.op("act", lambda e: e.activation(out=gs[:], in_=gs[:], func=AF.Sigmoid), w=[gs.b])
            import os
            PSTOP = int(os.environ.get("PREP_STOP", "99"))
            for g in range(3):
                gc = slice(g * 128, (g + 1) * 128)
                kb.dma("sp", tA[:], zT.t[g], r=[zT.b], w=[tA.b])
                kb.dma("pool", S['rb_fm'].t[g], tA[:], r=[tA.b], w=[self._prep_buf])
                to_tm(tA, S['r_tm'].t, g * 128)
                if PSTOP == 1:
                    break
                kb.dma("sp", tB[:], zT.t[6 + g], r=[zT.b], w=[tB.b])
                to_tm(tB, S['v_tm'].t, g * 128,
                      extra=[(S['v6_tm'].t[g * 2 + par].rearrange("(n p) (gg c) -> p n gg c", p=128, c=64)[:, :, g, :],
                              par * 64, par * 64 + 64) for par in range(2)])
                def fg(p_, t0, n):
                    kb.op("pe", lambda e: e.matmul(p_[:, :n], lhsT=gup[:, gc], rhs=gs[:, t0:t0 + n], start=True, stop=True),
                          r=[gup.b, gs.b], w=[p_.b])
                    kb.op("act", lambda e: e.copy(out=tA[:, t0:t0 + n], in_=p_[:, :n]), r=[p_.b], w=[tA.b])
                blocks(fg)
                to_tm(tA, S['g_tm'].t, g * 128)
                if PSTOP == 2:
                    break
                kb.dma("sp", kg[:], zT.t[3 + g], r=[zT.b], w=[kg.b])
                to_tm(kg, S['k_tm'].t, g * 128)
                kb.op("act", lambda e: e.activation(out=t1[:], in_=kg[:], func=AF.Copy, scale=kkc[:, g:g + 1]),
                      r=[kg.b, kkc.b], w=[t1.b])
                kb.op("dve", lambda e: e.tensor_tensor(out=tB[:], in0=t1[:], in1=t1[:], op=ALU.mult), r=[t1.b], w=[tB.b])
                def fk(p_, t0, n):
                    kb.op("pe", lambda e: e.matmul(p_[:, :n], lhsT=bones[:], rhs=tB[:, t0:t0 + n], start=True, stop=True),
                          r=[bones.b, tB.b], w=[p_.b])
                    kb.op("dve", lambda e: e.tensor_scalar(out=kkg[:, t0:t0 + n], in0=p_[:, :n], scalar1=1e-12, scalar2=None,
                                                           op0=ALU.add), r=[p_.b], w=[kkg.b])
                blocks(fk)
                kb.op("act", lambda e: e.activation(out=kkg[:], in_=kkg[:], func=AF.Sqrt), w=[kkg.b])
                kb.op("dve", lambda e: e.reciprocal(out=kkg[:], in_=kkg[:]), w=[kkg.b])
                kb.op("dve", lambda e: e.tensor_tensor(out=kkg[:], in0=kkg[:], in1=t1[:], op=ALU.mult), r=[t1.b], w=[kkg.b])
                kb.dma("pool", S['kk_fm'].t[g], kkg[:], r=[kkg.b], w=[self._prep_buf])
                to_tm(kkg, None, 0, extra=[(S['kk6_tm'].t[g * 2 + par].rearrange("(n p) c -> p n c", p=128)[:, :, par * 64:par * 64 + 64],
                                            par * 64, par * 64 + 64) for par in range(2)])
                if PSTOP == 3:
                    break
                for d in range(2):
                    def fd(p_, t0, n):
                        kb.op("pe", lambda e: e.matmul(p_[:, :n], lhsT=dup[0:64, d, gc], rhs=da[0:64, t0:t0 + n],
                                                       start=True, stop=True), r=[dup.b, da.b], w=[p_.b])
                        kb.op("act", lambda e: e.activation(out=tA[:, t0:t0 + n], in_=p_[:, :n], func=AF.Sigmoid,
                                                            bias=d0c[:, d, g:g + 1]), r=[p_.b, d0c.b], w=[tA.b])
                    blocks(fd)
                    kb.op("act", lambda e: e.activation(out=tA[:], in_=tA[:], func=AF.Exp, scale=-0.6065306597),
                          w=[tA.b])
                    kb.dma("sp", S['w_fm'].t[d, g], tA[:], r=[tA.b], w=[self._prep_buf])
                    def fa(p_, t0, n):
                        kb.op("pe", lambda e: e.matmul(p_[:, :n], lhsT=aup[:, d, gc], rhs=alo[:, t0:t0 + n],
                                                       start=True, stop=True), r=[aup.b, alo.b], w=[p_.b])
                        kb.op("act", lambda e: e.activation(out=adg[:, t0:t0 + n], in_=p_[:, :n], func=AF.Sigmoid,
                                                            bias=a0c[:, d, g:g + 1]), r=[p_.b, a0c.b], w=[adg.b])
                    blocks(fa)
                    kb.op("dve", lambda e: e.tensor_scalar(out=tA[:], in0=adg[:], scalar1=kac[:, g:g + 1],
                                                           scalar2=omka[:, g:g + 1], op0=ALU.mult, op1=ALU.add),
                          r=[adg.b, kac.b, omka.b], w=[tA.b])
                    kb.op("dve", lambda e: e.tensor_tensor(out=tA[:], in0=tA[:], in1=kg[:], op=ALU.mult),
                          r=[kg.b], w=[tA.b])
                    to_tm(tA, None, 0, extra=[(S['krep6_tm'].t[d, g * 2 + par].rearrange("(n p) c -> p n c", p=128)[:, :, par * 64:par * 64 + 64],
                                               par * 64, par * 64 + 64) for par in range(2)])
                    kb.op("dve", lambda e: e.scalar_tensor_tensor(out=tB[:], in0=adg[:], scalar=-1.0, in1=kkg[:],
                                                                  op0=ALU.mult, op1=ALU.mult),
                          r=[adg.b, kkg.b], w=[tB.b])
                    to_tm(tB, None, 0, extra=[(rows_dst(S['akk6_tm'].t[d, g * 2 + par], g, par * 64, par * 64 + 64),
                                               par * 64, par * 64 + 64) for par in range(2)])
        barrier(kb)

    def scr_b(self, ap):
        return self._prep_buf

    def phase_zero_init(self):
        kb = self.kb
        with ExitStack() as es:
            z = sb(es, kb, "zeros", [128, 8192], BF16)
            kb.op("pool", lambda e: e.memset(z[:], 0.0), w=[z.b])
            for (flat, tot) in ((self.scr['akk6_tm'].t.rearrange("d r t f -> (d r t f)"), 2 * 6 * T * 384),
                                (self.scr['krep6_tm'].t.rearrange("d r t f -> (d r t f)"), 2 * 6 * T * 128),
                                (self.scr['kk6_tm'].t.rearrange("r t f -> (r t f)"), 6 * T * 128),
                                (self.scr['v6_tm'].t.rearrange("r t f -> (r t f)"), 6 * T * 192)):
                per = 128 * 8192
                off = 0
                while off < tot:
                    n = min(per, tot - off)
                    cols = n // 128
                    kb.dma("sp", flat[off:off + n].rearrange("(p c) -> p c", p=128), z[:, :cols], r=[z.b],
                           w=[self._prep_buf])
                    off += n
        barrier(kb)

    def phase_rw_scan(self, li, nsteps=None):
        kb, nc = self.kb, self.nc
        S = self.scr
        TCF, TCR = 64, 8
        import os
        PARTS = os.environ.get("SCAN_PARTS", "SCUDYO")
        total = T if nsteps is None else nsteps
        pb = self._prep_buf
        with ExitStack() as es:
            St = sb(es, kb, "St", [128, 2, 3, 64], F32, nsub=2)
            kb.op("dve", lambda e: e.memset(St[:], 0.0), w=[St.sub[0], St.sub[1]])
            Stb = sb(es, kb, "Stb", [128, 2, 3, 64], BF16, nsub=2)
            kb.op("dve", lambda e: e.memset(Stb[:], 0.0), w=[Stb.sub[0], Stb.sub[1]])
            Wt = [[sb(es, kb, f"Wt{d}{c}", [128, 3, TCF], F32) for c in range(2)] for d in range(2)]
            KK = [[sb(es, kb, f"KK{d}{c}", [128, 3, 2, TCF], BF16) for c in range(2)] for d in range(2)]
            RR = [[sb(es, kb, f"RR{d}{c}", [128, 3, 2, TCF], BF16) for c in range(2)] for d in range(2)]
            L4 = [[sb(es, kb, f"L4{d}{c}", [4, TCR, 3, 128], BF16) for c in range(2)] for d in range(2)]
            RV = [[sb(es, kb, f"RV{d}{c}", [4, TCR, 3, 64], BF16) for c in range(2)] for d in range(2)]
            YF = [sb(es, kb, f"YF{d}", [64, 3, 2, TCF], F32) for d in range(2)]
            for d in range(2):
                for c in range(2):
                    for tl in (KK[d][c], RR[d][c]):
                        kb.op("pool", lambda e: e.memset(tl[:], 0.0), w=[tl.b])
            pskb = [ps(es, kb, f"psk{d}", [128, 512], F32) for d in range(2)]
            pdb = [ps(es, kb, f"pd{d}", [128, 512], F32) for d in range(2)]
            pyb = [ps(es, kb, f"py{d}", [128, 512], F32) for d in range(2)]
            psk_v = [pskb[d][0:2, 0:192].rearrange("p (g c) -> p g c", g=3) for d in range(2)]
            pd_v = [pdb[d][:, 0:192].rearrange("p (g c) -> p g c", g=3) for d in range(2)]
            py_v = [pyb[d][0:64, 0:6 * TCF].rearrange("p (g h t) -> p g h t", g=3, h=2) for d in range(2)]
            kk_pgt = S['kk_fm'].t.rearrange("g p t -> p g t")
            r_pgt = S['rb_fm'].t.rearrange("g p t -> p g t")

            def seg_time(s, d):
                if s < CTX:
                    return SEQ + s if d == 0 else T - 1 - s
                return s - CTX if d == 0 else SEQ - 1 - (s - CTX)

            def chunk_t0(s0, n, d):
                return seg_time(s0, d) if d == 0 else seg_time(s0 + n - 1, d)

            for s in range(total):
                cf, jf = divmod(s, TCF)
                cr, jr = divmod(s, TCR)
                cbf, cbr = cf % 2, cr % 2
                if jf == 0:
                    for d in range(2):
                        t0 = chunk_t0(s, TCF, d)
                        ts = slice(t0, t0 + TCF)
                        kb.dma("sp", Wt[d][cbf][:], S['w_fm'].t[d].rearrange("g p t -> p g t")[:, :, ts], r=[pb],
                               w=[Wt[d][cbf].b])
                        for h in range(2):
                            hp = slice(h * 64, (h + 1) * 64)
                            kb.dma("sp", KK[d][cbf][hp, :, h, :], kk_pgt[hp, :, ts], r=[pb], w=[KK[d][cbf].b])
                            kb.dma("sp", RR[d][cbf][hp, :, h, :], r_pgt[hp, :, ts], r=[pb], w=[RR[d][cbf].b])
                if jr == 0:
                    for d in range(2):
                        t0 = chunk_t0(s, TCR, d)
                        ts = slice(t0, t0 + TCR)
                        kb.dma("sp", L4[d][cbr][:].rearrange("r t g c -> r t (g c)"), S['L4_tm'].t[d, :, ts, :], r=[pb], w=[L4[d][cbr].b])
                        kb.dma("sp", RV[d][cbr][2:4, :, :, :], S['V2_tm'].t[:, ts, :, :], r=[pb], w=[RV[d][cbr].b])
                lf = [jf, TCF - 1 - jf]
                lr = [jr, TCR - 1 - jr]
                for d in range(2):
                    for g in range(3):
                        if 'S' not in PARTS:
                            continue
                        kb.op("pe", lambda e: e.matmul(psk_v[d][:, g, :], lhsT=KK[d][cbf][:, g, :, lf[d]],
                                                       rhs=Stb[:, d, g, :], start=True, stop=True),
                              r=[KK[d][cbf].b, Stb.sub[d]], w=[pskb[d].b])
                for d in range(2):
                    if 'C' not in PARTS:
                        continue
                    kb.op("act", lambda e: e.copy(out=RV[d][cbr][0:2, lr[d], :, :], in_=psk_v[d][:, :, :]),
                          r=[pskb[d].b], w=[RV[d][cbr].b])
                for d in range(2):
                    for g in range(3):
                        if 'U' not in PARTS:
                            continue
                        kb.op("pe", lambda e: e.matmul(pd_v[d][:, g, :], lhsT=L4[d][cbr][:, lr[d], g, :],
                                                       rhs=RV[d][cbr][:, lr[d], g, :], start=True, stop=True),
                              r=[L4[d][cbr].b, RV[d][cbr].b], w=[pdb[d].b])
                for d in range(2):
                    for g in range(3):
                        if 'D' not in PARTS:
                            continue
                        kb.op("dve", lambda e: e.scalar_tensor_tensor(out=Stb[:, d, g, :], in0=St[:, d, g, :],
                                                                      scalar=Wt[d][cbf][:, g, lf[d]:lf[d] + 1],
                                                                      in1=pd_v[d][:, g, :], op0=ALU.mult, op1=ALU.add),
                              r=[Wt[d][cbf].b, pdb[d].b, St.sub[d]], w=[Stb.sub[d]])
                for d in range(2):
                    for g in range(3):
                        if 'D' not in PARTS:
                            continue
                        kb.op("dve", lambda e: e.scalar_tensor_tensor(out=St[:, d, g, :], in0=St[:, d, g, :],
                                                                      scalar=Wt[d][cbf][:, g, lf[d]:lf[d] + 1],
                                                                      in1=pd_v[d][:, g, :], op0=ALU.mult, op1=ALU.add),
                              r=[Wt[d][cbf].b, pdb[d].b], w=[St.sub[d]])
                for d in range(2):
                    for g in range(3):
                        if 'Y' not in PARTS:
                            continue
                        kb.op("pe", lambda e: e.matmul(py_v[d][:, g, :, lf[d]], lhsT=Stb[:, d, g, :],
                                                       rhs=RR[d][cbf][:, g, :, lf[d]], start=True, stop=True),
                              r=[RR[d][cbf].b, Stb.sub[d]], w=[pyb[d].b])
                if (jf == TCF - 1 or s == total - 1) and 'O' in PARTS:
                    s0 = cf * TCF
                    for d in range(2):
                        t0 = chunk_t0(s0, TCF, d)
                        ts = slice(t0, t0 + TCF)
                        kb.op("act", lambda e: e.copy(out=YF[d][:], in_=py_v[d]), r=[pyb[d].b], w=[YF[d].b])
                        kb.dma("sp", S['y_fm'].t[d].rearrange("g h v t -> v g h t")[:, :, :, ts].rearrange("v g h t -> v (g h) t"),
                               YF[d][:].rearrange("v g h t -> v (g h) t"), r=[YF[d].b], w=[S['y_fm'].b])
        barrier(kb)


    def emit_peer_convert(self, es0, li):
        kb = self.kb
        S = self.scr
        cbuf = [sb(es0, kb, f"pcv{i}", [128, 8192], BF16) for i in range(3)]
        n = 0
        for (src, j) in ((self.din['peer_u'], 0), (self.din['peer_v'], 1)):
            dst = S['puv_bf']
            for c in range(16):
                cb_ = cbuf[n % 3]
                n += 1
                kb.dma("pool", cb_[:], src.t[li, c * 1024:(c + 1) * 1024, :].rearrange("(p r) d -> p (r d)", p=128),
                       w=[cb_.b])
                r0 = li * 16384 + c * 1024
                kb.dma("pool", dst.t[r0:r0 + 1024, j, :].rearrange("(p r) d -> p r d", p=128),
                       cb_[:].rearrange("p (r d) -> p r d", r=8), r=[cb_.b], w=[dst.b])


    def phase_rw_scan2(self, li, nsteps=None):
        kb, nc = self.kb, self.nc
        S = self.scr
        TCF, TCR = 64, 8
        AHEAD = 2
        NSL = 4
        total = T if nsteps is None else nsteps
        pb = self._prep_buf
        with ExitStack() as es:
            identf = self.make_ident(es, F32, "identf_s")
            if USE_PEER and nsteps is None:
                self.emit_peer_convert(es, li)
            Stb = sb(es, kb, "Stb2", [128, 2, 3, 64], BF16, nsub=2)
            kb.op("dve", lambda e: e.memset(Stb[:], 0.0), w=[Stb.sub[0], Stb.sub[1]])
            Wt = [[sb(es, kb, f"Wt{d}{c}", [128, 3, TCF], F32) for c in range(2)] for d in range(2)]
            RR = [[sb(es, kb, f"RR{d}{c}", [128, 3, 2, TCF], BF16) for c in range(2)] for d in range(2)]
            KRW = [[sb(es, kb, f"KRW{d}{c}", [6, TCR, 128], BF16) for c in range(3)] for d in range(2)]
            LA = [[sb(es, kb, f"LA{d}{c}", [6, TCR, 384], BF16) for c in range(3)] for d in range(2)]
            LK = [[sb(es, kb, f"LK{d}{c}", [6, TCR, 128], BF16) for c in range(3)] for d in range(2)]
            VR = [[sb(es, kb, f"VR{d}{c}", [6, TCR, 192], BF16) for c in range(3)] for d in range(2)]
            YF = [sb(es, kb, f"YF{d}", [64, 3, 2, TCF], F32) for d in range(2)]
            Ab = sb(es, kb, "Abuf", [128, NSL * 6, 128], BF16, nsub=NSL * 2)
            for d in range(2):
                for c in range(2):
                    kb.op("pool", lambda e: e.memset(RR[d][c][:], 0.0), w=[RR[d][c].b])
            pA = [[ps(es, kb, f"pA{d}{i}", [128, 512], F32) for i in range(2)] for d in range(2)]
            pSb = [ps(es, kb, f"pS{d}", [128, 512], F32) for d in range(2)]
            pyb = [ps(es, kb, f"py{d}", [128, 512], F32) for d in range(2)]
            pA_v = [[pA[d][i][:, 0:384].rearrange("p (g c) -> p g c", g=3) for i in range(2)] for d in range(2)]
            pS_v = [pSb[d][:, 0:192].rearrange("p (g c) -> p g c", g=3) for d in range(2)]
            py_v = [pyb[d][0:64, 0:6 * TCF].rearrange("p (g h t) -> p g h t", g=3, h=2) for d in range(2)]
            r_pgt = S['rb_fm'].t.rearrange("g p t -> p g t")

            def seg_time(s, d):
                if s < CTX:
                    return SEQ + s if d == 0 else T - 1 - s
                return s - CTX if d == 0 else SEQ - 1 - (s - CTX)

            def chunk_t0(s0, n, d):
                return seg_time(s0, d) if d == 0 else seg_time(s0 + n - 1, d)

            def load_f(s):
                cbf = (s // TCF) % 2
                for d in range(2):
                    t0 = chunk_t0(s, TCF, d)
                    ts = slice(t0, t0 + TCF)
                    kb.dma("sp", Wt[d][cbf][:], S['w_fm'].t[d].rearrange("g p t -> p g t")[:, :, ts], r=[pb], w=[Wt[d][cbf].b])
                    for h in range(2):
                        hp = slice(h * 64, (h + 1) * 64)
                        kb.dma("sp", RR[d][cbf][hp, :, h, :], r_pgt[hp, :, ts], r=[pb], w=[RR[d][cbf].b])

            def load_r(s):
                cbr = (s // TCR) % 3
                for d in range(2):
                    t0 = chunk_t0(s, TCR, d)
                    ts = slice(t0, t0 + TCR)
                    kb.dma("sp", KRW[d][cbr][:], S['kk6_tm'].t[:, ts, :], r=[pb], w=[KRW[d][cbr].b])
                    kb.dma("sp", LA[d][cbr][:], S['akk6_tm'].t[d, :, ts, :], r=[pb], w=[LA[d][cbr].b])
                    kb.dma("sp", LK[d][cbr][:], S['krep6_tm'].t[d, :, ts, :], r=[pb], w=[LK[d][cbr].b])
                    kb.dma("sp", VR[d][cbr][:], S['v6_tm'].t[:, ts, :], r=[pb], w=[VR[d][cbr].b])

            def idx(s):
                jf = s % TCF
                jr = s % TCR
                return (s // TCF) % 2, (s // TCR) % 3, [jf, TCF - 1 - jf], [jr, TCR - 1 - jr]

            def build_A(s):
                if s >= total:
                    return
                if s % TCF == 0:
                    load_f(s)
                if s % TCR == 0:
                    load_r(s)
                cbf, cbr, lf, lr = idx(s)
                sl = s % NSL
                for d in range(2):
                    pa = pA[d][s % 2]
                    kb.op("pe", lambda e: e.matmul(pa[:, 0:384], lhsT=KRW[d][cbr][:, lr[d], :],
                                                   rhs=LA[d][cbr][:, lr[d], :], start=True, stop=True),
                          r=[KRW[d][cbr].b, LA[d][cbr].b], w=[pa.b])
                    for g in range(3):
                        kb.op("dve", lambda e: e.scalar_tensor_tensor(out=Ab[:, sl * 6 + d * 3 + g, :], in0=identf[:],
                                                                      scalar=Wt[d][cbf][:, g, lf[d]:lf[d] + 1],
                                                                      in1=pA_v[d][s % 2][:, g, :], op0=ALU.mult, op1=ALU.add),
                              r=[identf.b, Wt[d][cbf].b, pa.b], w=[Ab.sub[sl * 2 + d]])

            for s0 in range(AHEAD):
                build_A(s0)
            for s in range(total):
                build_A(s + AHEAD)
                cbf, cbr, lf, lr = idx(s)
                sl = s % NSL
                for d in range(2):
                    kb.op("pe", lambda e: e.matmul(pSb[d][:, 0:192], lhsT=LK[d][cbr][:, lr[d], :], rhs=VR[d][cbr][:, lr[d], :],
                                                   start=True, stop=False),
                          r=[LK[d][cbr].b, VR[d][cbr].b], w=[pSb[d].b])
                    for g in range(3):
                        kb.op("pe", lambda e: e.matmul(pS_v[d][:, g, :], lhsT=Ab[:, sl * 6 + d * 3 + g, :], rhs=Stb[:, d, g, :],
                                                       start=False, stop=(g == 2)),
                              r=[Ab.sub[sl * 2 + d], Stb.sub[d]], w=[pSb[d].b])
                for d in range(2):
                    kb.op("act", lambda e: e.copy(out=Stb[:, d, :, :], in_=pS_v[d]), r=[pSb[d].b], w=[Stb.sub[d]])
                for d in range(2):
                    for g in range(3):
                        kb.op("pe", lambda e: e.matmul(py_v[d][:, g, :, lf[d]], lhsT=Stb[:, d, g, :],
                                                       rhs=RR[d][cbf][:, g, :, lf[d]], start=True, stop=True),
                              r=[RR[d][cbf].b, Stb.sub[d]], w=[pyb[d].b])
                if (s % TCF) == TCF - 1 or s == total - 1:
                    s0 = (s // TCF) * TCF
                    for d in range(2):
                        t0 = chunk_t0(s0, TCF, d)
                        ts = slice(t0, t0 + TCF)
                        kb.op("act", lambda e: e.copy(out=YF[d][:], in_=py_v[d]), r=[pyb[d].b], w=[YF[d].b])
                        kb.dma("sp", S['y_fm'].t[d].rearrange("g h v t -> v g h t")[:, :, :, ts].rearrange("v g h t -> v (g h) t"),
                               YF[d][:].rearrange("v g h t -> v (g h) t"), r=[YF[d].b], w=[S['y_fm'].b])
        barrier(kb)

    def phase_rw_readout(self, li, need_ctx):
        kb, nc = self.kb, self.nc
        S = self.scr
        pb = self._prep_buf
        ntile = NT if need_ctx else SEQ // 128
        with ExitStack() as es:
            identf = self.make_ident(es, F32, "identf2")
            gng = sb(es, kb, "gng", [128, 384], F32)
            gnb = sb(es, kb, "gnb", [128, 384], F32)
            rkb = sb(es, kb, "rkb", [128, 384], F32)
            kb.dma("sp", gng[:], self.din['rw_gn_g'].t[li:li + 1, :].partition_broadcast(128), w=[gng.b])
            kb.dma("sp", gnb[:], self.din['rw_gn_b'].t[li:li + 1, :].partition_broadcast(128), w=[gnb.b])
            kb.dma("sp", rkb[:], self.din['rw_r_k'].t[li:li + 1, :].partition_broadcast(128), w=[rkb.b])
            yfa = [sb(es, kb, f"yfa{i}", [64, 6, 128], F32) for i in range(2)]
            yfb = [sb(es, kb, f"yfb{i}", [64, 6, 128], F32) for i in range(2)]
            rt = [sb(es, kb, f"rt{i}", [128, 384], F32) for i in range(2)]
            kt = [sb(es, kb, f"kt{i}", [128, 384], F32) for i in range(2)]
            vt = [sb(es, kb, f"vt{i}", [128, 384], F32) for i in range(2)]
            gt = [sb(es, kb, f"gt{i}", [128, 384], F32) for i in range(2)]
            yc = sb(es, kb, "yc", [128, 6, 64], F32)
            sq = sb(es, kb, "sqr", [128, 6, 64], F32)
            mu = sb(es, kb, "mu", [128, 6], F32)
            var = sb(es, kb, "var", [128, 6], F32)
            bon = sb(es, kb, "bon", [128, 6], F32)
            ot = [sb(es, kb, f"ot{i}", [128, 384], F32) for i in range(2)]
            pyt = [ps(es, kb, f"pyt{i}", [128, 512], F32) for i in range(2)]
            mix = S['mix_tm']
            for tt in range(ntile):
                b = tt % 2
                ts = slice(tt * 128, (tt + 1) * 128)
                kb.dma("sp", yfa[b][:], S['y_fm'].t[0].rearrange("g h v t -> v (g h) t")[:, :, ts], r=[S['y_fm'].b], w=[yfa[b].b])
                kb.dma("sp", yfb[b][:], S['y_fm'].t[1].rearrange("g h v t -> v (g h) t")[:, :, ts], r=[S['y_fm'].b], w=[yfb[b].b])
                kb.dma("sp", rt[b][:], S['r_tm'].t[ts, :], r=[pb], w=[rt[b].b])
                kb.dma("sp", kt[b][:], S['k_tm'].t[ts, :], r=[pb], w=[kt[b].b])
                kb.dma("sp", vt[b][:], S['v_tm'].t[ts, :], r=[pb], w=[vt[b].b])
                kb.dma("sp", gt[b][:], S['g_tm'].t[ts, :], r=[pb], w=[gt[b].b])
                kb.op("pool", lambda e: e.tensor_tensor(out=yfa[b][:], in0=yfa[b][:], in1=yfb[b][:], op=ALU.add),
                      r=[yfb[b].b], w=[yfa[b].b])
                pv = pyt[b][:, 0:384].rearrange("p (h c) -> p h c", h=6)
                for h in range(6):
                    kb.op("pe", lambda e: e.transpose(pv[:, h, :], yfa[b][:, h, :], identf[0:64, 0:64]),
                          r=[yfa[b].b, identf.b], w=[pyt[b].b])
                kb.op("dve", lambda e: e.tensor_reduce(out=mu[:], in_=pv, axis=AX.X, op=ALU.add), r=[pyt[b].b], w=[mu.b])
                kb.op("dve", lambda e: e.tensor_scalar(out=mu[:], in0=mu[:], scalar1=-1.0 / 64, scalar2=None, op0=ALU.mult),
                      w=[mu.b])
                kb.op("dve", lambda e: e.tensor_tensor(out=yc[:], in0=pv, in1=mu[:].unsqueeze(2).to_broadcast([128, 6, 64]),
                                                       op=ALU.add), r=[pyt[b].b, mu.b], w=[yc.b])
                kb.op("pool", lambda e: e.tensor_tensor(out=sq[:], in0=yc[:], in1=yc[:], op=ALU.mult), r=[yc.b], w=[sq.b])
                kb.op("dve", lambda e: e.tensor_reduce(out=var[:], in_=sq[:], axis=AX.X, op=ALU.add), r=[sq.b], w=[var.b])
                kb.op("dve", lambda e: e.tensor_scalar(out=var[:], in0=var[:], scalar1=1.0 / 64, scalar2=64e-5,
                                                       op0=ALU.mult, op1=ALU.add), w=[var.b])
                kb.op("act", lambda e: e.activation(out=var[:], in_=var[:], func=AF.Sqrt), w=[var.b])
                kb.op("dve", lambda e: e.reciprocal(out=var[:], in_=var[:]), w=[var.b])
                kb.op("dve", lambda e: e.tensor_tensor(out=yc[:], in0=yc[:], in1=var[:].unsqueeze(2).to_broadcast([128, 6, 64]),
                                                       op=ALU.mult), r=[var.b], w=[yc.b])
                ycf = yc[:].rearrange("p h c -> p (h c)")
                kb.op("dve", lambda e: e.tensor_tensor(out=ycf, in0=ycf, in1=gng[:], op=ALU.mult), r=[gng.b], w=[yc.b])
                kb.op("dve", lambda e: e.tensor_tensor(out=ycf, in0=ycf, in1=gnb[:], op=ALU.add), r=[gnb.b], w=[yc.b])
                kb.op("pool", lambda e: e.tensor_tensor(out=rt[b][:], in0=rt[b][:], in1=kt[b][:], op=ALU.mult),
                      r=[kt[b].b], w=[rt[b].b])
                kb.op("pool", lambda e: e.tensor_tensor(out=rt[b][:], in0=rt[b][:], in1=rkb[:], op=ALU.mult),
                      r=[rkb.b], w=[rt[b].b])
                kb.op("dve", lambda e: e.tensor_reduce(out=bon[:], in_=rt[b][:].rearrange("p (h c) -> p h c", h=6),
                                                       axis=AX.X, op=ALU.add), r=[rt[b].b], w=[bon.b])
                v3 = vt[b][:].rearrange("p (h c) -> p h c", h=6)
                kb.op("dve", lambda e: e.tensor_tensor(out=v3, in0=v3, in1=bon[:].unsqueeze(2).to_broadcast([128, 6, 64]),
                                                       op=ALU.mult), r=[bon.b], w=[vt[b].b])
                kb.op("dve", lambda e: e.tensor_tensor(out=ot[b][:], in0=ycf, in1=vt[b][:], op=ALU.add),
                      r=[yc.b, vt[b].b], w=[ot[b].b])
                kb.op("dve", lambda e: e.tensor_tensor(out=ot[b][:], in0=ot[b][:], in1=gt[b][:], op=ALU.mult),
                      r=[gt[b].b], w=[ot[b].b])
                kb.dma("sp", mix.t[ts, 0:384], ot[b][:], r=[ot[b].b], w=[mix.b])
        barrier(kb)

    def phase_outproj(self, li, first, need_ctx, final_out=None):
        kb, nc = self.kb, self.nc
        S = self.scr
        mix = S['mix_tm']
        modrow = S['modrow']
        ntile = NT if need_ctx else SEQ // 128
        with ExitStack() as es:
            identb = self.make_ident(es, BF16, "identb2")
            Wo = sb(es, kb, "Wo", [128, 8, D], BF16)
            wst = [sb(es, kb, f"wost{i}", [128, D], F32) for i in range(2)]
            for k in range(8):
                kb.dma("sp", wst[k % 2][:], self.din['w_out'].t[li, k * 128:(k + 1) * 128, :], w=[wst[k % 2].b])
                kb.op("pool", lambda e: e.tensor_copy(out=Wo[:, k, :], in_=wst[k % 2][:]), r=[wst[k % 2].b], w=[Wo.b])
            gm = [sb(es, kb, f"gm{w_}", [128, D], F32) for w_ in range(2)]
            for w_ in range(2):
                kb.dma("sp", gm[w_][:], modrow.t[w_:w_ + 1, 2 * D:3 * D].partition_broadcast(128), r=[modrow.b], w=[gm[w_].b])
            mt = [sb(es, kb, f"mt{i}", [128, D], F32) for i in range(2)]
            mtb = [sb(es, kb, f"mtb{i}", [128, D], BF16) for i in range(2)]
            mT = [sb(es, kb, f"mT{i}", [128, 8, 128], BF16) for i in range(2)]
            xt = [sb(es, kb, f"xo{i}", [128, D], F32) for i in range(2)]
            tmp = [sb(es, kb, f"xtmp{i}", [128, D], F32) for i in range(2)]
            tp = [ps(es, kb, f"otp{i}", [128, 8, 128], BF16) for i in range(2)]
            po = [ps(es, kb, f"po{i}", [128, 512], F32) for i in range(4)]
            for tt in range(ntile):
                b = tt % 2
                ts = slice(tt * 128, (tt + 1) * 128)
                which = 0 if tt < 32 else 1
                if first:
                    xsrc = self.din['x'] if tt < 32 else self.din['ctx']
                    xs_ap = xsrc.t[ts, :] if tt < 32 else xsrc.t[(tt - 32) * 128:(tt - 31) * 128, :]
                else:
                    xsrc = S['xcur']
                    xs_ap = xsrc.t[ts, :]
                kb.dma("sp", mt[b][:], mix.t[ts, :], r=[mix.b], w=[mt[b].b])
                kb.dma("sp", xt[b][:], xs_ap, r=[xsrc.b], w=[xt[b].b])
                kb.op("pool", lambda e: e.tensor_copy(out=mtb[b][:], in_=mt[b][:]), r=[mt[b].b], w=[mtb[b].b])
                for k in range(8):
                    kb.op("pe", lambda e: e.transpose(tp[b][:, k, :], mtb[b][:, k * 128:(k + 1) * 128], identb[:]),
                          r=[mtb[b].b, identb.b], w=[tp[b].b])
                kb.op("act", lambda e: e.copy(out=mT[b][:], in_=tp[b][:]), r=[tp[b].b], w=[mT[b].b])
                for n in range(2):
                    p_ = po[(2 * tt + n) % 4]
                    for k in range(8):
                        kb.op("pe", lambda e: e.matmul(p_[:, :], lhsT=mT[b][:, k, :], rhs=Wo[:, k, n * 512:(n + 1) * 512],
                                                       start=(k == 0), stop=(k == 7)), r=[mT[b].b, Wo.b], w=[p_.b])
                    kb.op("dve", lambda e: e.tensor_tensor(out=tmp[b][:, n * 512:(n + 1) * 512], in0=p_[:, :],
                                                           in1=gm[which][:, n * 512:(n + 1) * 512], op=ALU.mult),
                          r=[p_.b, gm[which].b], w=[tmp[b].b])
                kb.op("pool", lambda e: e.tensor_tensor(out=xt[b][:], in0=xt[b][:], in1=tmp[b][:], op=ALU.add),
                      r=[tmp[b].b], w=[xt[b].b])
                if final_out is not None and tt < 32:
                    kb.dma("sp", final_out.t[ts, :], xt[b][:], r=[xt[b].b], w=[final_out.b])
                else:
                    kb.dma("sp", S['xcur'].t[ts, :], xt[b][:], r=[xt[b].b], w=[S['xcur'].b])
        barrier(kb)

    def phase_mla_prep(self, li):
        kb, nc = self.kb, self.nc
        S = self.scr
        zT = S['zT']
        inv96 = 1.0 / math.sqrt(96.0)
        with ExitStack() as es:
            identb = self.make_ident(es, BF16, "identb3")
            ones = sb(es, kb, "ones_c", [128, 1], F32)
            kb.op("pool", lambda e: e.memset(ones[:], 1.0), w=[ones.b])
            wq_st = sb(es, kb, "wq_st", [128, 2, 576], F32)
            wkv_st = sb(es, kb, "wkv_st", [128, 768], F32)
            qn = sb(es, kb, "qn_c", [128, 2], F32)
            kvn = sb(es, kb, "kvn_c", [128, 1], F32)
            Wq = sb(es, kb, "Wq", [128, 2, 576], BF16)
            Wkv = sb(es, kb, "Wkv", [128, 768], BF16)
            kb.dma("sp", wq_st[:], self.din['mla_w_uq'].t[li].rearrange("(k p) n -> p k n", p=128), w=[wq_st.b])
            kb.dma("sp", wkv_st[:], self.din['mla_w_ukv'].t[li], w=[wkv_st.b])
            kb.dma("sp", qn[:], self.din['mla_q_norm'].t[li].rearrange("(k p) -> p k", p=128), w=[qn.b],
                   allow_slow_non_contiguous=True)
            kb.dma("sp", kvn[:], self.din['mla_kv_norm'].t[li].rearrange("(k p) -> p k", p=128), w=[kvn.b],
                   allow_slow_non_contiguous=True)
            for k in range(2):
                kb.op("dve", lambda e: e.tensor_scalar(out=Wq[:, k, :], in0=wq_st[:, k, :], scalar1=qn[:, k:k + 1],
                                                       scalar2=None, op0=ALU.mult), r=[wq_st.b, qn.b], w=[Wq.b])
            kb.op("dve", lambda e: e.tensor_scalar(out=Wkv[:], in0=wkv_st[:], scalar1=kvn[:, 0:1], scalar2=None,
                                                   op0=ALU.mult), r=[wkv_st.b, kvn.b], w=[Wkv.b])
            qg = sb(es, kb, "qg_bc", [128, 96], F32)
            kg_ = sb(es, kb, "kg_bc", [128, 96], F32)
            kb.dma("sp", qg[:], self.din['mla_q_gain'].t[li:li + 1, :].partition_broadcast(128), w=[qg.b])
            kb.dma("sp", kg_[:], self.din['mla_k_gain'].t[li:li + 1, :].partition_broadcast(128), w=[kg_.b])
            kb.op("dve", lambda e: e.tensor_scalar(out=qg[:], in0=qg[:], scalar1=inv96, scalar2=None, op0=ALU.mult),
                  w=[qg.b])
            cq = sb(es, kb, "cq", [128, 2, T], F32)
            ckv = sb(es, kb, "ckv", [128, T], F32)
            cqb = sb(es, kb, "cqb", [128, 2, T], BF16)
            ckvb = sb(es, kb, "ckvb", [128, T], BF16)
            for k in range(2):
                kb.dma("sp", cq[:, k, :], zT.t[11 + k], r=[zT.b], w=[cq.b])
            kb.dma("sp", ckv[:], zT.t[13], r=[zT.b], w=[ckv.b])
            kb.op("pool", lambda e: e.tensor_copy(out=cqb[:], in_=cq[:]), r=[cq.b], w=[cqb.b])
            kb.op("pool", lambda e: e.tensor_copy(out=ckvb[:], in_=ckv[:]), r=[ckv.b], w=[ckvb.b])
            kb.op("act", lambda e: e.activation(out=cq[:], in_=cq[:], func=AF.Square), w=[cq.b])
            kb.op("act", lambda e: e.activation(out=ckv[:], in_=ckv[:], func=AF.Square), w=[ckv.b])
            pq = [ps(es, kb, f"mq{i}", [128, 512], F32) for i in range(2)]
            pkv = [ps(es, kb, f"mkv{i}", [128, 512], F32) for i in range(2)]
            pss = ps(es, kb, "mss", [128, 512], F32)
            ptr = [ps(es, kb, f"mtr{i}", [128, 1024], BF16) for i in range(2)]
            rs = sb(es, kb, "m_rs", [128, 2], F32)
            q = sb(es, kb, "m_q", [128, 6, 96], F32)
            kk_ = sb(es, kb, "m_k", [128, 6, 96], F32)
            kvt = sb(es, kb, "m_kv", [128, 6, 128], F32)
            sq = sb(es, kb, "m_sq", [128, 6, 96], F32)
            hs = sb(es, kb, "m_hs", [128, 6], F32)
            krt = sb(es, kb, "m_kr", [128, 32], F32)
            cs = sb(es, kb, "m_cs", [128, 32], F32)
            r1 = sb(es, kb, "m_r1", [128, 6, 2, 8], F32)
            r2 = sb(es, kb, "m_r2", [128, 6, 2, 8], F32)
            r3 = sb(es, kb, "m_r3", [128, 6, 2, 8], F32)
            qb = sb(es, kb, "m_qb", [128, 6, 96], BF16)
            kbb = sb(es, kb, "m_kb", [128, 6, 96], BF16)
            v1 = [sb(es, kb, f"m_v1{i}", [128, 6, 65], BF16) for i in range(2)]
            qTs = [sb(es, kb, f"m_qT{i}", [96, 6, 128], BF16) for i in range(2)]
            kTs = [sb(es, kb, f"m_kT{i}", [96, 6, 128], BF16) for i in range(2)]
            for i in range(2):
                kb.op("pool", lambda e: e.memset(v1[i][:], 1.0), w=[v1[i].b])

            def head_norm(x, gain):
                kb.op("pool", lambda e: e.tensor_tensor(out=sq[:], in0=x[:], in1=x[:], op=ALU.mult), r=[x.b], w=[sq.b])
                kb.op("dve", lambda e: e.tensor_reduce(out=hs[:], in_=sq[:], axis=AX.X, op=ALU.add), r=[sq.b], w=[hs.b])
                kb.op("dve", lambda e: e.tensor_scalar(out=hs[:], in0=hs[:], scalar1=1.0 / 96, scalar2=NORM_EPS,
                                                       op0=ALU.mult, op1=ALU.add), w=[hs.b])
                kb.op("act", lambda e: e.activation(out=hs[:], in_=hs[:], func=AF.Sqrt), w=[hs.b])
                kb.op("dve", lambda e: e.reciprocal(out=hs[:], in_=hs[:]), w=[hs.b])
                kb.op("dve", lambda e: e.tensor_tensor(out=x[:], in0=x[:], in1=hs[:].unsqueeze(2).to_broadcast([128, 6, 96]),
                                                       op=ALU.mult), r=[hs.b], w=[x.b])
                kb.op("dve", lambda e: e.tensor_tensor(out=x[:], in0=x[:], in1=gain[:].unsqueeze(1).to_broadcast([128, 6, 96]),
                                                       op=ALU.mult), r=[gain.b], w=[x.b])

            def rope(x):
                xr = x[:, :, 64:96].rearrange("p h (a f e) -> p h a f e", a=2, f=2)
                x1 = xr[:, :, :, 0, :]
                x2 = xr[:, :, :, 1, :]
                c_ = cs[:, 0:16].rearrange("p (a e) -> p a e", a=2).unsqueeze(1).to_broadcast([128, 6, 2, 8])
                s_ = cs[:, 16:32].rearrange("p (a e) -> p a e", a=2).unsqueeze(1).to_broadcast([128, 6, 2, 8])
                kb.op("dve", lambda e: e.tensor_tensor(out=r1[:], in0=x1, in1=c_, op=ALU.mult), r=[x.b, cs.b], w=[r1.b])
                kb.op("dve", lambda e: e.tensor_tensor(out=r2[:], in0=x2, in1=s_, op=ALU.mult), r=[x.b, cs.b], w=[r2.b])
                kb.op("dve", lambda e: e.tensor_tensor(out=r1[:], in0=r1[:], in1=r2[:], op=ALU.subtract), r=[r2.b], w=[r1.b])
                kb.op("dve", lambda e: e.tensor_tensor(out=r2[:], in0=x2, in1=c_, op=ALU.mult), r=[x.b, cs.b], w=[r2.b])
                kb.op("dve", lambda e: e.tensor_tensor(out=r3[:], in0=x1, in1=s_, op=ALU.mult), r=[x.b, cs.b], w=[r3.b])
                kb.op("dve", lambda e: e.tensor_tensor(out=x2, in0=r2[:], in1=r3[:], op=ALU.add), r=[r2.b, r3.b], w=[x.b])
                kb.op("dve", lambda e: e.tensor_copy(out=x1, in_=r1[:]), r=[r1.b], w=[x.b])

            for tt in range(NT):
                b = tt % 2
                ts = slice(tt * 128, (tt + 1) * 128)
                kb.dma("sp", krt[:], S['krope_tm'].t[ts, :], r=[S['krope_tm'].b], w=[krt.b])
                if tt < 32:
                    kb.dma("sp", cs[:], self.consts['rope_cs'].t[ts, :], w=[cs.b])
                for k in range(2):
                    kb.op("pe", lambda e: e.matmul(pss[:, 0:1], lhsT=cq[:, k, ts], rhs=ones[:, 0:1], start=(k == 0),
                                                   stop=(k == 1)), r=[cq.b, ones.b], w=[pss.b])
                kb.op("pe", lambda e: e.matmul(pss[:, 1:2], lhsT=ckv[:, ts], rhs=ones[:, 0:1], start=True, stop=True),
                      r=[ckv.b, ones.b], w=[pss.b])
                kb.op("dve", lambda e: e.tensor_scalar(out=rs[:, 0:1], in0=pss[:, 0:1], scalar1=1.0 / 256, scalar2=NORM_EPS,
                                                       op0=ALU.mult, op1=ALU.add), r=[pss.b], w=[rs.b])
                kb.op("dve", lambda e: e.tensor_scalar(out=rs[:, 1:2], in0=pss[:, 1:2], scalar1=1.0 / 128, scalar2=NORM_EPS,
                                                       op0=ALU.mult, op1=ALU.add), r=[pss.b], w=[rs.b])
                kb.op("act", lambda e: e.activation(out=rs[:], in_=rs[:], func=AF.Sqrt), w=[rs.b])
                kb.op("dve", lambda e: e.reciprocal(out=rs[:], in_=rs[:]), w=[rs.b])
                for n in range(2):
                    for k in range(2):
                        kb.op("pe", lambda e: e.matmul(pq[n][:, 0:288], lhsT=cqb[:, k, ts], rhs=Wq[:, k, n * 288:(n + 1) * 288],
                                                       start=(k == 0), stop=(k == 1)), r=[cqb.b, Wq.b], w=[pq[n].b])
                    kb.op("act", lambda e: e.activation(out=q[:, 3 * n:3 * n + 3, :].rearrange("p h c -> p (h c)"),
                                                        in_=pq[n][:, 0:288], func=AF.Copy, scale=rs[:, 0:1]),
                          r=[pq[n].b, rs.b], w=[q.b])
                for n in range(2):
                    kb.op("pe", lambda e: e.matmul(pkv[n][:, 0:384], lhsT=ckvb[:, ts], rhs=Wkv[:, n * 384:(n + 1) * 384],
                                                   start=True, stop=True), r=[ckvb.b, Wkv.b], w=[pkv[n].b])
                    kb.op("act", lambda e: e.activation(out=kvt[:, 3 * n:3 * n + 3, :].rearrange("p h c -> p (h c)"),
                                                        in_=pkv[n][:, 0:384], func=AF.Copy, scale=rs[:, 1:2]),
                          r=[pkv[n].b, rs.b], w=[kvt.b])
                kb.op("pool", lambda e: e.tensor_copy(out=kk_[:, :, 0:64], in_=kvt[:, :, 0:64]), r=[kvt.b], w=[kk_.b])
                kb.op("pool", lambda e: e.tensor_copy(out=kk_[:, :, 64:96],
                                                      in_=krt[:].unsqueeze(1).to_broadcast([128, 6, 32])),
                      r=[krt.b], w=[kk_.b])
                kb.op("pool", lambda e: e.tensor_copy(out=v1[b][:, :, 0:64], in_=kvt[:, :, 64:128]), r=[kvt.b], w=[v1[b].b])
                head_norm(q, qg)
                head_norm(kk_, kg_)
                if tt < 32:
                    rope(q)
                    rope(kk_)
                kb.op("pool", lambda e: e.tensor_copy(out=qb[:], in_=q[:]), r=[q.b], w=[qb.b])
                kb.op("pool", lambda e: e.tensor_copy(out=kbb[:], in_=kk_[:]), r=[kk_.b], w=[kbb.b])
                pt = ptr[b]
                for h in range(6):
                    kb.op("pe", lambda e: e.transpose(pt[0:96, h * 128:(h + 1) * 128], qb[:, h, :], identb[:]),
                          r=[qb.b, identb.b], w=[pt.b])
                kb.op("act", lambda e: e.copy(out=qTs[b][:].rearrange("p h t -> p (h t)"), in_=pt[0:96, 0:768]),
                      r=[pt.b], w=[qTs[b].b])
                for h in range(6):
                    kb.op("pe", lambda e: e.transpose(pt[0:96, h * 128:(h + 1) * 128], kbb[:, h, :], identb[:]),
                          r=[kbb.b, identb.b], w=[pt.b])
                kb.op("dve", lambda e: e.tensor_copy(out=kTs[b][:].rearrange("p h t -> p (h t)"), in_=pt[0:96, 0:768]),
                      r=[pt.b], w=[kTs[b].b])
                kb.dma("sp", S['qT'].t[:, :, ts], qTs[b][:], r=[qTs[b].b], w=[S['qT'].b])
                kb.dma("sp", S['kT'].t[:, :, ts], kTs[b][:], r=[kTs[b].b], w=[S['kT'].b])
                kb.dma("sp", S['V1'].t[ts, :, :], v1[b][:], r=[v1[b].b], w=[S['V1'].b])
        barrier(kb)

    def phase_mla_attn(self, li, need_ctx):
        kb, nc = self.kb, self.nc
        S = self.scr
        mix = S['mix_tm']
        with ExitStack() as es:
            kT = sb(es, kb, "a_kT", [96, 6, T], BF16)
            V1 = sb(es, kb, "a_V1", [128, NT, 6 * 65], BF16)
            kb.dma("sp", kT[:], S['kT'].t, r=[S['kT'].b], w=[kT.b])
            kb.dma("sp", V1[:], S['V1'].t.rearrange("(n p) h c -> p n (h c)", p=128), r=[S['V1'].b], w=[V1.b])
            qT = [sb(es, kb, f"a_qT{i}", [96, 6, 128], BF16) for i in range(2)]
            Pt = [sb(es, kb, f"a_P{i}", [128, 4, 128], BF16) for i in range(3)]
            psc = [ps(es, kb, f"a_sc{i}", [128, 512], F32) for i in range(3)]
            pov = [ps(es, kb, f"a_po{i}", [128, 512], F32) for i in range(2)]
            ot = [sb(es, kb, f"a_ot{i}", [128, 6, 64], F32) for i in range(2)]
            rcp = sb(es, kb, "a_rcp", [128, 1], F32)
            nsc = 0
            npo = 0
            qtiles = list(range(NT if need_ctx else 32))
            for qi, qt in enumerate(qtiles):
                b = qi % 2
                qs = slice(qt * 128, (qt + 1) * 128)
                kb.dma("sp", qT[b][:], S['qT'].t[:, :, qs], r=[S['qT'].b], w=[qT[b].b])
                ktiles = list(range(NT)) if qt < 32 else [32, 33]
                for h in range(6):
                    po = pov[npo % 2]
                    npo += 1
                    for g0 in range(0, len(ktiles), 4):
                        grp = ktiles[g0:g0 + 4]
                        sc = psc[nsc % 3]
                        P_ = Pt[nsc % 3]
                        nsc += 1
                        for j, kt_ in enumerate(grp):
                            kb.op("pe", lambda e: e.matmul(sc[:, j * 128:(j + 1) * 128], lhsT=kT[:, h, kt_ * 128:(kt_ + 1) * 128],
                                                           rhs=qT[b][:, h, :], start=True, stop=True),
                                  r=[kT.b, qT[b].b], w=[sc.b])
                        n = len(grp)
                        kb.op("act", lambda e: e.activation(out=P_[:, 0:n, :].rearrange("p j q -> p (j q)"),
                                                            in_=sc[:, 0:n * 128], func=AF.Exp), r=[sc.b], w=[P_.b])
                        for j, kt_ in enumerate(grp):
                            first = (g0 == 0 and j == 0)
                            last = (g0 + j == len(ktiles) - 1)
                            kb.op("pe", lambda e: e.matmul(po[:, 0:65], lhsT=P_[:, j, :], rhs=V1[:, kt_, h * 65:(h + 1) * 65],
                                                           start=first, stop=last), r=[P_.b, V1.b], w=[po.b])
                    kb.op("dve", lambda e: e.reciprocal(out=rcp[:], in_=po[:, 64:65]), r=[po.b], w=[rcp.b])
                    kb.op("dve", lambda e: e.tensor_scalar(out=ot[b][:, h, :], in0=po[:, 0:64], scalar1=rcp[:, 0:1],
                                                           scalar2=None, op0=ALU.mult), r=[po.b, rcp.b], w=[ot[b].b])
                kb.dma("sp", mix.t[qs, 384:768], ot[b][:].rearrange("p h c -> p (h c)"), r=[ot[b].b], w=[mix.b])
        barrier(kb)

    def phase_hy_filter(self, li, L, Hd, suffix):
        kb, nc = self.kb, self.nc
        TWO_PI = 2.0 * math.pi
        C = self.consts
        with ExitStack() as es:
            w1 = sb(es, kb, "hw1", [33, 64], F32)
            w2 = sb(es, kb, "hw2", [64, 64], F32)
            w3 = sb(es, kb, "hw3", [64, 1024], F32)
            b1 = sb(es, kb, "hb1", [64, 1], F32)
            f1 = sb(es, kb, "hf1", [64, 1], F32)
            b2 = sb(es, kb, "hb2", [64, 1], F32)
            f2 = sb(es, kb, "hf2", [64, 1], F32)
            b3 = sb(es, kb, "hb3", [128, 8], F32)
            nrate = sb(es, kb, "hnrate", [128, 2], F32)
            kb.dma("sp", w1[:], self.din['hy_w1'].t[li], w=[w1.b])
            kb.dma("sp", w2[:], self.din['hy_w2'].t[li], w=[w2.b])
            kb.dma("sp", w3[:], self.din['hy_w3'].t[li], w=[w3.b])
            for (tl, nm) in ((b1, 'hy_b1'), (f1, 'hy_freq1'), (b2, 'hy_b2'), (f2, 'hy_freq2')):
                kb.dma("sp", tl[:], self.din[nm].t[li].rearrange("(p o) -> p o", o=1), w=[tl.b])
            kb.dma("sp", b3[:], self.din['hy_b3'].t[li].rearrange("(k p) -> p k", p=128), w=[b3.b],
                   allow_slow_non_contiguous=True)
            kb.dma("sp", nrate[:], C['hy_nrate'].t.rearrange("(k p) -> p k", p=128), w=[nrate.b],
                   allow_slow_non_contiguous=True)
            zp = sb(es, kb, "hzp", [33, L], F32)
            tnb = sb(es, kb, "htn", [128, L], F32)
            h1 = sb(es, kb, "hh1", [64, L], F32)
            h2 = sb(es, kb, "hh2", [64, 2, L], F32)
            arg = sb(es, kb, "harg", [64, 512], F32)
            ki = sb(es, kb, "hki", [64, 512], I32)
            kf = sb(es, kb, "hkf", [64, 512], F32)
            msk = sb(es, kb, "hmsk", [64, 512], F32)
            dec = sb(es, kb, "hdec", [128, L], F32)
            H = sb(es, kb, "hH", [128, 8192], F32)
            Hb = sb(es, kb, "hHb", [128, 8192], BF16)
            junk = Hb
            ssq = sb(es, kb, "hssq", [128, 1], F32)
            pp = [ps(es, kb, f"hp{i}", [128, 512], F32) for i in range(3)]
            npp = [0]
            nb = max(1, L // 512)
            bw = min(512, L)

            def sin_layer(dstf, w, srcf, srcb, bcol, fcol, K):
                for d in [DCUR[0]]:
                    for blk in range(nb):
                        cs_ = slice(blk * bw, (blk + 1) * bw)
                        p_ = pp[npp[0] % 3]
                        npp[0] += 1
                        kb.op("pe", lambda e: e.matmul(p_[0:64, 0:bw], lhsT=w[0:K, :], rhs=srcf(cs_), start=True, stop=True),
                              r=[w.b, srcb], w=[p_.b])
                        a_ = arg[:, 0:bw]
                        kb.op("dve", lambda e: e.tensor_scalar(out=a_, in0=p_[0:64, 0:bw], scalar1=bcol[:, 0:1], scalar2=fcol[:, 0:1],
                                                               op0=ALU.add, op1=ALU.mult), r=[p_.b, bcol.b, fcol.b], w=[arg.b])
                        kb.op("dve", lambda e: e.tensor_scalar(out=ki[:, 0:bw], in0=a_, scalar1=1.0 / TWO_PI, scalar2=None,
                                                               op0=ALU.mult), r=[arg.b], w=[ki.b])
                        kb.op("dve", lambda e: e.tensor_copy(out=kf[:, 0:bw], in_=ki[:, 0:bw]), r=[ki.b], w=[kf.b])
                        kb.op("dve", lambda e: e.scalar_tensor_tensor(out=a_, in0=kf[:, 0:bw], scalar=-TWO_PI, in1=a_,
                                                                      op0=ALU.mult, op1=ALU.add), r=[kf.b], w=[arg.b])
                        kb.op("dve", lambda e: e.tensor_scalar(out=msk[:, 0:bw], in0=a_, scalar1=math.pi, scalar2=-TWO_PI,
                                                               op0=ALU.is_gt, op1=ALU.mult), r=[arg.b], w=[msk.b])
                        kb.op("dve", lambda e: e.tensor_tensor(out=a_, in0=a_, in1=msk[:, 0:bw], op=ALU.add), r=[msk.b], w=[arg.b])
                        kb.op("dve", lambda e: e.tensor_scalar(out=msk[:, 0:bw], in0=a_, scalar1=-math.pi, scalar2=TWO_PI,
                                                               op0=ALU.is_lt, op1=ALU.mult), r=[arg.b], w=[msk.b])
                        kb.op("dve", lambda e: e.tensor_tensor(out=a_, in0=a_, in1=msk[:, 0:bw], op=ALU.add), r=[msk.b], w=[arg.b])
                        kb.op("dve", lambda e: e.tensor_scalar(out=a_, in0=a_, scalar1=3.14159, scalar2=-3.14159,
                                                               op0=ALU.min, op1=ALU.max), w=[arg.b])
                        kb.op("act", lambda e: e.activation(out=dstf(cs_)[0], in_=a_, func=AF.Sin), r=[arg.b], w=[dstf(cs_)[1]])

            DCUR = [0]
            for d in range(2):
                DCUR[0] = d
                kb.dma("sp", zp[:], C['hy_z' + suffix].t[d], w=[zp.b])
                sin_layer(lambda cs_: (h1[:, cs_], h1.b), w1, lambda cs_: zp[0:33, cs_], zp.b, b1, f1, 33)
                sin_layer(lambda cs_: (h2[:, d, cs_], h2.b), w2, lambda cs_: h1[:, cs_], h1.b, b2, f2, 64)
            for o in range(2):
                for c in range(2):
                    kb.op("pool", lambda e: e.memset(H[:], 0.0), w=[H.b])
                    for d in range(2):
                        col0 = o * 512 + d * 256 + c * 128
                        kcol = col0 // 128
                        kb.dma("sp", tnb[:], C['hy_tn' + suffix].t[d:d + 1, :].partition_broadcast(128), w=[tnb.b])
                        kb.op("act", lambda e: e.activation(out=dec[:], in_=tnb[:], func=AF.Exp, scale=nrate[:, c:c + 1]),
                              r=[tnb.b, nrate.b], w=[dec.b])
                        for blk in range(nb):
                            cs_ = slice(blk * bw, (blk + 1) * bw)
                            p_ = pp[npp[0] % 3]
                            npp[0] += 1
                            kb.op("pe", lambda e: e.matmul(p_[:, 0:bw], lhsT=w3[:, col0:col0 + 128], rhs=h2[:, d, cs_],
                                                           start=True, stop=True), r=[w3.b, h2.b], w=[p_.b])
                            if d == 0:
                                n0 = 4096 + blk * bw
                                width = bw
                            else:
                                n0 = 4096 - L + 1 + blk * bw
                                width = bw if blk < nb - 1 else bw - 1
                            kb.op("dve", lambda e: e.scalar_tensor_tensor(out=H[:, n0:n0 + width], in0=p_[:, 0:width],
                                                                          scalar=b3[:, kcol:kcol + 1],
                                                                          in1=dec[:, blk * bw:blk * bw + width],
                                                                          op0=ALU.add, op1=ALU.mult),
                                  r=[p_.b, b3.b, dec.b], w=[H.b])
                    kb.op("act", lambda e: e.activation(out=junk[:], in_=H[:], func=AF.Square, accum_out=ssq[:]),
                          r=[H.b], w=[junk.b, ssq.b])
                    kb.op("act", lambda e: e.activation(out=ssq[:], in_=ssq[:], func=AF.Sqrt), w=[ssq.b])
                    kb.op("dve", lambda e: e.reciprocal(out=ssq[:], in_=ssq[:]), w=[ssq.b])
                    kb.op("dve", lambda e: e.tensor_scalar(out=Hb[:], in0=H[:], scalar1=ssq[:, 0:1], scalar2=None, op0=ALU.mult),
                          r=[H.b, ssq.b], w=[Hb.b])
                    kb.dma("sp", Hd.t[o, c * 128:(c + 1) * 128, :], Hb[:], r=[Hb.b], w=[Hd.b])
        barrier(kb)

    def phase_hyena(self, li, need_ctx):
        kb, nc = self.kb, self.nc
        S = self.scr
        zT = S['zT']
        mix = S['mix_tm']
        C = self.consts
        with ExitStack() as es:
            identf = self.make_ident(es, F32, "identf_h")
            src = [sb(es, kb, f"hsrc{i}", [128, T], F32) for i in range(2)]
            stg = [sb(es, kb, f"hstg{i}", [128, NT, 128], F32) for i in range(2)]
            tpp = [ps(es, kb, f"htq{i}", [128, 4, 128], F32) for i in range(2)]
            st = [0]
            for i in range(6):
                s_ = src[i % 2]
                kb.dma("sp", s_[:], zT.t[14 + i], r=[zT.b], w=[s_.b])
                self.fm_to_tm(s_, None, identf, stg[i % 2], tpp, NT, st)
                kb.dma("sp", S['hy_tm'].t[i // 2].rearrange("(n p) f -> p n f", p=128)[:, :, (i % 2) * 128:(i % 2 + 1) * 128],
                       stg[i % 2][:], r=[stg[i % 2].b], w=[S['hy_tm'].b])
        barrier(kb)
        segs = [(0, 32, S['Hd_l'], 31)]
        if need_ctx:
            segs.append((32, 2, S['Hd_c'], 1))
        with ExitStack() as es:
            J = sb(es, kb, "hJ", [128, 128], F32)
            kb.dma("sp", J[:], C['antiident'].t, w=[J.b])
            bias = sb(es, kb, "hbias", [128, 2, 256], F32)
            for o in range(2):
                kb.dma("sp", bias[:, o, :], self.din['hy_bias'].t[li, o:o + 1, :].partition_broadcast(128), w=[bias.b])
            U = sb(es, kb, "hU", [128, NT, 256], F32)
            G = sb(es, kb, "hG", [128, NT, 256], F32)
            Y = sb(es, kb, "hY", [128, NT, 256], F32)
            Uf = sb(es, kb, "hUf", [128, NT, 256], BF16)
            At = [sb(es, kb, f"hAt{i}", [128, 8064], BF16) for i in range(3)]
            pf = [ps(es, kb, f"hpf{i}", [128, 512], F32) for i in range(2)]
            pc = [ps(es, kb, f"hpc{i}", [128, 512], F32) for i in range(2)]
            nat = 0
            kb.dma("sp", U[:], S['hy_tm'].t[2].rearrange("(n p) f -> p n f", p=128), r=[S['hy_tm'].b], w=[U.b])
            for o in range(2):
                kb.dma("sp", G[:], S['hy_tm'].t[o].rearrange("(n p) f -> p n f", p=128), r=[S['hy_tm'].b], w=[G.b])
                ntl = NT if need_ctx else 32
                for tt in range(ntl):
                    p_ = pf[tt % 2]
                    kb.op("pe", lambda e: e.matmul(p_[:, 0:256], lhsT=J[:], rhs=U[:, tt, :], start=True, stop=True),
                          r=[J.b, U.b], w=[p_.b])
                    kb.op("act", lambda e: e.copy(out=Uf[:, tt, :], in_=p_[:, 0:256]), r=[p_.b], w=[Uf.b])
                for (tile0, nti, Hd, mmax) in segs:
                    ncol = (2 * mmax + 1) * 128
                    x0 = (31 - mmax) * 128
                    for cg in range(16):
                        pcb = pc[cg % 2]
                        for cc in range(16):
                            ch = cg * 16 + cc
                            A = At[nat % 3]
                            nat += 1
                            src_ap = bass.AP(Hd.t.tensor, Hd.t.offset + (o * 256 + ch) * 8192 + 1 + x0, [[1, 128], [1, ncol]])
                            kb.dma("sp", A[:, 0:ncol], src_ap, r=[Hd.b], w=[A.b])
                            outv = pcb[:, cc * 32:cc * 32 + nti]
                            order = [0] + [m for m in range(-mmax, mmax + 1) if m != 0]
                            for mi, m in enumerate(order):
                                i0 = max(0, m)
                                i1 = min(nti - 1, nti - 1 + m)
                                if i1 < i0:
                                    continue
                                n = i1 - i0 + 1
                                j0 = i0 - m
                                kb.op("pe", lambda e: e.matmul(pcb[:, cc * 32 + i0:cc * 32 + i0 + n],
                                                               lhsT=A[:, (m + mmax) * 128:(m + mmax + 1) * 128],
                                                               rhs=Uf[:, tile0 + j0:tile0 + j0 + n, ch],
                                                               start=(mi == 0), stop=(mi == len(order) - 1)),
                                      r=[A.b, Uf.b], w=[pcb.b])
                        kb.op("act", lambda e: e.copy(out=Y[:, tile0:tile0 + nti, cg * 16:(cg + 1) * 16].rearrange("p i c -> p c i"),
                                                      in_=pcb[:, :].rearrange("p (c i) -> p c i", c=16)[:, :, 0:nti]),
                              r=[pcb.b], w=[Y.b])
                bb = bias[:, o, :].unsqueeze(1).to_broadcast([128, ntl, 256])
                kb.op("pool", lambda e: e.tensor_tensor(out=U[:, 0:ntl, :], in0=U[:, 0:ntl, :], in1=bb, op=ALU.mult),
                      r=[bias.b], w=[U.b])
                kb.op("dve", lambda e: e.tensor_tensor(out=U[:, 0:ntl, :], in0=U[:, 0:ntl, :], in1=Y[:, 0:ntl, :], op=ALU.add),
                      r=[Y.b], w=[U.b])
                kb.op("dve", lambda e: e.tensor_tensor(out=U[:, 0:ntl, :], in0=U[:, 0:ntl, :], in1=G[:, 0:ntl, :], op=ALU.mult),
                      r=[G.b], w=[U.b])
            ntl = NT if need_ctx else 32
            kb.dma("sp", mix.t.rearrange("(n p) f -> p n f", p=128)[:, 0:ntl, 768:1024], U[:, 0:ntl, :], r=[U.b], w=[mix.b])
        barrier(kb)

    def phase_peer(self, li, need_ctx, final_out=None):
        kb, nc = self.kb, self.nc
        S = self.scr
        modrow = S['modrow']
        xcur = S['xcur']
        ntile = NT if need_ctx else 32
        puv = S['puv_bf'].t.rearrange("e j d -> e (j d)")
        if not hasattr(self, '_bc_reg'):
            self._bc_reg = nc.gpsimd.alloc_register("peer_bc")
            nc.gpsimd.reg_mov(self._bc_reg, 32767)
        bc_reg = self._bc_reg
        NB = 12
        with ExitStack() as es:
            identb = self.make_ident(es, BF16, "identb_p")
            identf = self.make_ident(es, F32, "identf_p")
            Wq = sb(es, kb, "pWq", [128, 8, 2048], BF16)
            keysT = sb(es, kb, "pkeysT", [128, 16, 128], BF16)
            es_w = ExitStack()
            wst = [sb(es_w, kb, f"pwst{i}", [128, 2048], F32) for i in range(2)]
            for k in range(8):
                kb.dma("sp", wst[k % 2][:], self.din['peer_wq'].t[li, k * 128:(k + 1) * 128, :], w=[wst[k % 2].b])
                kb.op("pool", lambda e: e.tensor_copy(out=Wq[:, k, :], in_=wst[k % 2][:]), r=[wst[k % 2].b], w=[Wq.b])
            pq = [ps(es, kb, f"ppq{i}", [128, 4, 128], F32) for i in range(2)]
            psr = [ps(es, kb, f"ppsr{i}", [128, 4, 128], F32) for i in range(2)]
            tp = [ps(es, kb, f"pptp{i}", [128, 8, 128], BF16) for i in range(2)]
            for c4 in range(4):
                kst = wst[c4 % 2]
                kb.dma("sp", kst[:, 0:512].rearrange("n (c d) -> n c d", c=4),
                       self.din['peer_keys'].t[li].rearrange("h p n d -> n (h p) d")[:, c4 * 4:(c4 + 1) * 4, :], w=[kst.b])
                p_ = pq[c4 % 2]
                for j in range(4):
                    kb.op("pe", lambda e: e.transpose(p_[:, j, :], kst[:, j * 128:(j + 1) * 128], identf[:]),
                          r=[kst.b, identf.b], w=[p_.b])
                kb.op("act", lambda e: e.copy(out=keysT[:, c4 * 4:(c4 + 1) * 4, :], in_=p_[:]), r=[p_.b], w=[keysT.b])
            barrier(kb)
            es_w.close()
            G = [sb(es, kb, f"pG{w_}", [128, D], F32) for w_ in range(2)]
            SH = [sb(es, kb, f"pSH{w_}", [128, D], F32) for w_ in range(2)]
            GF = [sb(es, kb, f"pGF{w_}", [128, D], F32) for w_ in range(2)]
            gn = sb(es, kb, "pgn", [128, D], F32)
            kb.dma("sp", gn[:], self.din['ffn_norm'].t[li:li + 1, :].partition_broadcast(128), w=[gn.b])
            for w_ in range(2 if need_ctx else 1):
                kb.dma("sp", G[w_][:], modrow.t[w_:w_ + 1, 4 * D:5 * D].partition_broadcast(128), r=[modrow.b], w=[G[w_].b])
                kb.dma("sp", SH[w_][:], modrow.t[w_:w_ + 1, 3 * D:4 * D].partition_broadcast(128), r=[modrow.b], w=[SH[w_].b])
                kb.dma("sp", GF[w_][:], modrow.t[w_:w_ + 1, 5 * D:6 * D].partition_broadcast(128), r=[modrow.b], w=[GF[w_].b])
                kb.op("dve", lambda e: e.scalar_tensor_tensor(out=G[w_][:], in0=G[w_][:], scalar=1.0, in1=gn[:],
                                                              op0=ALU.add, op1=ALU.mult), r=[gn.b], w=[G[w_].b])
            xt = [sb(es, kb, f"pxt{i}", [128, D], F32) for i in range(2)]
            hn = [sb(es, kb, f"phn{i}", [128, D], F32) for i in range(2)]
            hnb2 = [sb(es, kb, f"phnb{i}", [128, D], BF16) for i in range(2)]
            junk = sb(es, kb, "pjunk", [128, D], F32)
            junkb = sb(es, kb, "pjunkb", [128, D], BF16)
            ss = sb(es, kb, "pss", [128, 1], F32)
            hnT = sb(es, kb, "phnT", [128, 8, 128], BF16)
            qTs = sb(es, kb, "pqTs", [128, 16, 128], BF16)
            s_sb = sb(es, kb, "ps_sb", [128, 16, 128], F32)
            tmpc = sb(es, kb, "ptmpc", [128, 256], F32)
            sv = sb(es, kb, "psv", [128, 16, 16], F32)
            siu = sb(es, kb, "psiu", [128, 16, 16], U32)
            sif = sb(es, kb, "psif", [128, 16, 16], F32)
            s1x = sb(es, kb, "ps1x", [128, 8, 16], F32)
            cand = sb(es, kb, "pcand", [128, 8, 256], F32)
            ci = sb(es, kb, "pci", [128, 8, 256], F32)
            tsv = sb(es, kb, "ptsv", [128, 8, 16], F32)
            posu = sb(es, kb, "pposu", [128, 8, 16], U32)
            pau = sb(es, kb, "ppau", [128, 8, 16], U32)
            pbu = sb(es, kb, "ppbu", [128, 8, 16], U32)
            paf = sb(es, kb, "ppaf", [128, 8, 16], F32)
            pbf = sb(es, kb, "ppbf", [128, 8, 16], F32)
            oh = sb(es, kb, "poh", [128, 16, 16], F32)
            eidb = sb(es, kb, "peidb", [128, 8, 16], F32)
            iota16 = sb(es, kb, "piota", [128, 16], F32)
            kb.dma("sp", iota16[:], self.consts['iota16'].t[0:1, :].partition_broadcast(128), w=[iota16.b])
            eidf = sb(es, kb, "peidf", [128, 8, 16], F32)
            eidx = [sb(es, kb, f"peidx{i}", [128, 128], I32) for i in range(2)]
            gate2 = [sb(es, kb, f"pgate{i}", [128, 8, 16], F32) for i in range(2)]
            gsum = sb(es, kb, "pgsum", [128, 8], F32)
            act = sb(es, kb, "pact", [128, 128], F32, nsub=128)
            wgt = sb(es, kb, "pwgt", [128, 128], F32, nsub=128)
            acc = sb(es, kb, "pacc", [128, D], F32)
            dgs = [sb(es, kb, f"pdg{i}", [128, 128], BF16) for i in range(4)]
            pacc = [ps(es, kb, f"ppacc{i}", [128, 512], F32) for i in range(2)]
            ug = [sb(es, kb, f"pug{i}", [128, 2 * D], BF16) for i in range(NB)]
            nev = [0]

            def stage_a(tt):
                b = tt % 2
                ts = slice(tt * 128, (tt + 1) * 128)
                which = 0 if tt < 32 else 1
                x_, h_ = xt[b], hn[b]
                hnb = hnb2[b]
                gate = gate2[b]
                kb.dma("sp", x_[:], xcur.t[ts, :], r=[xcur.b], w=[x_.b])
                kb.op("act", lambda e: e.activation(out=junk[:], in_=x_[:], func=AF.Square, accum_out=ss[:]),
                      r=[x_.b], w=[junk.b, ss.b])
                kb.op("dve", lambda e: e.tensor_scalar(out=ss[:], in0=ss[:], scalar1=1.0 / D, scalar2=NORM_EPS,
                                                       op0=ALU.mult, op1=ALU.add), w=[ss.b])
                kb.op("act", lambda e: e.activation(out=ss[:], in_=ss[:], func=AF.Sqrt), w=[ss.b])
                kb.op("dve", lambda e: e.reciprocal(out=ss[:], in_=ss[:]), w=[ss.b])
                kb.op("dve", lambda e: e.scalar_tensor_tensor(out=h_[:], in0=x_[:], scalar=ss[:, 0:1], in1=G[which][:],
                                                              op0=ALU.mult, op1=ALU.mult), r=[x_.b, ss.b, G[which].b], w=[h_.b])
                kb.op("dve", lambda e: e.tensor_tensor(out=h_[:], in0=h_[:], in1=SH[which][:], op=ALU.add),
                      r=[SH[which].b], w=[h_.b])
                kb.op("act", lambda e: e.copy(out=hnb[:], in_=h_[:]), r=[h_.b], w=[hnb.b])
                for k in range(8):
                    kb.op("pe", lambda e: e.transpose(tp[b][:, k, :], hnb[:, k * 128:(k + 1) * 128], identb[:]),
                          r=[hnb.b, identb.b], w=[tp[b].b])
                kb.op("act", lambda e: e.copy(out=hnT[:], in_=tp[b][:]), r=[tp[b].b], w=[hnT.b])
                for c4 in range(4):
                    p_ = pq[c4 % 2]
                    for j in range(4):
                        c = c4 * 4 + j
                        for k in range(8):
                            kb.op("pe", lambda e: e.matmul(p_[:, j, :], lhsT=Wq[:, k, c * 128:(c + 1) * 128], rhs=hnT[:, k, :],
                                                           start=(k == 0), stop=(k == 7)), r=[Wq.b, hnT.b], w=[p_.b])
                    nev[0] += 1
                    if nev[0] % 2:
                        kb.op("act", lambda e: e.copy(out=qTs[:, c4 * 4:(c4 + 1) * 4, :], in_=p_[:]), r=[p_.b], w=[qTs.b])
                    else:
                        kb.op("dve", lambda e: e.tensor_copy(out=qTs[:, c4 * 4:(c4 + 1) * 4, :], in_=p_[:]), r=[p_.b], w=[qTs.b])
                for c4 in range(4):
                    p_ = psr[c4 % 2]
                    for j in range(4):
                        c = c4 * 4 + j
                        kb.op("pe", lambda e: e.matmul(p_[:, j, :], lhsT=qTs[:, c, :], rhs=keysT[:, c, :], start=True, stop=True),
                              r=[qTs.b, keysT.b], w=[p_.b])
                    kb.op("act", lambda e: e.copy(out=s_sb[:, c4 * 4:(c4 + 1) * 4, :], in_=p_[:]), r=[p_.b], w=[s_sb.b])
                for c in range(16):
                    kb.op("dve", lambda e: e.max(out=sv[:, c, 0:8], in_=s_sb[:, c, :]), r=[s_sb.b], w=[sv.b])
                    kb.op("dve", lambda e: e.max_index(out=siu[:, c, 0:8], in_max=sv[:, c, 0:8], in_values=s_sb[:, c, :]),
                          r=[s_sb.b, sv.b], w=[siu.b])
                    kb.op("dve", lambda e: e.match_replace(out=tmpc[:, 0:128], in_to_replace=sv[:, c, 0:8], in_values=s_sb[:, c, :],
                                                           imm_value=-1e30), r=[s_sb.b, sv.b], w=[tmpc.b])
                    kb.op("dve", lambda e: e.max(out=sv[:, c, 8:16], in_=tmpc[:, 0:128]), r=[tmpc.b], w=[sv.b])
                    kb.op("dve", lambda e: e.max_index(out=siu[:, c, 8:16], in_max=sv[:, c, 8:16], in_values=tmpc[:, 0:128]),
                          r=[tmpc.b, sv.b], w=[siu.b])
                kb.op("dve", lambda e: e.tensor_copy(out=sif[:], in_=siu[:]), r=[siu.b], w=[sif.b])
                sv4 = sv[:].rearrange("p (h q) k -> p h q k", q=2)
                si4 = sif[:].rearrange("p (h q) k -> p h q k", q=2)
                kb.op("dve", lambda e: e.tensor_scalar(out=s1x[:], in0=si4[:, :, 0, :], scalar1=128.0, scalar2=None, op0=ALU.mult),
                      r=[sif.b], w=[s1x.b])
                c4v = cand[:].rearrange("p h (a b) -> p h a b", a=16)
                i4v = ci[:].rearrange("p h (a b) -> p h a b", a=16)
                kb.op("dve", lambda e: e.tensor_tensor(out=c4v, in0=sv4[:, :, 0, :].unsqueeze(3).to_broadcast([128, 8, 16, 16]),
                                                       in1=sv4[:, :, 1, :].unsqueeze(2).to_broadcast([128, 8, 16, 16]), op=ALU.add),
                      r=[sv.b], w=[cand.b])
                kb.op("dve", lambda e: e.tensor_tensor(out=i4v, in0=s1x[:].unsqueeze(3).to_broadcast([128, 8, 16, 16]),
                                                       in1=si4[:, :, 1, :].unsqueeze(2).to_broadcast([128, 8, 16, 16]), op=ALU.add),
                      r=[s1x.b, sif.b], w=[ci.b])
                for h in range(8):
                    kb.op("dve", lambda e: e.max(out=tsv[:, h, 0:8], in_=cand[:, h, :]), r=[cand.b], w=[tsv.b])
                    kb.op("dve", lambda e: e.max_index(out=posu[:, h, 0:8], in_max=tsv[:, h, 0:8], in_values=cand[:, h, :]),
                          r=[cand.b, tsv.b], w=[posu.b])
                    kb.op("dve", lambda e: e.match_replace(out=tmpc[:], in_to_replace=tsv[:, h, 0:8], in_values=cand[:, h, :],
                                                           imm_value=-1e30), r=[cand.b, tsv.b], w=[tmpc.b])
                    kb.op("dve", lambda e: e.max(out=tsv[:, h, 8:16], in_=tmpc[:]), r=[tmpc.b], w=[tsv.b])
                    kb.op("dve", lambda e: e.max_index(out=posu[:, h, 8:16], in_max=tsv[:, h, 8:16], in_values=tmpc[:]),
                          r=[tmpc.b, tsv.b], w=[posu.b])
                kb.op("dve", lambda e: e.tensor_single_scalar(out=pau[:], in_=posu[:], scalar=4, op=ALU.logical_shift_right),
                      r=[posu.b], w=[pau.b])
                kb.op("dve", lambda e: e.tensor_single_scalar(out=pbu[:], in_=posu[:], scalar=15, op=ALU.bitwise_and),
                      r=[posu.b], w=[pbu.b])
                kb.op("dve", lambda e: e.tensor_copy(out=paf[:], in_=pau[:]), r=[pau.b], w=[paf.b])
                kb.op("dve", lambda e: e.tensor_copy(out=pbf[:], in_=pbu[:]), r=[pbu.b], w=[pbf.b])
                for h in range(8):
                    for (pf, src, first) in ((paf, s1x[:, h, :], True), (pbf, si4[:, h, 1, :], False)):
                        kb.op("dve", lambda e: e.tensor_tensor(out=oh[:], in0=pf[:, h, :].unsqueeze(2).to_broadcast([128, 16, 16]),
                                                               in1=iota16[:].unsqueeze(1).to_broadcast([128, 16, 16]),
                                                               op=ALU.is_equal), r=[pf.b, iota16.b], w=[oh.b])
                        kb.op("dve", lambda e: e.tensor_tensor(out=oh[:], in0=oh[:], in1=src.unsqueeze(1).to_broadcast([128, 16, 16]),
                                                               op=ALU.mult), r=[s1x.b, sif.b], w=[oh.b])
                        dst = eidf if first else eidb
                        kb.op("dve", lambda e: e.tensor_reduce(out=dst[:, h, :], in_=oh[:], axis=AX.X, op=ALU.add), r=[oh.b], w=[dst.b])
                kb.op("dve", lambda e: e.tensor_tensor(out=eidf[:], in0=eidf[:], in1=eidb[:], op=ALU.add), r=[eidb.b], w=[eidf.b])
                ei = eidx[b]
                if li > 0:
                    kb.op("dve", lambda e: e.tensor_scalar(out=eidf[:], in0=eidf[:], scalar1=float(li * 16384), scalar2=None,
                                                           op0=ALU.add), w=[eidf.b])
                kb.op("dve", lambda e: e.tensor_copy(out=ei[:], in_=eidf[:].rearrange("p h k -> p (h k)")), r=[eidf.b], w=[ei.b])
                kb.op("dve", lambda e: e.tensor_tensor(out=gate[:], in0=tsv[:], in1=tsv[:, :, 0:1].to_broadcast([128, 8, 16]),
                                                       op=ALU.subtract), r=[tsv.b], w=[gate.b])
                kb.op("act", lambda e: e.activation(out=gate[:], in_=gate[:], func=AF.Exp), w=[gate.b])
                kb.op("dve", lambda e: e.tensor_reduce(out=gsum[:], in_=gate[:], axis=AX.X, op=ALU.add), r=[gate.b], w=[gsum.b])
                kb.op("dve", lambda e: e.reciprocal(out=gsum[:], in_=gsum[:]), w=[gsum.b])
                kb.op("dve", lambda e: e.tensor_tensor(out=gate[:], in0=gate[:], in1=gsum[:].unsqueeze(2).to_broadcast([128, 8, 16]),
                                                       op=ALU.mult), r=[gsum.b], w=[gate.b])

            def stage_b(tt):
                b = tt % 2
                ts = slice(tt * 128, (tt + 1) * 128)
                which = 0 if tt < 32 else 1
                x_, h_ = xt[b], hn[b]
                hnb = hnb2[b]
                gate = gate2[b]
                ei = eidx[b]
                kb.op("dve", lambda e: e.memset(act[:], 0.0), w=list(act.sub))
                gflat = gate[:].rearrange("p h k -> p (h k)")
                for slot in range(128):
                    uv_ = ug[slot % NB]
                    kb.idma(uv_[:, :], puv, bass.IndirectOffsetOnAxis(ap=ei[:, slot:slot + 1], axis=0), r=[ei.b, S['puv_bf'].b],
                            w=[uv_.b], bounds_check=bc_reg, oob_is_err=False)
                    kb.op("dve", lambda e: e.scalar_tensor_tensor(out=junkb[:], in0=uv_[:, 0:D], scalar=1.0, in1=hnb[:],
                                                                  op0=ALU.mult, op1=ALU.mult, accum_out=act[:, slot:slot + 1]),
                          r=[uv_.b, hnb.b], w=[junkb.b, act.sub[slot]])
                    kb.op("act", lambda e: e.activation(out=wgt[:, slot:slot + 1], in_=act[:, slot:slot + 1], func=AF.Gelu),
                          r=[act.sub[slot]], w=[wgt.sub[slot]])
                    dg = dgs[slot % 4]
                    kb.op("dve", lambda e: e.tensor_scalar(out=dg[:], in0=identb[:], scalar1=wgt[:, slot:slot + 1],
                                                           scalar2=gflat[:, slot:slot + 1], op0=ALU.mult, op1=ALU.mult),
                          r=[identb.b, wgt.sub[slot], gate.b], w=[dg.b])
                    for n in range(2):
                        kb.op("pe", lambda e: e.matmul(pacc[n][:, :], lhsT=dg[:], rhs=uv_[:, D + n * 512:D + (n + 1) * 512],
                                                       start=(slot == 0), stop=(slot == 127)), r=[dg.b, uv_.b], w=[pacc[n].b])
                for n in range(2):
                    kb.op("dve", lambda e: e.tensor_tensor(out=acc[:, n * 512:(n + 1) * 512], in0=pacc[n][:, :],
                                                           in1=GF[which][:, n * 512:(n + 1) * 512], op=ALU.mult),
                          r=[pacc[n].b, GF[which].b], w=[acc.b])
                kb.op("dve", lambda e: e.tensor_tensor(out=x_[:], in0=x_[:], in1=acc[:], op=ALU.add), r=[acc.b], w=[x_.b])
                if final_out is not None and tt < 32:
                    kb.dma("sp", final_out.t[ts, :], x_[:], r=[x_.b], w=[final_out.b])
                else:
                    kb.dma("sp", xcur.t[ts, :], x_[:], r=[x_.b], w=[xcur.b])

            stage_a(0)
            for tt in range(ntile):
                if tt + 1 < ntile:
                    stage_a(tt + 1)
                stage_b(tt)
        barrier(kb)

    def zero_dram(self, tl, nelem):
        kb = self.kb
        with ExitStack() as es:
            z = sb(es, kb, "zeros2", [128, 8192], F32)
            kb.op("pool", lambda e: e.memset(z[:], 0.0), w=[z.b])
            nd = len(tl.t.shape)
            names = " ".join(f"a{i}" for i in range(nd))
            flat = tl.t.rearrange(f"{names} -> ({names})")
            per = 128 * 8192
            off = 0
            while off < nelem:
                n = min(per, nelem - off)
                cols = n // 128
                kb.dma("sp", flat[off:off + n].rearrange("(p c) -> p c", p=128), z[:, :cols], r=[z.b], w=[tl.b])
                off += n
        barrier(kb)

USE_HY = True
USE_SCAN2 = True
USE_PEER = True


def build(dbg=None, nlayers=DEPTH, scan_steps=None):
    P = Prog(dbg, nlayers)
    kb = P.kb
    P._prep_buf = Buf("prep")
    P.const_in('ident_f32', [128, 128])
    P.const_in('blockones', [128, 128])
    P.scratch('modrow', [2, 6 * D])
    P.scratch('zT', [20, 128, T])
    P.scratch('krope_tm', [T, 32])
    P.scratch('xcur', [T, D])
    P.scratch('w_fm', [2, 3, 128, T])
    P.scratch('kk_fm', [3, 128, T], BF16)
    P.scratch('rb_fm', [3, 128, T], BF16)
    P.scratch('kk6_tm', [6, T, 128], BF16)
    P.scratch('akk6_tm', [2, 6, T, 384], BF16)
    P.scratch('krep6_tm', [2, 6, T, 128], BF16)
    P.scratch('v6_tm', [6, T, 192], BF16)
    P.scratch('L4_tm', [2, 4, T, 384], BF16)
    P.scratch('V2_tm', [2, T, 3, 64], BF16)
    P.scratch('y_fm', [2, 3, 2, 64, T])
    for n in ('v_tm', 'r_tm', 'k_tm', 'g_tm'):
        P.scratch(n, [T, 384])
    P.scratch('mix_tm', [T, D])
    P.const_in('rope_cs', [SEQ, 32])
    P.const_in('antiident', [128, 128])
    P.const_in('iota16', [1, 16])
    P.const_in('hy_nrate', [256])
    P.const_in('hy_z_l', [2, 33, SEQ])
    P.const_in('hy_tn_l', [2, SEQ])
    P.const_in('hy_z_c', [2, 33, CTX])
    P.const_in('hy_tn_c', [2, CTX])
    P.scratch('Hd_l', [2, 256, 8192], BF16)
    P.scratch('Hd_c', [2, 256, 8192], BF16)
    P.scratch('hy_tm', [3, T, 256])
    P.scratch('puv_bf', [2 * 16384, 2, D], BF16)
    P.scratch('qT', [96, 6, T], BF16)
    P.scratch('kT', [96, 6, T], BF16)
    P.scratch('V1', [T, 6, 65], BF16)

    def dump(names):
        outs = []
        for n in names:
            src = P.scr[n]
            o = P.out_tensor("o_" + n, list(src.t.shape), src.t.dtype)
            kb.dma("sp", o.t, src.t, r=[src.b, P._prep_buf], w=[o.b])
            outs.append(o.b)
        kb.wait_all("sp", outs)

    out = None
    if dbg is None:
        out = P.out_tensor("out", [SEQ, D])
    P.phase_zero_init()
    P.zero_dram(P.scr['mix_tm'], T * D)
    for li in range(nlayers):
        first = (li == 0)
        need_ctx = li < DEPTH - 1
        last = (li == nlayers - 1)
        P.phase_mod(li)
        P.phase_inproj(li, first)
        if dbg == 'inproj':
            dump(['zT', 'krope_tm', 'modrow'])
            return P
        P.phase_rw_prep(li)
        if dbg == 'rwprep':
            dump(['w_fm', 'kk_fm', 'L4_tm', 'V2_tm', 'v_tm', 'r_tm', 'k_tm', 'g_tm'])
            return P
        if USE_SCAN2:
            P.phase_rw_scan2(li, scan_steps)
        else:
            P.phase_rw_scan(li, scan_steps)
        if dbg == 'rwscan':
            dump(['y_fm', 'w_fm', 'kk_fm', 'L4_tm', 'V2_tm', 'v_tm', 'r_tm', 'k_tm', 'g_tm'])
            return P
        P.phase_rw_readout(li, need_ctx)
        if dbg == 'readout':
            dump(['mix_tm', 'y_fm'])
            return P
        P.phase_mla_prep(li)
        if dbg == 'mlaprep':
            dump(['qT', 'kT', 'V1'])
            return P
        P.phase_mla_attn(li, need_ctx)
        if dbg == 'mla':
            dump(['mix_tm'])
            return P
        if USE_HY:
            P.phase_hy_filter(li, SEQ, P.scr['Hd_l'], '_l')
            if need_ctx:
                P.phase_hy_filter(li, CTX, P.scr['Hd_c'], '_c')
            if dbg == 'hyfilt':
                dump(['Hd_l', 'Hd_c'])
                return P
            P.phase_hyena(li, need_ctx)
            if dbg == 'hyena':
                dump(['mix_tm'])
                return P
        P.phase_outproj(li, first, need_ctx, final_out=(out if (last and dbg is None and not USE_PEER) else None))
        if dbg == 'outproj':
            dump(['xcur', 'mix_tm'])
            return P
        if USE_PEER:
            P.phase_peer(li, need_ctx, final_out=(out if (last and dbg is None) else None))
            if dbg == 'peer':
                dump(['xcur'])
                return P
    kb.wait_all("sp", [out.b])
    return P


def host_consts():
    c = {}
    c['ident_f32'] = np.eye(128, dtype=np.float32)
    bo = np.zeros((128, 128), np.float32)
    bo[:64, :64] = 1.0
    bo[64:, 64:] = 1.0
    c['blockones'] = bo
    rows = SEQ // 64
    row, col = np.meshgrid(np.arange(rows), np.arange(64), indexing='ij')
    inv = (10000.0 ** (-np.arange(0, 16, 2, dtype=np.float32) / 16)).astype(np.float32)
    pos = np.stack([row.reshape(-1), col.reshape(-1)], axis=-1).astype(np.float32)
    ang = (pos[:, :, None] * inv[None, None, :]).astype(np.float32)
    c['iota16'] = np.arange(16, dtype=np.float32).reshape(1, 16)
    c['antiident'] = np.ascontiguousarray(np.eye(128, dtype=np.float32)[::-1])
    rates = np.abs(np.linspace(math.log(1e-2) / 1.5, math.log(1e-2) / 0.3, 256, dtype=np.float32)).astype(np.float32)
    c['hy_nrate'] = (-rates).astype(np.float32)
    for L_, suf in ((SEQ, '_l'), (CTX, '_c')):
        tn = (np.arange(L_, dtype=np.float32) / np.float32(L_)).astype(np.float32)
        bands = np.arange(1, 17, dtype=np.float32)
        angh = (np.float32(2.0 * math.pi) * tn[:, None] * bands[None, :]).astype(np.float32)
        z = np.concatenate([tn[:, None], np.cos(angh), np.sin(angh)], axis=-1).astype(np.float32)
        c['hy_z' + suf] = np.ascontiguousarray(np.stack([z.T, z[::-1].T]).astype(np.float32))
        c['hy_tn' + suf] = np.ascontiguousarray(np.stack([tn, tn[::-1]]).astype(np.float32))
    c['rope_cs'] = np.concatenate([np.cos(ang).reshape(SEQ, 16), np.sin(ang).reshape(SEQ, 16)], axis=1).astype(np.float32)
    return c


_PROG = {}


def kernel(**inputs):
    if 'p' not in _PROG:
        _PROG['p'] = build(None)
    P = _PROG['p']
    hc = host_consts()
    in_maps = []
    for b in range(8):
        m = {'x': np.ascontiguousarray(inputs['x'][b], dtype=np.float32),
             'ctx': np.ascontiguousarray(inputs['ctx'][b], dtype=np.float32),
             'c2': np.ascontiguousarray(np.stack([inputs['c'][b], inputs['c_ctx']]), dtype=np.float32)}
        for n in INPUT_NAMES:
            m[n] = np.ascontiguousarray(inputs[n], dtype=np.float32)
        for k in P.consts:
            m[k] = hc[k]
        in_maps.append(m)
    res = run_bass_kernel_spmd(P.nc, in_maps, core_ids=list(range(8)))
    return np.stack([np.asarray(r['out'], dtype=np.float32) for r in res.results], axis=0)
```
